# Optimizing a Trainium2 kernel written in Bass

```python
import jax, jax.numpy as jnp
from jax import lax
import numpy as np

D_MODEL = 2048
BATCH = 2
SEQ = 8192
DEPTH = 4
DEC_BATCH = 32
DEC_SEQ = 32
PAST_LEN = 1024

CHUNK = 64
FOX_HEADS = 8
FOX_HD = 128
FOX_W = FOX_HEADS * FOX_HD
Q_BLOCK = 128
RWKV_HEADS = 16
RWKV_HD = 64
RWKV_W = RWKV_HEADS * RWKV_HD
RWKV_W_LORA = 64
RWKV_A_LORA = 64
RWKV_G_LORA = 128
RWKV_IN_W = 3 * RWKV_W + RWKV_W_LORA + RWKV_A_LORA + RWKV_G_LORA
GMLP_W = 1024
GMLP_GROUPS = 8
GMLP_GC = GMLP_W // GMLP_GROUPS
GMLP_CHUNK = 128
N_BRANCH = 3
IN_W = 3 * FOX_W + FOX_HEADS + RWKV_IN_W + 2 * GMLP_W + N_BRANCH * D_MODEL
D_FF = -(-(8 * D_MODEL) // (3 * 256)) * 256
RMS_EPS = 1e-6
LN_EPS = 1e-5
RWKV_GN_EPS = 64e-5
FOX_FORGET_INIT = 4.0
NEG_INF = -1e30

kernel_name = 'fox_rwkv7_gmlp_hybrid_stream_step'


def rms_norm(x, g):
    xf = x.astype(jnp.float32)
    y = xf * lax.rsqrt(jnp.mean(xf * xf, axis=-1, keepdims=True) + RMS_EPS)
    return (y * g.astype(jnp.float32)).astype(x.dtype)


def layer_norm(x, g, b, eps):
    xf = x.astype(jnp.float32)
    mu = jnp.mean(xf, axis=-1, keepdims=True)
    var = jnp.mean(jnp.square(xf - mu), axis=-1, keepdims=True)
    return ((xf - mu) * lax.rsqrt(var + eps) * g.astype(jnp.float32) + b.astype(jnp.float32)).astype(x.dtype)


def fox_attend_prompt(q, k, v, logf):
    B, T, H, D = q.shape
    nblk = T // Q_BLOCK
    ct = jnp.cumsum(logf, axis=1).transpose(0, 2, 1)
    qb = q.reshape(B, nblk, Q_BLOCK, H, D).transpose(1, 0, 2, 3, 4)
    cb = ct.reshape(B, H, nblk, Q_BLOCK).transpose(2, 0, 1, 3)
    kpos = jnp.arange(T)
    scale = FOX_HD ** -0.5

    def block(args):
        qi, ci, i = args
        s = jnp.einsum('bqhd,bkhd->bhqk', qi, k, preferred_element_type=jnp.float32) * scale
        s = s + ci[..., None] - ct[:, :, None, :]
        qpos = i * Q_BLOCK + jnp.arange(Q_BLOCK)
        s = jnp.where(kpos[None, :] <= qpos[:, None], s, NEG_INF)
        p = jax.nn.softmax(s, axis=-1)
        return jnp.einsum('bhqk,bkhd->bqhd', p.astype(v.dtype), v)

    o = lax.map(block, (qb, cb, jnp.arange(nblk)))
    return o.transpose(1, 0, 2, 3, 4).reshape(B, T, H * D)


def fox_attend_sample(q, k, v, logf, k_cache, v_cache, logf_cache):
    B, S, H, D = q.shape
    P = k_cache.shape[1]
    k_all = jnp.concatenate([k_cache.astype(k.dtype), k], axis=1)
    v_all = jnp.concatenate([v_cache.astype(v.dtype), v], axis=1)
    c = jnp.cumsum(jnp.concatenate([logf_cache.astype(jnp.float32), logf], axis=1), axis=1).transpose(0, 2, 1)
    s = jnp.einsum('bqhd,bkhd->bhqk', q, k_all, preferred_element_type=jnp.float32) * (FOX_HD ** -0.5)
    s = s + c[:, :, P:, None] - c[:, :, None, :]
    mask = jnp.arange(P + S)[None, :] <= (P + jnp.arange(S))[:, None]
    s = jnp.where(mask, s, NEG_INF)
    p = jax.nn.softmax(s, axis=-1)
    o = jnp.einsum('bhqk,bkhd->bqhd', p.astype(v_all.dtype), v_all)
    return o.reshape(B, S, H * D)


def rwkv_scan(r, w, k, v, a, b, state):
    def step(S, inp):
        r_t, w_t, k_t, v_t, a_t, b_t = inp
        sa = jnp.einsum('bhij,bhj->bhi', S, a_t)
        S = S * w_t[:, :, None, :] + sa[..., None] * b_t[:, :, None, :] + v_t[..., None] * k_t[:, :, None, :]
        y = jnp.einsum('bhij,bhj->bhi', S, r_t)
        return S, y
    xs = (r.swapaxes(0, 1), w.swapaxes(0, 1), k.swapaxes(0, 1), v.swapaxes(0, 1), a.swapaxes(0, 1), b.swapaxes(0, 1))
    S, ys = lax.scan(step, state, xs)
    return ys.swapaxes(0, 1), S


def rwkv_time_mix(p, prev, state, lw):
    f32 = jnp.float32
    B, T, _ = p.shape
    p_prev = jnp.concatenate([prev.astype(p.dtype), p[:, :-1]], axis=1)
    xs = p + (p_prev - p) * lw['rwkv_mu']
    o1 = 3 * RWKV_W
    r, k, v, dw, da, dg = jnp.split(xs, [RWKV_W, 2 * RWKV_W, o1, o1 + RWKV_W_LORA, o1 + RWKV_W_LORA + RWKV_A_LORA], axis=-1)
    r, k, v = r.astype(f32), k.astype(f32), v.astype(f32)
    wlog = -jax.nn.softplus(-(lw['rwkv_w0'] + jnp.tanh(dw) @ lw['rwkv_w_up']).astype(f32)) - 0.5
    decay = jnp.exp(-jnp.exp(wlog))
    a = jax.nn.sigmoid((lw['rwkv_a0'] + da @ lw['rwkv_a_up']).astype(f32))
    g = (jax.nn.sigmoid(dg) @ lw['rwkv_g_up']).astype(f32)

    def heads(t):
        return t.reshape(B, T, RWKV_HEADS, RWKV_HD)

    kk = heads(k * lw['rwkv_k_k'].astype(f32))
    kk = kk * lax.rsqrt(jnp.maximum(jnp.sum(kk * kk, axis=-1, keepdims=True), 1e-24))
    k = k * (1.0 + (a - 1.0) * lw['rwkv_k_a'].astype(f32))
    r_h, k_h, v_h, a_h = heads(r), heads(k), heads(v), heads(a)
    y, S = rwkv_scan(r_h, heads(decay), k_h, v_h, -kk, kk * a_h, state.astype(f32))
    mu = jnp.mean(y, axis=-1, keepdims=True)
    var = jnp.mean(jnp.square(y - mu), axis=-1, keepdims=True)
    yn = ((y - mu) * lax.rsqrt(var + RWKV_GN_EPS)).reshape(B, T, RWKV_W)
    yn = yn * lw['rwkv_gn_g'].astype(f32) + lw['rwkv_gn_b'].astype(f32)
    bonus = jnp.sum(r_h * k_h * lw['rwkv_r_k'].astype(f32), axis=-1, keepdims=True) * v_h
    out = (yn + bonus.reshape(B, T, RWKV_W)) * g
    return out.astype(p.dtype), S, p[:, -1:]


def gmlp_spatial_gate(gu, gv, lw, rows):
    B, T, _ = gu.shape
    u = jax.nn.gelu(gu)
    vn = layer_norm(jax.nn.gelu(gv), lw['gmlp_ln_g'], lw['gmlp_ln_b'], LN_EPS)
    n = T // rows
    vc = vn.reshape(B, n, rows, GMLP_GROUPS, GMLP_GC)
    w = jnp.tril(lw['gmlp_w_s'][:, :rows, :rows])
    bias = lw['gmlp_b_s'][:, :rows].T
    s = jnp.einsum('gij,bnjgc->bnigc', w, vc) + bias[None, None, :, :, None]
    return u * s.reshape(B, T, GMLP_W), vn


def trunk_layer(x, lw, cache):
    B, T, _ = x.shape
    h = rms_norm(x, lw['pre_mix_g'])
    proj = h @ lw['w_in']
    o_f = 3 * FOX_W + FOX_HEADS
    o_r = o_f + RWKV_IN_W
    fq, fk, fv, ff, rw, gu, gv, gates = jnp.split(
        proj, [FOX_W, 2 * FOX_W, 3 * FOX_W, o_f, o_r, o_r + GMLP_W, o_r + 2 * GMLP_W], axis=-1)
    q = fq.reshape(B, T, FOX_HEADS, FOX_HD)
    k = fk.reshape(B, T, FOX_HEADS, FOX_HD)
    v = fv.reshape(B, T, FOX_HEADS, FOX_HD)
    logf = jax.nn.log_sigmoid((ff + lw['fox_bf']).astype(jnp.float32))
    if cache is None:
        o_a = fox_attend_prompt(q, k, v, logf)
        prev = jnp.zeros((B, 1, RWKV_IN_W), x.dtype)
        s0 = jnp.zeros((B, RWKV_HEADS, RWKV_HD, RWKV_HD), jnp.float32)
        rows = GMLP_CHUNK
    else:
        o_a = fox_attend_sample(q, k, v, logf, cache['k'], cache['v'], cache['logf'])
        prev = cache['shift']
        s0 = cache['state']
        rows = T
    o_b, s_new, shift_new = rwkv_time_mix(rw, prev, s0, lw)
    o_c, vn = gmlp_spatial_gate(gu, gv, lw, rows)
    g_a, g_b, g_c = jnp.split(jax.nn.sigmoid(gates), N_BRANCH, axis=-1)
    m = (g_a * (o_a @ lw['w_br_fox']) + g_b * (o_b @ lw['w_br_rwkv'])
         + g_c * (o_c @ lw['w_br_gmlp']))
    x = x + rms_norm(m @ lw['w_out'], lw['post_mix_g'])
    h2 = rms_norm(x, lw['pre_ffn_g'])
    f = (jax.nn.silu(h2 @ lw['ffn_w1']) * (h2 @ lw['ffn_w3'])) @ lw['ffn_w2']
    x = x + rms_norm(f, lw['post_ffn_g'])
    return x, (k, v, logf, s_new, shift_new, vn)


def setup_inputs(seed: int = 0) -> dict:
    key = jax.random.key(seed)
    ks = iter(jax.random.split(key, 48))
    f32 = jnp.float32

    def nrm(shape, scale=1.0):
        return jax.random.normal(next(ks), shape, f32) * scale

    def gain(shape):
        return 1.0 + nrm(shape, 0.02)

    L = DEPTH
    return {
        'x_prompt': nrm((BATCH, SEQ, D_MODEL)),
        'x_sample': nrm((DEC_BATCH, DEC_SEQ, D_MODEL)),
        'cache_fox_k': nrm((L, DEC_BATCH, PAST_LEN, FOX_HEADS, FOX_HD)),
        'cache_fox_v': nrm((L, DEC_BATCH, PAST_LEN, FOX_HEADS, FOX_HD)),
        'cache_fox_logf': jax.nn.log_sigmoid(FOX_FORGET_INIT + nrm((L, DEC_BATCH, PAST_LEN, FOX_HEADS))),
        'state_rwkv': nrm((L, DEC_BATCH, RWKV_HEADS, RWKV_HD, RWKV_HD), 0.3),
        'state_rwkv_shift': nrm((L, DEC_BATCH, 1, RWKV_IN_W)),
        'pre_mix_g': gain((L, D_MODEL)),
        'w_in': nrm((L, D_MODEL, IN_W), D_MODEL ** -0.5),
        'fox_bf': FOX_FORGET_INIT + nrm((L, FOX_HEADS), 0.1),
        'rwkv_mu': jax.random.uniform(next(ks), (L, RWKV_IN_W), f32),
        'rwkv_w0': -1.0 + nrm((L, RWKV_W), 0.5),
        'rwkv_w_up': nrm((L, RWKV_W_LORA, RWKV_W), RWKV_W_LORA ** -0.5),
        'rwkv_a0': nrm((L, RWKV_W), 0.1),
        'rwkv_a_up': nrm((L, RWKV_A_LORA, RWKV_W), RWKV_A_LORA ** -0.5),
        'rwkv_g_up': nrm((L, RWKV_G_LORA, RWKV_W), RWKV_G_LORA ** -0.5),
        'rwkv_k_k': 0.85 + nrm((L, RWKV_W), 0.02),
        'rwkv_k_a': gain((L, RWKV_W)),
        'rwkv_r_k': nrm((L, RWKV_HEADS, RWKV_HD), 0.1),
        'rwkv_gn_g': gain((L, RWKV_W)),
        'rwkv_gn_b': nrm((L, RWKV_W), 0.02),
        'gmlp_ln_g': gain((L, GMLP_W)),
        'gmlp_ln_b': nrm((L, GMLP_W), 0.02),
        'gmlp_w_s': nrm((L, GMLP_GROUPS, GMLP_CHUNK, GMLP_CHUNK), GMLP_CHUNK ** -0.5),
        'gmlp_b_s': gain((L, GMLP_GROUPS, GMLP_CHUNK)),
        'w_br_fox': nrm((L, FOX_W, D_MODEL), FOX_W ** -0.5),
        'w_br_rwkv': nrm((L, RWKV_W, D_MODEL), RWKV_W ** -0.5),
        'w_br_gmlp': nrm((L, GMLP_W, D_MODEL), GMLP_W ** -0.5),
        'w_out': nrm((L, D_MODEL, D_MODEL), D_MODEL ** -0.5),
        'post_mix_g': gain((L, D_MODEL)),
        'pre_ffn_g': gain((L, D_MODEL)),
        'ffn_w1': nrm((L, D_MODEL, D_FF), D_MODEL ** -0.5),
        'ffn_w3': nrm((L, D_MODEL, D_FF), D_MODEL ** -0.5),
        'ffn_w2': nrm((L, D_FF, D_MODEL), D_FF ** -0.5),
        'post_ffn_g': gain((L, D_MODEL)),
    }


def reference(x_prompt, x_sample, cache_fox_k, cache_fox_v, cache_fox_logf, state_rwkv, state_rwkv_shift,
              pre_mix_g, w_in, fox_bf, rwkv_mu, rwkv_w0, rwkv_w_up, rwkv_a0, rwkv_a_up, rwkv_g_up,
              rwkv_k_k, rwkv_k_a, rwkv_r_k, rwkv_gn_g, rwkv_gn_b, gmlp_ln_g, gmlp_ln_b, gmlp_w_s, gmlp_b_s,
              w_br_fox, w_br_rwkv, w_br_gmlp, w_out, post_mix_g, pre_ffn_g, ffn_w1, ffn_w3, ffn_w2, post_ffn_g):
    xp, xs = x_prompt, x_sample
    kp, vp, lp, sp, shp = [], [], [], [], []
    ksm, vsm, lsm, ssm, shs, gvs = [], [], [], [], [], []
    for l in range(DEPTH):
        lw = {
            'pre_mix_g': pre_mix_g[l], 'w_in': w_in[l], 'fox_bf': fox_bf[l],
            'rwkv_mu': rwkv_mu[l], 'rwkv_w0': rwkv_w0[l], 'rwkv_w_up': rwkv_w_up[l],
            'rwkv_a0': rwkv_a0[l], 'rwkv_a_up': rwkv_a_up[l], 'rwkv_g_up': rwkv_g_up[l],
            'rwkv_k_k': rwkv_k_k[l], 'rwkv_k_a': rwkv_k_a[l], 'rwkv_r_k': rwkv_r_k[l],
            'rwkv_gn_g': rwkv_gn_g[l], 'rwkv_gn_b': rwkv_gn_b[l],
            'gmlp_ln_g': gmlp_ln_g[l], 'gmlp_ln_b': gmlp_ln_b[l], 'gmlp_w_s': gmlp_w_s[l], 'gmlp_b_s': gmlp_b_s[l],
            'w_br_fox': w_br_fox[l], 'w_br_rwkv': w_br_rwkv[l], 'w_br_gmlp': w_br_gmlp[l],
            'w_out': w_out[l], 'post_mix_g': post_mix_g[l], 'pre_ffn_g': pre_ffn_g[l],
            'ffn_w1': ffn_w1[l], 'ffn_w3': ffn_w3[l], 'ffn_w2': ffn_w2[l], 'post_ffn_g': post_ffn_g[l],
        }
        xp, (k_p, v_p, lf_p, s_p, sh_p, _vn_p) = trunk_layer(xp, lw, None)
        cache = {'k': cache_fox_k[l], 'v': cache_fox_v[l], 'logf': cache_fox_logf[l],
                 'state': state_rwkv[l], 'shift': state_rwkv_shift[l]}
        xs, (k_s, v_s, lf_s, s_s, sh_s, vn_s) = trunk_layer(xs, lw, cache)
        kp.append(k_p); vp.append(v_p); lp.append(lf_p); sp.append(s_p); shp.append(sh_p)
        ksm.append(k_s); vsm.append(v_s); lsm.append(lf_s); ssm.append(s_s); shs.append(sh_s); gvs.append(vn_s)
    fox_k_prompt = jnp.stack(kp)
    fox_v_prompt = jnp.stack(vp)
    fox_logf_prompt = jnp.stack(lp)
    rwkv_state_prompt = jnp.stack(sp)
    rwkv_shift_prompt = jnp.stack(shp)
    fox_k_sample = jnp.stack(ksm)
    fox_v_sample = jnp.stack(vsm)
    fox_logf_sample = jnp.stack(lsm)
    rwkv_state_sample = jnp.stack(ssm)
    rwkv_shift_sample = jnp.stack(shs)
    gmlp_v_sample = jnp.stack(gvs)
    return (xp, xs, fox_k_prompt, fox_v_prompt, fox_logf_prompt, rwkv_state_prompt, rwkv_shift_prompt,
            fox_k_sample, fox_v_sample, fox_logf_sample, rwkv_state_sample, rwkv_shift_sample, gmlp_v_sample)
```

```python
import numpy as np
from contextlib import ExitStack
import concourse.bass as bass
import concourse.mybir as mybir
from concourse.bass_utils import run_bass_kernel_spmd

F32 = mybir.dt.float32
BF16 = mybir.dt.bfloat16
AF = mybir.ActivationFunctionType
ALU = mybir.AluOpType
AX = mybir.AxisListType

D = 2048
KC = D // 128
NCORE = 8
DEPTH = 4
TPROMPT = 8192
FOXH = 8
RH = 16
RN = 64
RWW = 3328
INW = 14600
DFF = 5632
PAST = 1024
O_F = 3072
O_RW = 3080
O_GU = O_RW + RWW
O_GV = O_GU + 1024
O_GATE = O_GV + 1024
ATT_SCALE = 128 ** -0.5
import os
KIND_FILTER = set(os.environ.get('P1KINDS', '').split(',')) - {''}
P1DBG = int(os.environ.get('P1DBG', '9'))
DBG_OAB = int(os.environ.get('DBG_OAB', '0'))
SKIP = set(os.environ.get('SKIP', '').split(',')) - {''}


class Tr:
    __slots__ = ("w", "r", "sem", "excl", "uid")
    _n = [0]

    def __init__(self, excl=False):
        Tr._n[0] += 1
        self.uid = Tr._n[0]
        self.w = None
        self.r = {}
        self.sem = None
        self.excl = excl


def PTr():
    return Tr(excl=True)


class KB:
    def __init__(self, nc, es):
        self.nc = nc
        self.es = es
        self.eng = {"pe": nc.tensor, "act": nc.scalar, "dve": nc.vector, "pool": nc.gpsimd, "sp": nc.sync}
        self.esem = {e: es.enter_context(nc.semaphore("s_" + e)) for e in ("pe", "act", "dve", "pool")}
        self.ecnt = {e: 0 for e in self.esem}
        self.waited = {}
        self.dtr = []
        self.nsem = 0
        self.sempool = {}

    def _wait(self, eng, tok):
        key, sem, val = tok
        if self.waited.get((eng, key), 0) >= val:
            return
        self.eng[eng].wait_ge(sem, val)
        self.waited[(eng, key)] = val

    def _deps(self, eng, reads, writes):
        for t in reads:
            if t.w is not None:
                self._dep1(eng, t.w)
        for t in writes:
            if t.w is not None:
                self._dep1(eng, t.w)
            for tok in t.r.values():
                self._dep1(eng, tok)

    def _dep1(self, eng, tok):
        if tok[0] == eng and eng == "pe":
            return
        self._wait(eng, tok)

    def op(self, eng, reads, writes, fn):
        if any(t.excl for t in reads):
            writes = list(writes) + [t for t in reads if t.excl]
            reads = [t for t in reads if not t.excl]
        self._deps(eng, reads, writes)
        ins = fn(self.eng[eng])
        self.ecnt[eng] += 1
        ins.then_inc(self.esem[eng], 1)
        tok = (eng, self.esem[eng], self.ecnt[eng])
        for t in reads:
            t.r[eng] = tok
        for t in writes:
            t.w = tok
            t.r = {}
        return ins

    def dma(self, q, out, in_, reads, writes, semtr):
        self._deps(q, reads, writes)
        if semtr.sem is None:
            semtr.sem = {}
        if q not in semtr.sem:
            pool = self.sempool.setdefault(q, [])
            if pool:
                semtr.sem[q] = pool.pop()
            else:
                semtr.sem[q] = [self.es.enter_context(self.nc.semaphore("d%d" % self.nsem)), 0]
                self.nsem += 1
            self.dtr.append((semtr, q))
        ent = semtr.sem[q]
        ins = self.eng[q].dma_start(out=out, in_=in_)
        ent[1] += 16
        ins.then_inc(ent[0], 16)
        key = (semtr.uid, q)
        tok = (key, ent[0], ent[1])
        for t in reads:
            t.r[key] = tok
        for t in writes:
            t.w = tok
            t.r = {}
        return ins

    def recycle(self, keep=()):
        kept = []
        for (t, q) in self.dtr:
            if any(t is k for k in keep):
                kept.append((t, q))
            else:
                self.sempool.setdefault(q, []).append(t.sem[q])
        self.dtr = kept

    def barrier(self, engines=("pe", "act", "dve", "pool", "sp")):
        for e in engines:
            for s in self.esem:
                if s != e and self.ecnt[s] > 0:
                    self._wait(e, (s, self.esem[s], self.ecnt[s]))
            for (t, q) in self.dtr:
                ent = t.sem[q]
                if ent[1] > 0:
                    self._wait(e, ((t.uid, q), ent[0], ent[1]))


def _colblocks():
    blocks = []
    for j in range(2):
        blocks.append(("q", j, j * 512, 512))
    for j in range(2):
        blocks.append(("k", j, 1024 + j * 512, 512))
    for j in range(2):
        blocks.append(("v", j, 2048 + j * 512, 512))
    blocks.append(("f", 0, O_F, 8))
    for j in range(6):
        blocks.append(("rw", j, O_RW + j * 512, 512))
    blocks.append(("rw", 6, O_RW + 3072, 256))
    for j in range(2):
        blocks.append(("gu", j, O_GU + j * 512, 512))
    for j in range(2):
        blocks.append(("gv", j, O_GV + j * 512, 512))
    for j in range(12):
        blocks.append(("gate", j, O_GATE + j * 512, 512))
    return blocks


class Prog:
    def __init__(self, TP, NL, stop_after=None):
        self.TP = TP
        self.NL = NL
        self.NT = TP // 128
        self.TT = TP + 128
        self.stop_after = stop_after
        self.nc = bass.Bass("TRN2", target_bir_lowering=False)
        self._declare()

    def _declare(self):
        nc, TP, NL, TT = self.nc, self.TP, self.NL, self.TT

        def inp(name, shape, dt=F32):
            return nc.dram_tensor(name, list(shape), dt, kind="ExternalInput").ap()

        def outp(name, shape, dt=F32):
            return nc.dram_tensor(name, list(shape), dt, kind="ExternalOutput").ap()

        def scr(name, shape, dt):
            return nc.dram_tensor(name, list(shape), dt, kind="Internal").ap()

        i = {}
        i["xp"] = inp("xp", [TP, D])
        i["xs"] = inp("xs", [128, D])
        if not (self.stop_after and self.stop_after[0].startswith("p1")):
            i["ck"] = inp("ck", [NL, 4, PAST, 1024])
            i["cv"] = inp("cv", [NL, 4, PAST, 1024])
        i["clf"] = inp("clf", [NL, 4, PAST, 8])
        i["st"] = inp("st", [NL, 4, RH, RN, RN])
        i["sh"] = inp("sh", [NL, 4, RWW])
        for nm, shp in (("pre_mix_g", [NL, D]), ("w_in", [NL, D, INW]), ("fox_bf", [NL, 8]),
                        ("rwkv_mu", [NL, RWW]), ("rwkv_w0", [NL, 1024]), ("rwkv_w_up", [NL, 64, 1024]),
                        ("rwkv_a0", [NL, 1024]), ("rwkv_a_up", [NL, 64, 1024]), ("rwkv_g_up", [NL, 128, 1024]),
                        ("rwkv_k_k", [NL, 1024]), ("rwkv_k_a", [NL, 1024]), ("rwkv_r_k", [NL, 1024]),
                        ("rwkv_gn_g", [NL, 1024]), ("rwkv_gn_b", [NL, 1024]),
                        ("gmlp_ln_g", [NL, 1024]), ("gmlp_ln_b", [NL, 1024]),
                        ("gmlp_w_s", [NL, 8, 128, 128]), ("gmlp_b_s", [NL, 8, 128]),
                        ("w_br_fox", [NL, 1024, D]), ("w_br_rwkv", [NL, 1024, D]), ("w_br_gmlp", [NL, 1024, D]),
                        ("w_out", [NL, D, D]), ("post_mix_g", [NL, D]), ("pre_ffn_g", [NL, D]),
                        ("ffn_w1", [NL, D, DFF]), ("ffn_w3", [NL, D, DFF]), ("ffn_w2", [NL, DFF, D]),
                        ("post_ffn_g", [NL, D])):
            if self.stop_after and self.stop_after[0].startswith("p1") and nm in (
                    "w_br_fox", "w_br_rwkv", "w_br_gmlp", "w_out", "ffn_w1", "ffn_w3", "ffn_w2"):
                continue
            i[nm] = inp(nm, shp)
        if DBG_OAB:
            i["dbg_oa"] = inp("dbg_oa", [TT, 1024])
            i["dbg_ob"] = inp("dbg_ob", [TT, 1024])
        i["c_ident"] = inp("c_ident", [128, 128])
        i["c_triu"] = inp("c_triu", [128, 128])
        i["c_triu32"] = inp("c_triu32", [128, 128])
        i["c_sel"] = inp("c_sel", [16, 8 * 128])
        self.i = i
        o = {}
        o["yp"] = outp("yp", [TP, D])
        o["ys"] = outp("ys", [128, D])
        o["kp"] = outp("kp", [NL, TP, 1024])
        o["vp"] = outp("vp", [NL, TP, 1024])
        o["lfp"] = outp("lfp", [NL, TP, 8])
        o["stp"] = outp("stp", [NL, RH, RN, RN])
        o["shp"] = outp("shp", [NL, RWW])
        o["ks"] = outp("ks", [NL, 128, 1024])
        o["vs"] = outp("vs", [NL, 128, 1024])
        o["lfs"] = outp("lfs", [NL, 128, 8])
        o["sts"] = outp("sts", [NL, 4, RH, RN, RN])
        o["shs"] = outp("shs", [NL, 4, RWW])
        o["gvs"] = outp("gvs", [NL, 128, 1024])
        self.o = o
        s = {}
        s["x"] = scr("s_x", [TT, D], F32)
        s["qT"] = scr("s_qT", [8, 128, TT], BF16)
        s["kT"] = scr("s_kT", [8, 128, TT], BF16)
        s["v"] = scr("s_v", [TT, 1024], BF16)
        s["cT"] = scr("s_cT", [16, TT], BF16)
        s["negc"] = scr("s_negc", [TT, 8], F32)
        s["prw"] = scr("s_prw", [TT, RWW], F32)
        s["oa"] = scr("s_oa", [TT, 1024], BF16)
        s["ob"] = scr("s_ob", [TT, 1024], BF16)
        s["oc"] = scr("s_oc", [TT, 1024], BF16)
        s["gate"] = scr("s_gate", [TT, 6144], BF16)
        self.s = s

    def sb(self, es, name, shape, dt):
        self._uid = getattr(self, "_uid", 0) + 1
        return es.enter_context(self.nc.sbuf_tensor("%s_%d" % (name, self._uid), list(shape), dt))

    def ps(self, es, name, shape, dt):
        self._uid = getattr(self, "_uid", 0) + 1
        return es.enter_context(self.nc.psum_tensor("%s_%d" % (name, self._uid), list(shape), dt))

    def groups(self):
        g = []
        t = 0
        while t < self.NT:
            g.append(list(range(t, min(t + 4, self.NT))))
            t += 4
        g.append([self.NT])
        return g

    def rmsnorm_rstd(self, kb, ssum_ap, rstd_ap, tr_ss, tr_rstd, n, eps):
        kb.op("dve", [tr_ss], [tr_rstd], lambda e: e.tensor_scalar(
            out=rstd_ap, in0=ssum_ap, scalar1=1.0 / n, scalar2=eps, op0=ALU.mult, op1=ALU.add))
        kb.op("act", [tr_rstd], [tr_rstd], lambda e: e.activation(out=rstd_ap, in_=rstd_ap, func=AF.Sqrt))
        kb.op("dve", [tr_rstd], [tr_rstd], lambda e: e.reciprocal(out=rstd_ap, in_=rstd_ap))

    def build(self):
        nc = self.nc
        with ExitStack() as es:
            kb = KB(nc, es)
            self.kb = kb
            self.identb = self.sb(es, "identb", [128, 128], BF16)
            self.identf = self.sb(es, "identf", [128, 128], F32)
            self.triuf = self.sb(es, "triuf", [128, 128], F32)
            self.triu32f = self.sb(es, "triu32f", [128, 128], F32)
            self.onesf = self.sb(es, "onesf", [128, 128], F32)
            self.selb = self.sb(es, "selb", [16, 1024], BF16)
            self.tr_const = Tr()
            c = self.tr_const
            kb.dma("pool", self.identb[:], self.i["c_ident"][:, :], [], [c], c)
            kb.dma("sp", self.identf[:], self.i["c_ident"][:, :], [], [c], c)
            kb.dma("sp", self.triuf[:], self.i["c_triu"][:, :], [], [c], c)
            kb.dma("sp", self.triu32f[:], self.i["c_triu32"][:, :], [], [c], c)
            kb.dma("pool", self.selb[:], self.i["c_sel"][:, :], [], [c], c)
            kb.op("dve", [], [c], lambda e: e.memset(self.onesf[:], 1.0))
            kb.barrier()
            for l in range(self.NL):
                self.phase1(l)
                kb.barrier()
                kb.recycle(keep=(self.tr_const,))
                if self.stop_after in (("p1", l), ("p1s", l)):
                    break
                if DBG_OAB:
                    self.dbg_fill(l)
                    kb.barrier()
                    kb.recycle(keep=(self.tr_const,))
                if "p2a" not in SKIP:
                    self.phase2a(l)
                    kb.barrier()
                    kb.recycle(keep=(self.tr_const,))
                if "p2b" not in SKIP:
                    self.phase2b(l)
                    kb.barrier()
                    kb.recycle(keep=(self.tr_const,))
                if "p3" not in SKIP:
                    self.phase3(l)
                kb.barrier()
                kb.recycle(keep=(self.tr_const,))
                if self.stop_after == ("p3", l):
                    break
            kb.barrier()
        return nc

    def phase1(self, l):
        nc, kb, i, o, s = self.nc, self.kb, self.i, self.o, self.s
        TP, NT = self.TP, self.NT
        xin_p = i["xp"] if l == 0 else s["x"]
        blocks = _colblocks()
        with ExitStack() as es:
            sb = lambda n, shp, dt: self.sb(es, "p1_" + n, shp, dt)
            g_bc = sb("g_bc", [128, D], F32)
            bf_bc = sb("bf_bc", [128, 8], F32)
            lng_bc = sb("lng_bc", [128, 1024], F32)
            lnb_bc = sb("lnb_bc", [128, 1024], F32)
            wsT = sb("wsT", [128, 8, 128], BF16)
            wsT_s = sb("wsT_s", [128, 8, 128], BF16)
            wtmp = sb("wtmp", [128, 8, 128], BF16)
            wtmp_s = sb("wtmp_s", [128, 8, 128], BF16)
            biasT = sb("biasT", [128, 8], F32)
            biasT_s = sb("biasT_s", [128, 8], F32)
            trilb = sb("trilb", [128, 128], BF16)
            tp = Tr()
            kb.dma("sp", g_bc[:], i["pre_mix_g"][l, :].partition_broadcast(128), [], [tp], tp)
            kb.dma("sp", bf_bc[:], i["fox_bf"][l, :].partition_broadcast(128), [], [tp], tp)
            kb.dma("sp", lng_bc[:], i["gmlp_ln_g"][l, :].partition_broadcast(128), [], [tp], tp)
            kb.dma("sp", lnb_bc[:], i["gmlp_ln_b"][l, :].partition_broadcast(128), [], [tp], tp)
            kb.dma("pool", wtmp[:], i["gmlp_w_s"][l].rearrange("g i j -> i g j"), [], [tp], tp)
            kb.op("dve", [], [tp], lambda e: e.memset(wtmp_s[:], 0.0))
            for b in range(4):
                kb.dma("pool", wtmp_s[b * 32:(b + 1) * 32, :, b * 32:(b + 1) * 32],
                       i["gmlp_w_s"][l, :, 0:32, 0:32].rearrange("g i j -> i g j"), [tp], [tp], tp)
                with nc.allow_non_contiguous_dma("tiny bias table"):
                    kb.dma("sp", biasT_s[b * 32:(b + 1) * 32, :],
                           i["gmlp_b_s"][l, :, 0:32].rearrange("g i -> i g"), [], [tp], tp)
            with nc.allow_non_contiguous_dma("tiny bias table"):
                kb.dma("sp", biasT[:], i["gmlp_b_s"][l].rearrange("g i -> i g"), [], [tp], tp)
            ptmp = self.ps(es, "p1_ptmp", [128, 8, 128], BF16)
            t_ptmp = PTr()
            triub = sb("triub", [128, 128], BF16)
            kb.op("dve", [self.tr_const], [tp], lambda e: e.tensor_copy(out=triub[:], in_=self.triuf[:]))
            kb.op("pe", [tp, self.tr_const], [t_ptmp], lambda e: e.transpose(ptmp[:, 0, :], triub[:], self.identb[:]))
            kb.op("dve", [t_ptmp], [tp], lambda e: e.tensor_copy(out=trilb[:], in_=ptmp[:, 0, :]))
            for (wt, wT) in ((wtmp, wsT), (wtmp_s, wsT_s)):
                kb.op("dve", [tp], [tp], lambda e: e.tensor_tensor(
                    out=wt[:], in0=wt[:], in1=trilb[:].unsqueeze(1).to_broadcast([128, 8, 128]), op=ALU.mult))
                for g in range(8):
                    kb.op("pe", [tp, self.tr_const], [t_ptmp],
                          lambda e: e.transpose(ptmp[:, g, :], wt[:, g, :], self.identb[:]))
                kb.op("dve", [t_ptmp], [tp], lambda e: e.tensor_copy(out=wT[:], in_=ptmp[:]))

            if self.stop_after == ("p1s", l):
                kb.barrier()
                return
            xt = sb("xt", [128, D], F32)
            t_xt = Tr()
            junk = sb("junk", [128, D], BF16)
            t_junk = Tr()
            hb = sb("hb", [128, D], BF16)
            t_hb = Tr()
            small = sb("small", [128, 16], F32)
            t_small = Tr()
            hT = [sb("hT%d" % k, [128, KC, 128], BF16) for k in range(4)]
            t_hT = [Tr() for _ in range(4)]
            wb = [sb("wb%d" % k, [128, KC, 512], BF16) for k in range(2)]
            t_wb = [Tr(), Tr()]
            acc = [self.ps(es, "p1_acc%d" % k, [128, 512], F32) for k in range(4)]
            t_acc = [PTr() for _ in range(4)]
            pmix = self.ps(es, "p1_pmix", [128, 2, 512], F32)
            t_pmix = PTr()
            pmisc = self.ps(es, "p1_pmisc", [128, 512], F32)
            t_pmisc = PTr()
            stg = [sb("stg%d" % k, [128, 512], F32) for k in range(3)]
            t_stg = [Tr() for _ in range(3)]
            stgb = [sb("stgb%d" % k, [128, 512], BF16) for k in range(3)]
            t_stgb = [Tr() for _ in range(3)]
            qTg = sb("qTg", [128, 8, 512], BF16)
            t_qTg = Tr()
            kTg = sb("kTg", [128, 8, 512], BF16)
            t_kTg = Tr()
            cTg = sb("cTg", [16, 512], BF16)
            t_cTg = Tr()
            ug = [sb("ug%d" % k, [128, 1024], F32) for k in range(4)]
            t_ug = [Tr() for _ in range(4)]
            vg = [sb("vg%d" % k, [128, 1024], F32) for k in range(4)]
            t_vg = [Tr() for _ in range(4)]
            vnb = sb("vnb", [128, 1024], BF16)
            t_vnb = Tr()
            ocb = sb("ocb", [128, 1024], BF16)
            t_ocb = Tr()
            lf = sb("lf", [128, 8], F32)
            t_lf = Tr()
            cc = sb("cc", [128, 40], F32)
            t_cc = Tr()
            chl = sb("chl", [128, 16], BF16)
            t_chl = Tr()
            carry = sb("carry", [128, 8], F32)
            t_carry = Tr()
            kb.op("dve", [], [t_carry], lambda e: e.memset(carry[:], 0.0))
            gtmp = sb("gtmp", [128, 512], BF16)
            t_gtmp = Tr()
            rr = {"stg": 0, "stgb": 0, "wb": 0}

            def next_stg():
                k = rr["stg"] % 3
                rr["stg"] += 1
                return stg[k], t_stg[k]

            def next_stgb():
                k = rr["stgb"] % 3
                rr["stgb"] += 1
                return stgb[k], t_stgb[k]

            w_l = i["w_in"][l].rearrange("(kc p) n -> p kc n", p=128)

            for grp in self.groups():
                is_s = grp[0] == NT
                ng = len(grp)
                r_g0 = grp[0] * 128
                for sl, t in enumerate(grp):
                    src = i["xs"][:, :] if (is_s and l == 0) else xin_p[t * 128:(t + 1) * 128, :] if not is_s else s["x"][TP:TP + 128, :]
                    if is_s and l == 0:
                        src = i["xs"][:, :]
                    elif is_s:
                        src = s["x"][TP:TP + 128, :]
                    else:
                        src = xin_p[t * 128:(t + 1) * 128, :]
                    kb.dma("sp", xt[:], src, [], [t_xt], t_xt)
                    kb.op("act", [t_xt], [t_junk, t_small], lambda e: e.activation(
                        out=junk[:], in_=xt[:], func=AF.Square, accum_out=small[:, 0:1]))
                    self.rmsnorm_rstd(kb, small[:, 0:1], small[:, 1:2], t_small, t_small, D, 1e-6)
                    kb.op("dve", [t_xt, t_small, tp], [t_hb], lambda e: e.scalar_tensor_tensor(
                        out=hb[:], in0=xt[:], scalar=small[:, 1:2], in1=g_bc[:], op0=ALU.mult, op1=ALU.mult))
                    if P1DBG < 2:
                        continue
                    for half in range(2):
                        for k8 in range(8):
                            kc = half * 8 + k8
                            kb.op("pe", [t_hb, self.tr_const], [t_ptmp], lambda e: e.transpose(
                                ptmp[:, k8, :], hb[:, kc * 128:(kc + 1) * 128], self.identb[:]))
                        kb.op("act", [t_ptmp], [t_hT[sl]], lambda e: e.activation(
                            out=hT[sl][:, half * 8:(half + 1) * 8, :], in_=ptmp[:], func=AF.Copy))
                for (kind, j, c0, w) in blocks:
                    if (KIND_FILTER and kind not in KIND_FILTER) or P1DBG < 3:
                        continue
                    ws = rr["wb"] % 2
                    rr["wb"] += 1
                    kb.dma("pool", wb[ws][:, :, 0:w], w_l[:, :, c0:c0 + w], [], [t_wb[ws]], t_wb[ws])
                    for sl, t in enumerate(grp):
                        r0 = t * 128
                        for kc in range(KC):
                            kb.op("pe", [t_hT[sl], t_wb[ws]], [t_acc[sl]], lambda e: e.matmul(
                                acc[sl][:, 0:w], hT[sl][:, kc, :], wb[ws][:, kc, 0:w],
                                start=(kc == 0), stop=(kc == KC - 1)))
                        A = acc[sl]
                        tA = t_acc[sl]
                        if P1DBG < 4:
                            continue
                        if kind == "q":
                            sg, tsg = next_stgb()
                            kb.op("act", [tA], [tsg], lambda e: e.activation(
                                out=sg[:], in_=A[:], func=AF.Copy, scale=ATT_SCALE))
                            for hh in range(4):
                                kb.op("pe", [tsg, self.tr_const], [t_ptmp], lambda e: e.transpose(
                                    ptmp[:, hh, :], sg[:, hh * 128:(hh + 1) * 128], self.identb[:]))
                            kb.op("dve", [t_ptmp], [t_qTg], lambda e: e.tensor_copy(
                                out=qTg[:, 4 * j:4 * j + 4, sl * 128:(sl + 1) * 128], in_=ptmp[:, 0:4, :]))
                        elif kind == "k":
                            sf, tsf = next_stg()
                            kb.op("dve", [tA], [tsf], lambda e: e.tensor_copy(out=sf[:], in_=A[:]))
                            dst = (o["ks"][l, :, c0 - 1024:c0 - 1024 + 512] if is_s
                                   else o["kp"][l, r0:r0 + 128, c0 - 1024:c0 - 1024 + 512])
                            kb.dma("sp", dst, sf[:], [tsf], [], tsf)
                            sg, tsg = next_stgb()
                            kb.op("act", [tA], [tsg], lambda e: e.activation(out=sg[:], in_=A[:], func=AF.Copy))
                            for hh in range(4):
                                kb.op("pe", [tsg, self.tr_const], [t_ptmp], lambda e: e.transpose(
                                    ptmp[:, hh, :], sg[:, hh * 128:(hh + 1) * 128], self.identb[:]))
                            kb.op("dve", [t_ptmp], [t_kTg], lambda e: e.tensor_copy(
                                out=kTg[:, 4 * j:4 * j + 4, sl * 128:(sl + 1) * 128], in_=ptmp[:, 0:4, :]))
                        elif kind == "v":
                            sf, tsf = next_stg()
                            kb.op("dve", [tA], [tsf], lambda e: e.tensor_copy(out=sf[:], in_=A[:]))
                            dst = (o["vs"][l, :, c0 - 2048:c0 - 2048 + 512] if is_s
                                   else o["vp"][l, r0:r0 + 128, c0 - 2048:c0 - 2048 + 512])
                            kb.dma("sp", dst, sf[:], [tsf], [], tsf)
                            sg, tsg = next_stgb()
                            kb.op("dve", [tA], [tsg], lambda e: e.tensor_copy(out=sg[:], in_=A[:]))
                            kb.dma("sp", s["v"][r0:r0 + 128, c0 - 2048:c0 - 2048 + 512], sg[:], [tsg], [], tsg)
                        elif kind == "f":
                            kb.op("dve", [tA, tp], [t_lf], lambda e: e.tensor_tensor(
                                out=lf[:], in0=A[:, 0:8], in1=bf_bc[:], op=ALU.add))
                            kb.op("act", [t_lf], [t_lf], lambda e: e.activation(
                                out=lf[:], in_=lf[:], func=AF.Exp, scale=-1.0))
                            kb.op("act", [t_lf], [t_lf], lambda e: e.activation(
                                out=lf[:], in_=lf[:], func=AF.Ln, bias=1.0))
                            kb.op("dve", [t_lf], [t_lf], lambda e: e.tensor_scalar(
                                out=lf[:], in0=lf[:], scalar1=-1.0, scalar2=None, op0=ALU.mult))
                            dst = o["lfs"][l, :, :] if is_s else o["lfp"][l, r0:r0 + 128, :]
                            kb.dma("sp", dst, lf[:], [t_lf], [], t_lf)
                            tri = self.triu32f if is_s else self.triuf
                            kb.op("pe", [t_lf, self.tr_const], [t_pmisc], lambda e: e.matmul(
                                pmisc[:, 0:8], tri[:], lf[:], start=True, stop=True))
                            kb.op("pe", [t_lf, self.tr_const], [t_pmisc], lambda e: e.matmul(
                                pmisc[:, 8:16], self.onesf[:], lf[:], start=True, stop=True))
                            if is_s:
                                kb.op("dve", [t_pmisc], [t_cc], lambda e: e.tensor_copy(out=cc[:, 0:8], in_=pmisc[:, 0:8]))
                            else:
                                kb.op("dve", [t_pmisc, t_carry], [t_cc], lambda e: e.tensor_tensor(
                                    out=cc[:, 0:8], in0=pmisc[:, 0:8], in1=carry[:], op=ALU.add))
                            if not is_s:
                                kb.op("dve", [t_pmisc, t_carry], [t_carry], lambda e: e.tensor_tensor(
                                    out=carry[:], in0=pmisc[:, 8:16], in1=carry[:], op=ALU.add))
                            kb.op("dve", [t_cc], [t_cc], lambda e: e.tensor_scalar(
                                out=cc[:, 8:16], in0=cc[:, 0:8], scalar1=-1.0, scalar2=None, op0=ALU.mult))
                            kb.dma("sp", s["negc"][r0:r0 + 128, :], cc[:, 8:16], [t_cc], [], t_cc)
                            kb.op("dve", [t_cc], [t_chl], lambda e: e.tensor_copy(out=chl[:, 0:8], in_=cc[:, 0:8]))
                            kb.op("dve", [t_chl, t_cc], [t_cc], lambda e: e.tensor_tensor(
                                out=cc[:, 16:24], in0=cc[:, 0:8], in1=chl[:, 0:8], op=ALU.subtract))
                            kb.op("dve", [t_cc], [t_chl], lambda e: e.tensor_copy(out=chl[:, 8:16], in_=cc[:, 16:24]))
                            kb.op("pe", [t_chl, self.tr_const], [t_ptmp], lambda e: e.transpose(
                                ptmp[0:16, 0, :], chl[:], self.identb[:]))
                            kb.op("dve", [t_ptmp], [t_cTg], lambda e: e.tensor_copy(
                                out=cTg[:, sl * 128:(sl + 1) * 128], in_=ptmp[0:16, 0, :]))
                        elif kind == "rw":
                            sf, tsf = next_stg()
                            kb.op("dve", [tA], [tsf], lambda e: e.tensor_copy(out=sf[:, 0:w], in_=A[:, 0:w]))
                            cc0 = c0 - O_RW
                            kb.dma("sp", s["prw"][r0:r0 + 128, cc0:cc0 + w], sf[:, 0:w], [tsf], [], tsf)
                            if is_s:
                                for b in range(4):
                                    kb.dma("sp", o["shs"][l, b:b + 1, cc0:cc0 + w],
                                           sf[b * 32 + 31:b * 32 + 32, 0:w], [tsf], [], tsf)
                            elif t == NT - 1:
                                kb.dma("sp", o["shp"][l:l + 1, cc0:cc0 + w], sf[127:128, 0:w], [tsf], [], tsf)
                        elif kind in ("gu", "gv"):
                            dstt, tdst = (ug[sl], t_ug[sl]) if kind == "gu" else (vg[sl], t_vg[sl])
                            dv = dstt[:, j * 512:(j + 1) * 512]
                            sf, tsf = next_stg()
                            kb.op("act", [tA], [tsf], lambda e: e.activation(out=sf[:], in_=A[:], func=AF.Square))
                            kb.op("dve", [tsf], [tsf], lambda e: e.tensor_scalar(
                                out=sf[:], in0=sf[:], scalar1=0.044715, scalar2=1.0, op0=ALU.mult, op1=ALU.add))
                            kb.op("dve", [tsf, tA], [tsf], lambda e: e.tensor_tensor(
                                out=sf[:], in0=sf[:], in1=A[:], op=ALU.mult))
                            kb.op("act", [tsf], [tsf], lambda e: e.activation(
                                out=sf[:], in_=sf[:], func=AF.Sigmoid, scale=1.5957691216))
                            kb.op("dve", [tsf, tA], [tdst], lambda e: e.tensor_tensor(
                                out=dv, in0=sf[:], in1=A[:], op=ALU.mult))
                            if kind == "gv" and j == 1:
                                V = vg[sl]
                                tV = t_vg[sl]
                                kb.op("dve", [tV], [t_small], lambda e: e.reduce_sum(
                                    out=small[:, 4:5], in_=V[:], axis=AX.X))
                                kb.op("dve", [t_small], [t_small], lambda e: e.tensor_scalar(
                                    out=small[:, 5:6], in0=small[:, 4:5], scalar1=-1.0 / 1024, scalar2=None, op0=ALU.mult))
                                kb.op("dve", [tV, t_small], [tV], lambda e: e.tensor_scalar(
                                    out=V[:], in0=V[:], scalar1=small[:, 5:6], scalar2=None, op0=ALU.add))
                                kb.op("act", [tV], [t_junk, t_small], lambda e: e.activation(
                                    out=junk[:, 0:1024], in_=V[:], func=AF.Square, accum_out=small[:, 6:7]))
                                self.rmsnorm_rstd(kb, small[:, 6:7], small[:, 7:8], t_small, t_small, 1024, 1e-5)
                                kb.op("dve", [tV, t_small, tp], [tV], lambda e: e.scalar_tensor_tensor(
                                    out=V[:], in0=V[:], scalar=small[:, 7:8], in1=lng_bc[:], op0=ALU.mult, op1=ALU.mult))
                                kb.op("dve", [tV, tp], [tV], lambda e: e.tensor_tensor(
                                    out=V[:], in0=V[:], in1=lnb_bc[:], op=ALU.add))
                                if is_s:
                                    kb.dma("sp", o["gvs"][l, :, :], V[:], [tV], [], tV)
                                kb.op("act", [tV], [t_vnb], lambda e: e.activation(out=vnb[:], in_=V[:], func=AF.Copy))
                                WT = wsT_s if is_s else wsT
                                BT = biasT_s if is_s else biasT
                                for g in range(8):
                                    kb.op("pe", [t_vnb, tp], [t_pmix], lambda e: e.matmul(
                                        pmix[:, g // 4, (g % 4) * 128:(g % 4 + 1) * 128], WT[:, g, :],
                                        vnb[:, g * 128:(g + 1) * 128], start=True, stop=True))
                                for g in range(8):
                                    kb.op("dve", [t_pmix, tp, t_ug[sl]], [t_ocb], lambda e: e.scalar_tensor_tensor(
                                        out=ocb[:, g * 128:(g + 1) * 128],
                                        in0=pmix[:, g // 4, (g % 4) * 128:(g % 4 + 1) * 128],
                                        scalar=BT[:, g:g + 1], in1=ug[sl][:, g * 128:(g + 1) * 128],
                                        op0=ALU.add, op1=ALU.mult))
                                kb.dma("sp", s["oc"][r0:r0 + 128, :], ocb[:], [t_ocb], [], t_ocb)
                        elif kind == "gate":
                            sg, tsg = next_stgb()
                            kb.op("act", [tA], [t_gtmp], lambda e: e.activation(out=gtmp[:], in_=A[:], func=AF.Sigmoid))
                            kb.op("dve", [t_gtmp], [tsg], lambda e: e.tensor_copy(out=sg[:], in_=gtmp[:]))
                            kb.dma("sp", s["gate"][r0:r0 + 128, j * 512:(j + 1) * 512], sg[:], [tsg], [], tsg)
                nt = ng * 128
                kb.dma("sp", s["qT"][:, :, r_g0:r_g0 + nt].rearrange("h d t -> d h t"), qTg[:, :, 0:nt],
                       [t_qTg], [], t_qTg)
                kb.dma("sp", s["kT"][:, :, r_g0:r_g0 + nt].rearrange("h d t -> d h t"), kTg[:, :, 0:nt],
                       [t_kTg], [], t_kTg)
                kb.dma("sp", s["cT"][:, r_g0:r_g0 + nt], cTg[:, 0:nt], [t_cTg], [], t_cTg)
            kb.barrier()


    def phase2a(self, l):
        nc, kb, i, o, s = self.nc, self.kb, self.i, self.o, self.s
        TP, NT = self.TP, self.NT
        NQB = TP // 512
        with ExitStack() as es:
            sb = lambda n, shp, dt: self.sb(es, "pa_" + n, shp, dt)
            triub = sb("triub", [128, 128], BF16)
            t_c = Tr()
            kb.op("dve", [self.tr_const], [t_c], lambda e: e.tensor_copy(out=triub[:], in_=self.triuf[:]))
            negc = sb("negc", [128, NT, 8], F32)
            cT = sb("cT", [16, TP + 128], BF16)
            with nc.allow_non_contiguous_dma("small bias table"):
                kb.dma("sp", negc[:], s["negc"][0:TP, :].rearrange("(kb p) h -> p kb h", p=128), [], [t_c], t_c)
            kb.dma("sp", cT[:], s["cT"][:, :], [], [t_c], t_c)
            KT = [sb("KT%d" % k, [128, TP], BF16) for k in range(2)]
            t_KT = [Tr(), Tr()]
            V = [sb("V%d" % k, [128, NT, 129], BF16) for k in range(2)]
            t_V = [Tr(), Tr()]
            for k in range(2):
                kb.op("dve", [], [t_V[k]], lambda e: e.memset(V[k][:, :, 128:129], 1.0))
            QT = [sb("QT%d" % k, [128, 512], BF16) for k in range(2)]
            t_QT = [Tr(), Tr()]
            pT = [sb("pT%d" % k, [128, 512], BF16) for k in range(3)]
            t_pT = [Tr() for _ in range(3)]
            osb = [sb("osb%d" % k, [128, 4, 128], BF16) for k in range(2)]
            t_osb = [Tr(), Tr()]
            rcp = sb("rcp", [128, 8], F32)
            t_rcp = Tr()
            ps_s = [self.ps(es, "pa_s%d" % k, [128, 512], F32) for k in range(2)]
            t_ps = [PTr(), PTr()]
            pob = [self.ps(es, "pa_o%d" % k, [128, 512], F32) for k in range(4)]
            t_pob = [PTr() for _ in range(4)]
            rr = {"ps": 0, "pT": 0, "po": 0, "QT": 0, "osb": 0}

            def rot(name, n):
                k = rr[name] % n
                rr[name] += 1
                return k

            for h in range(8):
                hs = h % 2
                kb.dma("sp", KT[hs][:], s["kT"][h, :, 0:TP], [], [t_KT[hs]], t_KT[hs])
                kb.dma("sp", V[hs][:, :, 0:128],
                       s["v"][0:TP, h * 128:(h + 1) * 128].rearrange("(kb p) d -> p kb d", p=128),
                       [], [t_V[hs]], t_V[hs])
                for j in range(NQB):
                    qs = rot("QT", 2)
                    kb.dma("sp", QT[qs][:], s["qT"][h, :, j * 512:(j + 1) * 512], [], [t_QT[qs]], t_QT[qs])
                    nkb = 4 * j + 4
                    for kbi in range(nkb):
                        i0 = max(0, kbi - 4 * j)
                        c_lo = i0 * 128
                        sk = rot("ps", 2)
                        kb.op("pe", [t_KT[hs], t_QT[qs]], [t_ps[sk]], lambda e: e.matmul(
                            ps_s[sk][:, c_lo:512], KT[hs][:, kbi * 128:(kbi + 1) * 128], QT[qs][:, c_lo:512],
                            start=True, stop=False))
                        kb.op("pe", [self.tr_const, t_c], [t_ps[sk]], lambda e: e.matmul(
                            ps_s[sk][:, c_lo:512], self.selb[0:16, h * 128:(h + 1) * 128],
                            cT[0:16, j * 512 + c_lo:(j + 1) * 512], start=False, stop=True))
                        tk = rot("pT", 3)
                        kb.op("act", [t_ps[sk], t_c], [t_pT[tk]], lambda e: e.activation(
                            out=pT[tk][:, c_lo:512], in_=ps_s[sk][:, c_lo:512], func=AF.Exp,
                            bias=negc[:, kbi, h:h + 1]))
                        if kbi >= 4 * j:
                            kb.op("pool", [t_pT[tk], t_c], [t_pT[tk]], lambda e: e.tensor_tensor(
                                out=pT[tk][:, c_lo:c_lo + 128], in0=pT[tk][:, c_lo:c_lo + 128],
                                in1=triub[:], op=ALU.mult))
                        for ii in range(i0, 4):
                            kb.op("pe", [t_pT[tk], t_V[hs]], [t_pob[ii]], lambda e: e.matmul(
                                pob[ii][:, 0:129], pT[tk][:, ii * 128:(ii + 1) * 128], V[hs][:, kbi, :],
                                start=(kbi == 0), stop=(kbi == 4 * j + ii)))
                    ok = rot("osb", 2)
                    for ii in range(4):
                        kb.op("dve", [t_pob[ii]], [t_rcp], lambda e: e.reciprocal(
                            out=rcp[:, ii:ii + 1], in_=pob[ii][:, 128:129]))
                        kb.op("dve", [t_pob[ii], t_rcp], [t_osb[ok]], lambda e: e.tensor_scalar(
                            out=osb[ok][:, ii, :], in0=pob[ii][:, 0:128], scalar1=rcp[:, ii:ii + 1],
                            scalar2=None, op0=ALU.mult))
                    kb.dma("sp", s["oa"][j * 512:(j + 1) * 512, h * 128:(h + 1) * 128].rearrange(
                        "(i p) d -> p i d", p=128), osb[ok][:], [t_osb[ok]], [], t_osb[ok])

            ckb = sb("ckb", [128, 8, 1024], BF16)
            t_ckb = Tr()
            KTc = sb("KTc", [128, 8, 1024], BF16)
            t_KTc = Tr()
            Vc = sb("Vc", [128, 8, 8, 129], BF16)
            t_Vc = Tr()
            kb.op("dve", [], [t_Vc], lambda e: e.memset(Vc[:, :, :, 128:129], 1.0))
            Vn = sb("Vn", [32, 8, 129], BF16)
            t_Vn = Tr()
            kb.op("dve", [], [t_Vn], lambda e: e.memset(Vn[:, :, 128:129], 1.0))
            KTn = sb("KTn", [128, 8, 32], BF16)
            t_KTn = Tr()
            QTs = sb("QTs", [128, 8, 32], BF16)
            t_QTs = Tr()
            clf = sb("clf", [128, 8, 8], F32)
            t_clf = Tr()
            cbias = sb("cbias", [128, 8, 8], F32)
            t_cb = Tr()
            carry = sb("carry", [128, 8], F32)
            t_carry = Tr()
            negn = sb("negn", [32, 8], F32)
            t_negn = Tr()
            pts = self.ps(es, "pa_pts", [128, 8, 128], BF16)
            t_pts = PTr()
            pcs = self.ps(es, "pa_pcs", [128, 512], F32)
            t_pcs = PTr()
            for b in range(4):
                r0 = TP + 32 * b
                kb.dma("pool", ckb[:], i["ck"][l, b].rearrange("(kb p) c -> p kb c", p=128), [], [t_ckb], t_ckb)
                for kbi in range(8):
                    kb.dma("pool", Vc[:, kbi, :, 0:128],
                           i["cv"][l, b, kbi * 128:(kbi + 1) * 128, :].rearrange("p (h d) -> p h d", h=8),
                           [], [t_Vc], t_Vc)
                with nc.allow_non_contiguous_dma("small table"):
                    kb.dma("sp", clf[:], i["clf"][l, b].rearrange("(kb p) h -> p kb h", p=128), [], [t_clf], t_clf)
                    kb.dma("sp", negn[:], s["negc"][r0:r0 + 32, :], [], [t_negn], t_negn)
                    kb.dma("sp", KTn[:], s["kT"][:, :, r0:r0 + 32].rearrange("h d t -> d h t"), [], [t_KTn], t_KTn)
                    kb.dma("sp", QTs[:], s["qT"][:, :, r0:r0 + 32].rearrange("h d t -> d h t"), [], [t_QTs], t_QTs)
                kb.dma("sp", Vn[:, :, 0:128], s["v"][r0:r0 + 32, :].rearrange("t (h d) -> t h d", h=8),
                       [], [t_Vn], t_Vn)
                for kbi in range(8):
                    for hh in range(8):
                        kb.op("pe", [t_ckb, self.tr_const], [t_pts], lambda e: e.transpose(
                            pts[:, hh, :], ckb[:, kbi, hh * 128:(hh + 1) * 128], self.identb[:]))
                    kb.op("act", [t_pts], [t_KTc], lambda e: e.activation(
                        out=KTc[:, :, kbi * 128:(kbi + 1) * 128], in_=pts[:], func=AF.Copy))
                kb.op("dve", [], [t_carry], lambda e: e.memset(carry[:], 0.0))
                for kbi in range(8):
                    kb.op("pe", [t_clf, self.tr_const], [t_pcs], lambda e: e.matmul(
                        pcs[:, 0:8], self.triuf[:], clf[:, kbi, :], start=True, stop=True))
                    kb.op("pe", [t_clf, self.tr_const], [t_pcs], lambda e: e.matmul(
                        pcs[:, 8:16], self.onesf[:], clf[:, kbi, :], start=True, stop=True))
                    kb.op("dve", [t_pcs, t_carry], [t_cb], lambda e: e.tensor_tensor(
                        out=cbias[:, kbi, :], in0=pcs[:, 0:8], in1=carry[:], op=ALU.add))
                    kb.op("dve", [t_pcs, t_carry], [t_carry], lambda e: e.tensor_tensor(
                        out=carry[:], in0=pcs[:, 8:16], in1=carry[:], op=ALU.add))
                kb.op("dve", [t_cb, t_carry], [t_cb], lambda e: e.tensor_tensor(
                    out=cbias[:], in0=carry[:].unsqueeze(1).to_broadcast([128, 8, 8]), in1=cbias[:], op=ALU.subtract))
                for h in range(8):
                    pk = rot("po", 4)
                    for kbi in range(9):
                        nk = 128 if kbi < 8 else 32
                        sk = rot("ps", 2)
                        lhs = KTc[:, h, kbi * 128:(kbi + 1) * 128] if kbi < 8 else KTn[:, h, :]
                        tl = t_KTc if kbi < 8 else t_KTn
                        kb.op("pe", [tl, t_QTs], [t_ps[sk]], lambda e: e.matmul(
                            ps_s[sk][0:nk, 0:32], lhs, QTs[:, h, :], start=True, stop=False))
                        kb.op("pe", [self.tr_const, t_c], [t_ps[sk]], lambda e: e.matmul(
                            ps_s[sk][0:nk, 0:32], self.selb[0:16, h * 128:h * 128 + nk],
                            cT[0:16, r0:r0 + 32], start=False, stop=True))
                        tk = rot("pT", 3)
                        bias = cbias[:, kbi, h:h + 1] if kbi < 8 else negn[:, h:h + 1]
                        tb_ = t_cb if kbi < 8 else t_negn
                        kb.op("act", [t_ps[sk], tb_], [t_pT[tk]], lambda e: e.activation(
                            out=pT[tk][0:nk, 0:32], in_=ps_s[sk][0:nk, 0:32], func=AF.Exp, bias=bias))
                        if kbi == 8:
                            kb.op("pool", [t_pT[tk], t_c], [t_pT[tk]], lambda e: e.tensor_tensor(
                                out=pT[tk][0:32, 0:32], in0=pT[tk][0:32, 0:32], in1=triub[0:32, 0:32], op=ALU.mult))
                        rhs = Vc[:, kbi, h, :] if kbi < 8 else Vn[:, h, :]
                        tv = t_Vc if kbi < 8 else t_Vn
                        kb.op("pe", [t_pT[tk], tv], [t_pob[pk]], lambda e: e.matmul(
                            pob[pk][0:32, 0:129], pT[tk][0:nk, 0:32], rhs, start=(kbi == 0), stop=(kbi == 8)))
                    ok = rot("osb", 2)
                    kb.op("dve", [t_pob[pk]], [t_rcp], lambda e: e.reciprocal(
                        out=rcp[0:32, 0:1], in_=pob[pk][0:32, 128:129]))
                    kb.op("dve", [t_pob[pk], t_rcp], [t_osb[ok]], lambda e: e.tensor_scalar(
                        out=osb[ok][0:32, 0, :], in0=pob[pk][0:32, 0:128], scalar1=rcp[0:32, 0:1],
                        scalar2=None, op0=ALU.mult))
                    kb.dma("sp", s["oa"][r0:r0 + 32, h * 128:(h + 1) * 128], osb[ok][0:32, 0, :],
                           [t_osb[ok]], [], t_osb[ok])
            kb.barrier()

    def phase2b(self, l):
        nc, kb, i, o, s = self.nc, self.kb, self.i, self.o, self.s
        TP = self.TP
        with ExitStack() as es:
            sb = lambda n, shp, dt: self.sb(es, "pb_" + n, shp, dt)
            h3 = lambda ap: ap.rearrange("p (h n) -> p h n", h=16)
            tprm = Tr()
            mu = sb("mu", [64, RWW], F32)
            prm = {}
            for nm in ("rwkv_w0", "rwkv_a0", "rwkv_k_k", "rwkv_k_a", "rwkv_r_k", "rwkv_gn_g", "rwkv_gn_b"):
                prm[nm] = sb(nm, [64, 1024], F32)
                kb.dma("sp", prm[nm][:], i[nm][l, :].partition_broadcast(64), [], [tprm], tprm)
            kb.dma("sp", mu[:], i["rwkv_mu"][l, :].partition_broadcast(64), [], [tprm], tprm)
            wup = sb("wup", [64, 1024], BF16)
            aup = sb("aup", [64, 1024], BF16)
            gup = sb("gup", [128, 1024], BF16)
            kb.dma("pool", wup[:], i["rwkv_w_up"][l], [], [tprm], tprm)
            kb.dma("pool", aup[:], i["rwkv_a_up"][l], [], [tprm], tprm)
            kb.dma("pool", gup[:], i["rwkv_g_up"][l], [], [tprm], tprm)
            mSU = sb("mSU", [64, 64], F32)
            mSL = sb("mSL", [64, 64], F32)
            kb.op("dve", [self.tr_const], [tprm], lambda e: e.tensor_tensor(
                out=mSU[:], in0=self.triuf[0:64, 0:64], in1=self.identf[0:64, 0:64], op=ALU.subtract))
            kb.op("dve", [self.tr_const], [tprm], lambda e: e.tensor_tensor(
                out=mSL[:], in0=self.onesf[0:64, 0:64], in1=self.triuf[0:64, 0:64], op=ALU.subtract))
            identb64 = self.identb
            names_f = ["p", "xs"]
            pt = sb("p", [64, RWW], F32); t_p = Tr()
            xs = sb("xs", [64, RWW], F32); t_xs = Tr()
            F = {}
            TF = {}
            for nm in ("logw", "a", "g", "kk", "k2", "t1", "t2", "t3", "Up", "Hf", "ysb"):
                F[nm] = sb(nm, [64, 1024], F32)
                TF[nm] = Tr()
            B = {}
            TB = {}
            for nm in ("At", "Bt", "Kt", "Rt", "Vb", "Zb", "Ub", "Hb", "outb", "BT", "KT", "P0", "P1", "Q0", "Q1",
                       "Tt0", "Tt1", "LakT", "ArbT", "ArkT", "WT"):
                B[nm] = sb(nm, [64, 1024], BF16)
                TB[nm] = Tr()
            ART = sb("ART", [64, 16, 2, 64], BF16); t_ART = Tr()
            lin = sb("lin", [64, 256], BF16); t_lin = Tr()
            linT = sb("linT", [128, 3, 64], BF16); t_linT = Tr()
            sm = sb("sm", [64, 96], F32); t_sm = Tr()
            gC = sb("gC", [64, 16], F32); t_gC = Tr()
            stio = sb("stio", [64, 16, 64], F32); t_stio = Tr()
            big = [self.ps(es, "pb_big%d" % k, [64, 1024], F32) for k in range(3)]
            t_big = [PTr() for _ in range(3)]
            ptb = [self.ps(es, "pb_ptb%d" % k, [128, 16, 64], BF16) for k in range(2)]
            t_ptb = [PTr(), PTr()]
            rr = {"big": 0, "ptb": 0}

            def rot(name, n):
                k = rr[name] % n
                rr[name] += 1
                return k

            def heads_mm(C, out_fn, lhs_fn, rhs_fn, reads, tout):
                for h in range(16):
                    kb.op("pe", reads, [tout], lambda e: e.matmul(out_fn(h), lhs_fn(h), rhs_fn(h), start=True, stop=True))

            def hs(h):
                return slice(h * 64, (h + 1) * 64)

            def transpose_heads(C, src, t_src, dst_fn, t_dst):
                k = rot("ptb", 2)
                for h in range(16):
                    kb.op("pe", [t_src, self.tr_const], [t_ptb[k]], lambda e: e.transpose(
                        ptb[k][0:64, h, 0:C], src[0:C, hs(h)], self.identb[0:C, 0:C]))
                kb.op("act", [t_ptb[k]], [t_dst], lambda e: e.activation(
                    out=dst_fn(), in_=ptb[k][0:64, :, 0:C], func=AF.Copy))

            def chunk(C, r0, prev_kind, prev_ap):
                nrounds = 5 if C == 64 else 4
                cs = slice(0, C)
                kb.dma("sp", pt[cs, :], s["prw"][r0:r0 + C, :], [], [t_p], t_p)
                if prev_kind == "zero":
                    kb.op("dve", [], [t_xs], lambda e: e.memset(xs[0:1, :], 0.0))
                    kb.dma("sp", xs[1:C, :], s["prw"][r0:r0 + C - 1, :], [], [t_xs], t_xs)
                elif prev_kind == "ap":
                    kb.dma("sp", xs[0:1, :], prev_ap, [], [t_xs], t_xs)
                    kb.dma("sp", xs[1:C, :], s["prw"][r0:r0 + C - 1, :], [], [t_xs], t_xs)
                else:
                    kb.dma("sp", xs[cs, :], s["prw"][r0 - 1:r0 + C - 1, :], [], [t_xs], t_xs)
                kb.op("pool", [t_xs, t_p], [t_xs], lambda e: e.tensor_tensor(out=xs[cs, :], in0=xs[cs, :], in1=pt[cs, :], op=ALU.subtract))
                kb.op("pool", [t_xs, tprm], [t_xs], lambda e: e.tensor_tensor(out=xs[cs, :], in0=xs[cs, :], in1=mu[cs, :], op=ALU.mult))
                kb.op("dve", [t_xs, t_p], [t_xs], lambda e: e.tensor_tensor(out=xs[cs, :], in0=xs[cs, :], in1=pt[cs, :], op=ALU.add))
                r_ = xs[cs, 0:1024]
                k_ = xs[cs, 1024:2048]
                v_ = xs[cs, 2048:3072]
                kb.op("act", [t_xs], [t_lin], lambda e: e.activation(out=lin[cs, 0:64], in_=xs[cs, 3072:3136], func=AF.Tanh))
                kb.op("act", [t_xs], [t_lin], lambda e: e.activation(out=lin[cs, 64:128], in_=xs[cs, 3136:3200], func=AF.Copy))
                kb.op("act", [t_xs], [t_lin], lambda e: e.activation(out=lin[cs, 128:256], in_=xs[cs, 3200:3328], func=AF.Sigmoid))
                k2 = rot("ptb", 2)
                kb.op("pe", [t_lin, self.tr_const], [t_ptb[k2]], lambda e: e.transpose(ptb[k2][0:64, 0, 0:C], lin[cs, 0:64], self.identb[0:C, 0:C]))
                kb.op("pe", [t_lin, self.tr_const], [t_ptb[k2]], lambda e: e.transpose(ptb[k2][0:64, 1, 0:C], lin[cs, 64:128], self.identb[0:C, 0:C]))
                kb.op("pe", [t_lin, self.tr_const], [t_ptb[k2]], lambda e: e.transpose(ptb[k2][0:128, 2, 0:C], lin[cs, 128:256], self.identb[0:C, 0:C]))
                kb.op("act", [t_ptb[k2]], [t_linT], lambda e: e.activation(out=linT[0:64, 0:2, 0:C], in_=ptb[k2][0:64, 0:2, 0:C], func=AF.Copy))
                kb.op("act", [t_ptb[k2]], [t_linT], lambda e: e.activation(out=linT[:, 2, 0:C], in_=ptb[k2][:, 2, 0:C], func=AF.Copy))
                bw = rot("big", 3)
                for hf in range(2):
                    kb.op("pe", [t_linT, tprm], [t_big[bw]], lambda e: e.matmul(
                        big[bw][cs, hf * 512:(hf + 1) * 512], linT[0:64, 0, 0:C], wup[:, hf * 512:(hf + 1) * 512], start=True, stop=True))
                kb.op("dve", [t_big[bw], tprm], [TF["logw"]], lambda e: e.tensor_tensor(
                    out=F["logw"][cs, :], in0=big[bw][cs, :], in1=prm["rwkv_w0"][cs, :], op=ALU.add))
                kb.op("act", [TF["logw"]], [TF["logw"]], lambda e: e.activation(out=F["logw"][cs, :], in_=F["logw"][cs, :], func=AF.Sigmoid))
                kb.op("pool", [TF["logw"]], [TF["logw"]], lambda e: e.tensor_scalar(
                    out=F["logw"][cs, :], in0=F["logw"][cs, :], scalar1=-0.6065306597126334, scalar2=None, op0=ALU.mult))
                ba = rot("big", 3)
                for hf in range(2):
                    kb.op("pe", [t_linT, tprm], [t_big[ba]], lambda e: e.matmul(
                        big[ba][cs, hf * 512:(hf + 1) * 512], linT[0:64, 1, 0:C], aup[:, hf * 512:(hf + 1) * 512], start=True, stop=True))
                kb.op("dve", [t_big[ba], tprm], [TF["a"]], lambda e: e.tensor_tensor(
                    out=F["a"][cs, :], in0=big[ba][cs, :], in1=prm["rwkv_a0"][cs, :], op=ALU.add))
                kb.op("act", [TF["a"]], [TF["a"]], lambda e: e.activation(out=F["a"][cs, :], in_=F["a"][cs, :], func=AF.Sigmoid))
                bg = rot("big", 3)
                for hf in range(2):
                    kb.op("pe", [t_linT, tprm], [t_big[bg]], lambda e: e.matmul(
                        big[bg][cs, hf * 512:(hf + 1) * 512], linT[0:128, 2, 0:C], gup[:, hf * 512:(hf + 1) * 512], start=True, stop=True))
                kb.op("act", [t_big[bg]], [TF["g"]], lambda e: e.activation(out=F["g"][cs, :], in_=big[bg][cs, :], func=AF.Copy))
                kb.op("pool", [t_xs, tprm], [TF["kk"]], lambda e: e.tensor_tensor(out=F["kk"][cs, :], in0=k_, in1=prm["rwkv_k_k"][cs, :], op=ALU.mult))
                kb.op("pool", [TF["kk"]], [TF["t1"]], lambda e: e.tensor_tensor(out=F["t1"][cs, :], in0=F["kk"][cs, :], in1=F["kk"][cs, :], op=ALU.mult))
                kb.op("dve", [TF["t1"]], [t_sm], lambda e: e.reduce_sum(out=sm[cs, 0:16], in_=h3(F["t1"][cs, :]), axis=AX.X))
                kb.op("dve", [t_sm], [t_sm], lambda e: e.tensor_scalar(out=sm[cs, 0:16], in0=sm[cs, 0:16], scalar1=1e-24, scalar2=None, op0=ALU.max))
                kb.op("act", [t_sm], [t_sm], lambda e: e.activation(out=sm[cs, 0:16], in_=sm[cs, 0:16], func=AF.Sqrt))
                kb.op("dve", [t_sm], [t_sm], lambda e: e.reciprocal(out=sm[cs, 0:16], in_=sm[cs, 0:16]))
                kb.op("dve", [TF["kk"], t_sm], [TF["kk"]], lambda e: e.tensor_tensor(
                    out=h3(F["kk"][cs, :]), in0=h3(F["kk"][cs, :]), in1=sm[cs, 0:16].unsqueeze(2).to_broadcast([C, 16, 64]), op=ALU.mult))
                kb.op("dve", [TF["a"], tprm], [TF["k2"]], lambda e: e.scalar_tensor_tensor(
                    out=F["k2"][cs, :], in0=F["a"][cs, :], scalar=-1.0, in1=prm["rwkv_k_a"][cs, :], op0=ALU.add, op1=ALU.mult))
                kb.op("dve", [TF["k2"], t_xs], [TF["k2"]], lambda e: e.scalar_tensor_tensor(
                    out=F["k2"][cs, :], in0=F["k2"][cs, :], scalar=1.0, in1=k_, op0=ALU.add, op1=ALU.mult))
                kb.op("pool", [t_xs, TF["k2"]], [TF["t1"]], lambda e: e.tensor_tensor(out=F["t1"][cs, :], in0=r_, in1=F["k2"][cs, :], op=ALU.mult))
                kb.op("pool", [TF["t1"], tprm], [TF["t1"]], lambda e: e.tensor_tensor(out=F["t1"][cs, :], in0=F["t1"][cs, :], in1=prm["rwkv_r_k"][cs, :], op=ALU.mult))
                kb.op("dve", [TF["t1"]], [t_sm], lambda e: e.reduce_sum(out=sm[cs, 16:32], in_=h3(F["t1"][cs, :]), axis=AX.X))
                bl = rot("big", 3)
                for hf in range(2):
                    kb.op("pe", [TF["logw"], self.tr_const], [t_big[bl]], lambda e: e.matmul(
                        big[bl][cs, hf * 512:(hf + 1) * 512], self.triuf[0:C, 0:C], F["logw"][cs, hf * 512:(hf + 1) * 512], start=True, stop=True))
                kb.op("act", [t_big[bl]], [TF["t1"]], lambda e: e.activation(out=F["t1"][cs, :], in_=big[bl][cs, :], func=AF.Exp))
                kb.op("act", [t_big[bl]], [TF["t2"]], lambda e: e.activation(out=F["t2"][cs, :], in_=big[bl][cs, :], func=AF.Exp, scale=-1.0))
                kb.op("dve", [t_big[bl], TF["logw"]], [TF["t3"]], lambda e: e.tensor_tensor(out=F["t3"][cs, :], in0=big[bl][cs, :], in1=F["logw"][cs, :], op=ALU.subtract))
                kb.op("act", [TF["t3"]], [TF["t3"]], lambda e: e.activation(out=F["t3"][cs, :], in_=F["t3"][cs, :], func=AF.Exp))
                bq = rot("big", 3)
                for h in range(16):
                    kb.op("pe", [TF["logw"], self.tr_const], [t_big[bq]], lambda e: e.matmul(
                        big[bq][0:64, h:h + 1], F["logw"][cs, hs(h)], self.onesf[0:C, 0:1], start=True, stop=True))
                kb.op("act", [t_big[bq]], [t_gC], lambda e: e.activation(out=gC[:], in_=big[bq][0:64, 0:16], func=AF.Exp))
                kb.op("dve", [TF["kk"], TF["t3"]], [TB["At"]], lambda e: e.scalar_tensor_tensor(
                    out=B["At"][cs, :], in0=F["kk"][cs, :], scalar=-1.0, in1=F["t3"][cs, :], op0=ALU.mult, op1=ALU.mult))
                kb.op("pool", [TF["kk"], TF["a"]], [TF["t3"]], lambda e: e.tensor_tensor(out=F["t3"][cs, :], in0=F["kk"][cs, :], in1=F["a"][cs, :], op=ALU.mult))
                kb.op("dve", [TF["t3"], TF["t2"]], [TB["Bt"]], lambda e: e.tensor_tensor(out=B["Bt"][cs, :], in0=F["t3"][cs, :], in1=F["t2"][cs, :], op=ALU.mult))
                kb.op("dve", [TF["k2"], TF["t2"]], [TB["Kt"]], lambda e: e.tensor_tensor(out=B["Kt"][cs, :], in0=F["k2"][cs, :], in1=F["t2"][cs, :], op=ALU.mult))
                kb.op("dve", [t_xs, TF["t1"]], [TB["Rt"]], lambda e: e.tensor_tensor(out=B["Rt"][cs, :], in0=r_, in1=F["t1"][cs, :], op=ALU.mult))
                kb.op("pool", [t_xs], [TB["Vb"]], lambda e: e.tensor_copy(out=B["Vb"][cs, :], in_=v_))
                transpose_heads(C, B["At"], TB["At"], lambda: ART[:, :, 0, 0:C], t_ART)
                transpose_heads(C, B["Rt"], TB["Rt"], lambda: ART[:, :, 1, 0:C], t_ART)
                transpose_heads(C, B["Bt"], TB["Bt"], lambda: h3(B["BT"][:, :])[:, :, 0:C], TB["BT"])
                transpose_heads(C, B["Kt"], TB["Kt"], lambda: h3(B["KT"][:, :])[:, :, 0:C], TB["KT"])
                BT3 = h3(B["BT"][:, :])
                KT3 = h3(B["KT"][:, :])
                b1 = rot("big", 3)
                heads_mm(C, lambda h: h3(big[b1][:, :])[cs, h, 0:C], lambda h: ART[:, h, 0, 0:C], lambda h: BT3[:, h, 0:C],
                         [t_ART, TB["BT"]], t_big[b1])
                kb.op("dve", [t_big[b1], tprm], [TB["Q0"]], lambda e: e.tensor_tensor(
                    out=h3(B["Q0"][:, :])[cs, :, 0:C], in0=h3(big[b1][:, :])[cs, :, 0:C],
                    in1=mSL[cs, 0:C].unsqueeze(1).to_broadcast([C, 16, C]), op=ALU.mult))
                for (lhsT3, tl, dstA, dstB) in ((BT3, TB["BT"], "P0", "ArbT"), (KT3, TB["KT"], "LakT", "ArkT")):
                    for half in range(2):
                        bb = rot("big", 3)
                        v4 = big[bb][:, :].rearrange("p (h two n) -> p h two n", h=8, two=2)
                        for hh in range(8):
                            h = half * 8 + hh
                            if C == 64:
                                kb.op("pe", [tl, t_ART], [t_big[bb]], lambda e: e.matmul(
                                    v4[cs, hh, :, 0:C], lhsT3[:, h, 0:C], ART[:, h, :, 0:C], start=True, stop=True))
                            else:
                                for two in range(2):
                                    kb.op("pe", [tl, t_ART], [t_big[bb]], lambda e: e.matmul(
                                        v4[cs, hh, two, 0:C], lhsT3[:, h, 0:C], ART[:, h, two, 0:C], start=True, stop=True))
                        dA = h3(B[dstA][:, :])[cs, half * 8:(half + 1) * 8, 0:C]
                        dB = h3(B[dstB][:, :])[cs, half * 8:(half + 1) * 8, 0:C]
                        kb.op("dve", [t_big[bb], tprm], [TB[dstA]], lambda e: e.tensor_tensor(
                            out=dA, in0=v4[cs, :, 0, 0:C], in1=mSU[cs, 0:C].unsqueeze(1).to_broadcast([C, 8, C]), op=ALU.mult))
                        kb.op("dve", [t_big[bb], self.tr_const], [TB[dstB]], lambda e: e.tensor_tensor(
                            out=dB, in0=v4[cs, :, 1, 0:C], in1=self.triuf[0:C, 0:C].unsqueeze(1).to_broadcast([C, 8, C]), op=ALU.mult))
                Pn, Qn, Tn = "P0", "Q0", "Tt0"
                kb.op("dve", [TB["P0"], self.tr_const], [TB["Tt0"]], lambda e: e.tensor_tensor(
                    out=h3(B["Tt0"][:, :])[cs, :, 0:C], in0=h3(B["P0"][:, :])[cs, :, 0:C],
                    in1=self.identf[0:C, 0:C].unsqueeze(1).to_broadcast([C, 16, C]), op=ALU.add))
                for rd in range(nrounds):
                    Pn2 = "P1" if Pn == "P0" else "P0"
                    Qn2 = "Q1" if Qn == "Q0" else "Q0"
                    Tn2 = "Tt1" if Tn == "Tt0" else "Tt0"
                    P3, Q3, T3 = h3(B[Pn][:, :]), h3(B[Qn][:, :]), h3(B[Tn][:, :])
                    bp = rot("big", 3)
                    heads_mm(C, lambda h: h3(big[bp][:, :])[cs, h, 0:C], lambda h: Q3[cs, h, 0:C], lambda h: P3[cs, h, 0:C],
                             [TB[Pn], TB[Qn]], t_big[bp])
                    kb.op("act", [t_big[bp]], [TB[Pn2]], lambda e: e.activation(
                        out=h3(B[Pn2][:, :])[cs, :, 0:C], in_=h3(big[bp][:, :])[cs, :, 0:C], func=AF.Copy))
                    bqq = rot("big", 3)
                    heads_mm(C, lambda h: h3(big[bqq][:, :])[cs, h, 0:C], lambda h: P3[cs, h, 0:C], lambda h: Q3[cs, h, 0:C],
                             [TB[Pn], TB[Qn]], t_big[bqq])
                    kb.op("act", [t_big[bqq]], [TB[Qn2]], lambda e: e.activation(
                        out=h3(B[Qn2][:, :])[cs, :, 0:C], in_=h3(big[bqq][:, :])[cs, :, 0:C], func=AF.Copy))
                    Q3n = h3(B[Qn2][:, :])
                    bt = rot("big", 3)
                    heads_mm(C, lambda h: h3(big[bt][:, :])[cs, h, 0:C], lambda h: Q3n[cs, h, 0:C], lambda h: T3[cs, h, 0:C],
                             [TB[Qn2], TB[Tn]], t_big[bt])
                    kb.op("dve", [t_big[bt], TB[Tn]], [TB[Tn2]], lambda e: e.tensor_tensor(
                        out=h3(B[Tn2][:, :])[cs, :, 0:C], in0=h3(big[bt][:, :])[cs, :, 0:C], in1=T3[cs, :, 0:C], op=ALU.add))
                    Pn, Qn, Tn = Pn2, Qn2, Tn2
                T3 = h3(B[Tn][:, :])
                LakT3, ArbT3, ArkT3 = h3(B["LakT"][:, :]), h3(B["ArbT"][:, :]), h3(B["ArkT"][:, :])
                bz = rot("big", 3)
                heads_mm(C, lambda h: big[bz][cs, hs(h)], lambda h: LakT3[cs, h, 0:C], lambda h: B["Vb"][cs, hs(h)],
                         [TB["LakT"], TB["Vb"]], t_big[bz])
                kb.op("act", [t_big[bz]], [TB["Zb"]], lambda e: e.activation(out=B["Zb"][cs, :], in_=big[bz][cs, :], func=AF.Copy))
                bu = rot("big", 3)
                heads_mm(C, lambda h: big[bu][cs, hs(h)], lambda h: T3[cs, h, 0:C], lambda h: B["Zb"][cs, hs(h)],
                         [TB[Tn], TB["Zb"]], t_big[bu])
                kb.op("act", [t_big[bu]], [TF["Up"]], lambda e: e.activation(out=F["Up"][cs, :], in_=big[bu][cs, :], func=AF.Copy))
                bwt = rot("big", 3)
                heads_mm(C, lambda h: h3(big[bwt][:, :])[0:64, h, 0:C], lambda h: B["At"][cs, hs(h)], lambda h: T3[cs, h, 0:C],
                         [TB["At"], TB[Tn]], t_big[bwt])
                WT3 = h3(B["WT"][:, :])
                kb.op("act", [t_big[bwt]], [TB["WT"]], lambda e: e.activation(
                    out=WT3[:, :, 0:C], in_=h3(big[bwt][:, :])[0:64, :, 0:C], func=AF.Copy))
                Hb3 = h3(B["Hb"][:, :])
                b_u = rot("big", 3)
                heads_mm(C, lambda h: big[b_u][cs, hs(h)], lambda h: WT3[:, h, 0:C], lambda h: Hb3[:, h, :],
                         [TB["WT"], TB["Hb"]], t_big[b_u])
                kb.op("dve", [t_big[b_u], TF["Up"]], [TB["Ub"]], lambda e: e.tensor_tensor(
                    out=B["Ub"][cs, :], in0=big[b_u][cs, :], in1=F["Up"][cs, :], op=ALU.add))
                b_y = rot("big", 3)
                for h in range(16):
                    kb.op("pe", [t_ART, TB["Hb"]], [t_big[b_y]], lambda e: e.matmul(
                        big[b_y][cs, hs(h)], ART[:, h, 1, 0:C], Hb3[:, h, :], start=True, stop=False))
                    kb.op("pe", [TB["ArbT"], TB["Ub"]], [t_big[b_y]], lambda e: e.matmul(
                        big[b_y][cs, hs(h)], ArbT3[cs, h, 0:C], B["Ub"][cs, hs(h)], start=False, stop=False))
                    kb.op("pe", [TB["ArkT"], TB["Vb"]], [t_big[b_y]], lambda e: e.matmul(
                        big[b_y][cs, hs(h)], ArkT3[cs, h, 0:C], B["Vb"][cs, hs(h)], start=False, stop=True))
                b_h = rot("big", 3)
                for h in range(16):
                    kb.op("pe", [TB["Bt"], TB["Ub"]], [t_big[b_h]], lambda e: e.matmul(
                        big[b_h][0:64, hs(h)], B["Bt"][cs, hs(h)], B["Ub"][cs, hs(h)], start=True, stop=False))
                    kb.op("pe", [TB["Kt"], TB["Vb"]], [t_big[b_h]], lambda e: e.matmul(
                        big[b_h][0:64, hs(h)], B["Kt"][cs, hs(h)], B["Vb"][cs, hs(h)], start=False, stop=True))
                kb.op("dve", [t_big[b_h], TF["Hf"]], [TF["Hf"]], lambda e: e.tensor_tensor(
                    out=F["Hf"][:, :], in0=big[b_h][0:64, :], in1=F["Hf"][:, :], op=ALU.add))
                kb.op("dve", [TF["Hf"], t_gC], [TF["Hf"]], lambda e: e.tensor_tensor(
                    out=h3(F["Hf"][:, :]), in0=h3(F["Hf"][:, :]), in1=gC[:, :].unsqueeze(2).to_broadcast([64, 16, 64]), op=ALU.mult))
                kb.op("pool", [TF["Hf"]], [TB["Hb"]], lambda e: e.tensor_copy(out=B["Hb"][:, :], in_=F["Hf"][:, :]))
                Y = F["ysb"]
                tY = TF["ysb"]
                kb.op("act", [t_big[b_y]], [tY], lambda e: e.activation(out=Y[cs, :], in_=big[b_y][cs, :], func=AF.Copy))
                kb.op("dve", [tY], [t_sm], lambda e: e.reduce_sum(out=sm[cs, 32:48], in_=h3(Y[cs, :]), axis=AX.X))
                kb.op("dve", [t_sm], [t_sm], lambda e: e.tensor_scalar(out=sm[cs, 32:48], in0=sm[cs, 32:48], scalar1=-1.0 / 64, scalar2=None, op0=ALU.mult))
                kb.op("dve", [tY, t_sm], [tY], lambda e: e.tensor_tensor(
                    out=h3(Y[cs, :]), in0=h3(Y[cs, :]), in1=sm[cs, 32:48].unsqueeze(2).to_broadcast([C, 16, 64]), op=ALU.add))
                kb.op("pool", [tY], [TF["t1"]], lambda e: e.tensor_tensor(out=F["t1"][cs, :], in0=Y[cs, :], in1=Y[cs, :], op=ALU.mult))
                kb.op("dve", [TF["t1"]], [t_sm], lambda e: e.reduce_sum(out=sm[cs, 48:64], in_=h3(F["t1"][cs, :]), axis=AX.X))
                kb.op("dve", [t_sm], [t_sm], lambda e: e.tensor_scalar(out=sm[cs, 48:64], in0=sm[cs, 48:64], scalar1=1.0 / 64, scalar2=64e-5, op0=ALU.mult, op1=ALU.add))
                kb.op("act", [t_sm], [t_sm], lambda e: e.activation(out=sm[cs, 48:64], in_=sm[cs, 48:64], func=AF.Sqrt))
                kb.op("dve", [t_sm], [t_sm], lambda e: e.reciprocal(out=sm[cs, 48:64], in_=sm[cs, 48:64]))
                kb.op("dve", [tY, t_sm], [tY], lambda e: e.tensor_tensor(
                    out=h3(Y[cs, :]), in0=h3(Y[cs, :]), in1=sm[cs, 48:64].unsqueeze(2).to_broadcast([C, 16, 64]), op=ALU.mult))
                kb.op("pool", [tY, tprm], [tY], lambda e: e.tensor_tensor(out=Y[cs, :], in0=Y[cs, :], in1=prm["rwkv_gn_g"][cs, :], op=ALU.mult))
                kb.op("pool", [tY, tprm], [tY], lambda e: e.tensor_tensor(out=Y[cs, :], in0=Y[cs, :], in1=prm["rwkv_gn_b"][cs, :], op=ALU.add))
                kb.op("dve", [t_xs, t_sm], [TF["t1"]], lambda e: e.tensor_tensor(
                    out=h3(F["t1"][cs, :]), in0=h3(v_), in1=sm[cs, 16:32].unsqueeze(2).to_broadcast([C, 16, 64]), op=ALU.mult))
                kb.op("pool", [tY, TF["t1"]], [tY], lambda e: e.tensor_tensor(out=Y[cs, :], in0=Y[cs, :], in1=F["t1"][cs, :], op=ALU.add))
                kb.op("dve", [tY, TF["g"]], [TB["outb"]], lambda e: e.tensor_tensor(out=B["outb"][cs, :], in0=Y[cs, :], in1=F["g"][cs, :], op=ALU.mult))
                kb.dma("sp", s["ob"][r0:r0 + C, :], B["outb"][cs, :], [TB["outb"]], [], TB["outb"])

            def store_state(dst_ap):
                bs = rot("big", 3)
                for h in range(16):
                    kb.op("pe", [TF["Hf"], self.tr_const], [t_big[bs]], lambda e: e.transpose(
                        big[bs][0:64, hs(h)], F["Hf"][:, hs(h)], self.identf[0:64, 0:64]))
                kb.op("dve", [t_big[bs]], [t_stio], lambda e: e.tensor_copy(out=stio[:], in_=h3(big[bs][:, :])))
                kb.dma("sp", dst_ap.rearrange("h i j -> i h j"), stio[:], [t_stio], [], t_stio)

            def load_state(src_ap):
                kb.dma("sp", stio[:], src_ap.rearrange("h i j -> i h j"), [], [t_stio], t_stio)
                bs = rot("big", 3)
                for h in range(16):
                    kb.op("pe", [t_stio, self.tr_const], [t_big[bs]], lambda e: e.transpose(
                        big[bs][0:64, hs(h)], stio[:, h, :], self.identf[0:64, 0:64]))
                kb.op("dve", [t_big[bs]], [TF["Hf"]], lambda e: e.tensor_copy(out=F["Hf"][:, :], in_=big[bs][:, :]))
                kb.op("pool", [TF["Hf"]], [TB["Hb"]], lambda e: e.tensor_copy(out=B["Hb"][:, :], in_=F["Hf"][:, :]))

            kb.op("dve", [], [TF["Hf"]], lambda e: e.memset(F["Hf"][:, :], 0.0))
            kb.op("dve", [], [TB["Hb"]], lambda e: e.memset(B["Hb"][:, :], 0.0))
            for c in range(TP // 64):
                chunk(64, c * 64, "zero" if c == 0 else "rows", None)
            store_state(o["stp"][l])
            for b in range(4):
                load_state(i["st"][l, b])
                chunk(32, TP + 32 * b, "ap", i["sh"][l, b:b + 1, :])
                store_state(o["sts"][l, b])
            kb.barrier()

    def dbg_fill(self, l):
        kb, i, s = self.kb, self.i, self.s
        with ExitStack() as es:
            a = self.sb(es, "dbg_a", [128, 1024], F32)
            b = self.sb(es, "dbg_b", [128, 1024], BF16)
            ta, tb = Tr(), Tr()
            for (src, dst) in ((i["dbg_oa"], s["oa"]), (i["dbg_ob"], s["ob"])):
                for t in range(self.NT + 1):
                    kb.dma("sp", a[:], src[t * 128:(t + 1) * 128, :], [], [ta], ta)
                    kb.op("dve", [ta], [tb], lambda e: e.tensor_copy(out=b[:], in_=a[:]))
                    kb.dma("sp", dst[t * 128:(t + 1) * 128, :], b[:], [tb], [], tb)
            kb.barrier()

    def phase3(self, l):
        nc, kb, i, o, s = self.nc, self.kb, self.i, self.o, self.s
        TP, NT = self.TP, self.NT
        last = (l == self.NL - 1)
        xin_p = i["xp"] if l == 0 else s["x"]
        with ExitStack() as es:
            sb = lambda n, shp, dt: self.sb(es, "p3_" + n, shp, dt)
            g_bc = sb("g_bc", [128, D], F32)
            t_g = Tr()
            xt = [sb("xt%d" % k, [128, D], F32) for k in range(4)]
            t_xt = [Tr() for _ in range(4)]
            yb = [sb("yb%d" % k, [128, D], F32) for k in range(4)]
            t_yb = [Tr() for _ in range(4)]
            big = [sb("big%d" % k, [128, 44, 128], BF16) for k in range(4)]
            t_big = [Tr() for _ in range(4)]
            mT = [sb("mT%d" % k, [128, KC, 128], BF16) for k in range(4)]
            t_mT = [Tr() for _ in range(4)]
            wb = [sb("wb%d" % k, [128, KC, 512], BF16) for k in range(2)]
            t_wb = [Tr(), Tr()]
            ld = [sb("ld%d" % k, [128, 1024], BF16) for k in range(2)]
            t_ld = [Tr(), Tr()]
            gt = [sb("gt%d" % k, [128, 512], BF16) for k in range(4)]
            t_gt = [Tr() for _ in range(4)]
            tmpf = [sb("tmpf%d" % k, [128, 512], F32) for k in range(2)]
            t_tmpf = [Tr(), Tr()]
            macc = [sb("macc%d" % k, [128, 512], F32) for k in range(4)]
            t_macc = [Tr() for _ in range(4)]
            tmpb = [sb("tmpb%d" % k, [128, 512], BF16) for k in range(2)]
            t_tmpb = [Tr(), Tr()]
            sil = [sb("sil%d" % k, [128, 512], F32) for k in range(4)]
            t_sil = [Tr() for _ in range(4)]
            hb = sb("hb", [128, D], BF16)
            t_hb = Tr()
            junk = sb("junk", [128, 512], BF16)
            t_junk = Tr()
            small = [sb("small%d" % k, [128, 16], F32) for k in range(4)]
            t_small = [Tr() for _ in range(4)]
            acc = [self.ps(es, "p3_acc%d" % k, [128, 512], F32) for k in range(4)]
            t_acc = [PTr() for _ in range(4)]
            ptmp = [self.ps(es, "p3_ptmp%d" % k, [128, 8, 128], BF16) for k in range(2)]
            t_ptmp = [PTr(), PTr()]
            rr = {"wb": 0, "ld": 0, "gt": 0, "tmpf": 0, "tmpb": 0, "pt": 0}

            def rot(name, n):
                k = rr[name] % n
                rr[name] += 1
                return k

            def transpose_into(src_ap_fn, t_src, nblk, dst_ap_fn, t_dst):
                b = 0
                while b < nblk:
                    n = min(8, nblk - b)
                    p = rot("pt", 2)
                    for q in range(n):
                        kb.op("pe", [t_src, self.tr_const], [t_ptmp[p]], lambda e: e.transpose(
                            ptmp[p][:, q, :], src_ap_fn(b + q), self.identb[:]))
                    kb.op("act", [t_ptmp[p]], [t_dst], lambda e: e.activation(
                        out=dst_ap_fn(b, n), in_=ptmp[p][:, 0:n, :], func=AF.Copy))
                    b += n

            def linear(grp, lhs_fn, t_lhs, W2d, K, ncb, evac, kpart=KC):
                nkc = K // 128
                Wv = W2d.rearrange("(kc p) n -> p kc n", p=128)
                for cb in range(ncb):
                    k0 = 0
                    while k0 < nkc:
                        nk = min(kpart, nkc - k0)
                        ws = rot("wb", 2)
                        kb.dma("pool", wb[ws][:, 0:nk, :], Wv[:, k0:k0 + nk, cb * 512:(cb + 1) * 512],
                               [], [t_wb[ws]], t_wb[ws])
                        for sl in range(len(grp)):
                            for kc in range(nk):
                                kb.op("pe", [t_lhs[sl], t_wb[ws]], [t_acc[sl]], lambda e: e.matmul(
                                    acc[sl][:], lhs_fn(sl, k0 + kc), wb[ws][:, kc, :],
                                    start=(k0 + kc == 0), stop=(k0 + kc == nkc - 1)))
                            if k0 + nk == nkc:
                                evac(sl, cb)
                        k0 += nk

            def rms_apply(sl, gname, dst_is_x):
                sm = small[sl]
                kb.op("dve", [t_small[sl]], [t_small[sl]], lambda e: e.reduce_sum(
                    out=sm[:, 4:5], in_=sm[:, 0:4], axis=AX.X))
                self.rmsnorm_rstd(kb, sm[:, 4:5], sm[:, 5:6], t_small[sl], t_small[sl], D, 1e-6)
                kb.op("dve", [t_yb[sl], t_small[sl], t_g], [t_yb[sl]], lambda e: e.scalar_tensor_tensor(
                    out=yb[sl][:], in0=yb[sl][:], scalar=sm[:, 5:6], in1=g_bc[:], op0=ALU.mult, op1=ALU.mult))
                kb.op("dve", [t_yb[sl], t_xt[sl]], [t_xt[sl]], lambda e: e.tensor_tensor(
                    out=xt[sl][:], in0=xt[sl][:], in1=yb[sl][:], op=ALU.add))

            br_w = [i["w_br_fox"][l], i["w_br_rwkv"][l], i["w_br_gmlp"][l]]
            br_o = [s["oa"], s["ob"], s["oc"]]

            for grp in self.groups():
                is_s = grp[0] == NT
                ng = len(grp)
                rows = [(TP if is_s else t * 128) for t in grp]
                for sl, t in enumerate(grp):
                    r0 = rows[sl]
                    if l == 0:
                        src = i["xs"][:, :] if is_s else i["xp"][r0:r0 + 128, :]
                    else:
                        src = s["x"][r0:r0 + 128, :]
                    kb.dma("sp", xt[sl][:], src, [], [t_xt[sl]], t_xt[sl])
                    for b in range(3):
                        k = rot("ld", 2)
                        kb.dma("sp", ld[k][:], br_o[b][r0:r0 + 128, :], [], [t_ld[k]], t_ld[k])
                        transpose_into(lambda q: ld[k][:, q * 128:(q + 1) * 128], t_ld[k], 8,
                                       lambda b0, n: big[sl][:, b * 8 + b0:b * 8 + b0 + n, :], t_big[sl])
                for cb in range(4):
                    for b in range(3):
                        def evacA(sl, _cb, b=b, cb=cb):
                            r0 = rows[sl]
                            gk = rot("gt", 4)
                            kb.dma("sp", gt[gk][:], s["gate"][r0:r0 + 128, b * 2048 + cb * 512:b * 2048 + (cb + 1) * 512],
                                   [], [t_gt[gk]], t_gt[gk])
                            if b == 0:
                                kb.op("dve", [t_acc[sl], t_gt[gk]], [t_macc[sl]], lambda e: e.tensor_tensor(
                                    out=macc[sl][:], in0=acc[sl][:], in1=gt[gk][:], op=ALU.mult))
                            else:
                                tk = rot("tmpf", 2)
                                kb.op("dve", [t_acc[sl], t_gt[gk]], [t_tmpf[tk]], lambda e: e.tensor_tensor(
                                    out=tmpf[tk][:], in0=acc[sl][:], in1=gt[gk][:], op=ALU.mult))
                                if b == 1:
                                    kb.op("dve", [t_tmpf[tk], t_macc[sl]], [t_macc[sl]], lambda e: e.tensor_tensor(
                                        out=macc[sl][:], in0=macc[sl][:], in1=tmpf[tk][:], op=ALU.add))
                                else:
                                    bk = rot("tmpb", 2)
                                    kb.op("dve", [t_tmpf[tk], t_macc[sl]], [t_tmpb[bk]], lambda e: e.tensor_tensor(
                                        out=tmpb[bk][:], in0=macc[sl][:], in1=tmpf[tk][:], op=ALU.add))
                                    transpose_into(lambda q: tmpb[bk][:, q * 128:(q + 1) * 128], t_tmpb[bk], 4,
                                                   lambda b0, n: mT[sl][:, cb * 4 + b0:cb * 4 + b0 + n, :], t_mT[sl])
                        Wv = br_w[b].rearrange("(kc p) n -> p kc n", p=128)
                        ws = rot("wb", 2)
                        kb.dma("pool", wb[ws][:, 0:8, :], Wv[:, :, cb * 512:(cb + 1) * 512], [], [t_wb[ws]], t_wb[ws])
                        for sl in range(ng):
                            for kc in range(8):
                                kb.op("pe", [t_big[sl], t_wb[ws]], [t_acc[sl]], lambda e: e.matmul(
                                    acc[sl][:], big[sl][:, b * 8 + kc, :], wb[ws][:, kc, :],
                                    start=(kc == 0), stop=(kc == 7)))
                            evacA(sl, cb)
                kb.dma("sp", g_bc[:], i["post_mix_g"][l, :].partition_broadcast(128), [], [t_g], t_g)

                def evacB(sl, cb):
                    kb.op("act", [t_acc[sl]], [t_junk, t_small[sl]], lambda e: e.activation(
                        out=junk[:], in_=acc[sl][:], func=AF.Square, accum_out=small[sl][:, cb:cb + 1]))
                    kb.op("dve", [t_acc[sl]], [t_yb[sl]], lambda e: e.tensor_copy(
                        out=yb[sl][:, cb * 512:(cb + 1) * 512], in_=acc[sl][:]))
                linear(grp, lambda sl, kc: mT[sl][:, kc, :], t_mT, i["w_out"][l], D, 4, evacB)
                for sl in range(ng):
                    rms_apply(sl, "post_mix_g", False)
                kb.dma("sp", g_bc[:], i["pre_ffn_g"][l, :].partition_broadcast(128), [], [t_g], t_g)
                for sl in range(ng):
                    sm = small[sl]
                    kb.op("act", [t_xt[sl]], [t_hb, t_small[sl]], lambda e: e.activation(
                        out=hb[:], in_=xt[sl][:], func=AF.Square, accum_out=sm[:, 6:7]))
                    self.rmsnorm_rstd(kb, sm[:, 6:7], sm[:, 7:8], t_small[sl], t_small[sl], D, 1e-6)
                    kb.op("dve", [t_xt[sl], t_small[sl], t_g], [t_hb], lambda e: e.scalar_tensor_tensor(
                        out=hb[:], in0=xt[sl][:], scalar=sm[:, 7:8], in1=g_bc[:], op0=ALU.mult, op1=ALU.mult))
                    transpose_into(lambda q: hb[:, q * 128:(q + 1) * 128], t_hb, 16,
                                   lambda b0, n: mT[sl][:, b0:b0 + n, :], t_mT[sl])
                for cb in range(DFF // 512):
                    def evac1(sl, _cb):
                        kb.op("act", [t_acc[sl]], [t_sil[sl]], lambda e: e.activation(
                            out=sil[sl][:], in_=acc[sl][:], func=AF.Silu))

                    def evac3(sl, _cb, cb=cb):
                        bk = rot("tmpb", 2)
                        kb.op("dve", [t_acc[sl], t_sil[sl]], [t_tmpb[bk]], lambda e: e.tensor_tensor(
                            out=tmpb[bk][:], in0=acc[sl][:], in1=sil[sl][:], op=ALU.mult))
                        transpose_into(lambda q: tmpb[bk][:, q * 128:(q + 1) * 128], t_tmpb[bk], 4,
                                       lambda b0, n: big[sl][:, cb * 4 + b0:cb * 4 + b0 + n, :], t_big[sl])
                    for (Wd, ev) in ((i["ffn_w1"][l], evac1), (i["ffn_w3"][l], evac3)):
                        Wv = Wd.rearrange("(kc p) n -> p kc n", p=128)
                        ws = rot("wb", 2)
                        kb.dma("pool", wb[ws][:], Wv[:, :, cb * 512:(cb + 1) * 512], [], [t_wb[ws]], t_wb[ws])
                        for sl in range(ng):
                            for kc in range(KC):
                                kb.op("pe", [t_mT[sl], t_wb[ws]], [t_acc[sl]], lambda e: e.matmul(
                                    acc[sl][:], mT[sl][:, kc, :], wb[ws][:, kc, :],
                                    start=(kc == 0), stop=(kc == KC - 1)))
                            ev(sl, cb)
                kb.dma("sp", g_bc[:], i["post_ffn_g"][l, :].partition_broadcast(128), [], [t_g], t_g)
                linear(grp, lambda sl, kc: big[sl][:, kc, :], t_big, i["ffn_w2"][l], DFF, 4, evacB, kpart=11)
                for sl in range(ng):
                    rms_apply(sl, "post_ffn_g", True)
                    r0 = rows[sl]
                    if last:
                        dst = o["ys"][:, :] if is_s else o["yp"][r0:r0 + 128, :]
                    else:
                        dst = s["x"][r0:r0 + 128, :]
                    kb.dma("sp", dst, xt[sl][:], [t_xt[sl]], [], t_xt[sl])
            kb.barrier()

_CACHE = {}
PROG_INPUTS = {}


def _consts():
    ident = np.eye(128, dtype=np.float32)
    triu = np.triu(np.ones((128, 128), np.float32))
    triu32 = np.zeros((128, 128), np.float32)
    for b in range(4):
        triu32[b * 32:(b + 1) * 32, b * 32:(b + 1) * 32] = np.triu(np.ones((32, 32), np.float32))
    sel = np.zeros((16, 8, 128), np.float32)
    for h in range(8):
        sel[h, h, :] = 1.0
        sel[8 + h, h, :] = 1.0
    return {"c_ident": ident, "c_triu": triu, "c_triu32": triu32, "c_sel": sel.reshape(16, 1024)}


WEIGHTS = ["pre_mix_g", "w_in", "fox_bf", "rwkv_mu", "rwkv_w0", "rwkv_w_up", "rwkv_a0", "rwkv_a_up", "rwkv_g_up",
           "rwkv_k_k", "rwkv_k_a", "rwkv_r_k", "rwkv_gn_g", "rwkv_gn_b", "gmlp_ln_g", "gmlp_ln_b", "gmlp_w_s",
           "gmlp_b_s", "w_br_fox", "w_br_rwkv", "w_br_gmlp", "w_out", "post_mix_g", "pre_ffn_g", "ffn_w1",
           "ffn_w3", "ffn_w2", "post_ffn_g"]


def run(inputs, TP, NL, stop_after=None, trace=False):
    key = (TP, NL, stop_after)
    if key not in _CACHE:
        pg = Prog(TP, NL, stop_after)
        _CACHE[key] = pg.build()
        PROG_INPUTS[key] = set(pg.i.keys())
    nc = _CACHE[key]
    f = lambda a: np.ascontiguousarray(np.asarray(a, dtype=np.float32))
    shared = {nm: f(inputs[nm])[:NL] for nm in WEIGHTS}
    shared["rwkv_r_k"] = shared["rwkv_r_k"].reshape(NL, 1024)
    shared.update(_consts())
    if DBG_OAB:
        shared["dbg_oa"] = f(inputs["dbg_oa"])
        shared["dbg_ob"] = f(inputs["dbg_ob"])
    xp = f(inputs["x_prompt"])
    xs = f(inputs["x_sample"])
    nb = xp.shape[0]
    nsb = xs.shape[0] // 4
    in_maps = []
    for c in range(NCORE):
        m = dict(shared)
        m["xp"] = np.ascontiguousarray(xp[c % nb, :TP])
        sc = c % nsb
        m["xs"] = np.ascontiguousarray(xs[4 * sc:4 * sc + 4].reshape(128, D))
        m["ck"] = np.ascontiguousarray(f(inputs["cache_fox_k"])[:NL, 4 * sc:4 * sc + 4].reshape(NL, 4, PAST, 1024))
        m["cv"] = np.ascontiguousarray(f(inputs["cache_fox_v"])[:NL, 4 * sc:4 * sc + 4].reshape(NL, 4, PAST, 1024))
        m["clf"] = np.ascontiguousarray(f(inputs["cache_fox_logf"])[:NL, 4 * sc:4 * sc + 4])
        m["st"] = np.ascontiguousarray(f(inputs["state_rwkv"])[:NL, 4 * sc:4 * sc + 4])
        m["sh"] = np.ascontiguousarray(f(inputs["state_rwkv_shift"])[:NL, 4 * sc:4 * sc + 4].reshape(NL, 4, RWW))
        in_maps.append({k: v for k, v in m.items() if k in PROG_INPUTS[key]})
    res = run_bass_kernel_spmd(nc, in_maps, core_ids=list(range(NCORE)), trace=trace)
    return res, nb, nsb


def assemble(res, TP, NL, nb, nsb):
    R = res.results
    cat_p = lambda k: np.stack([R[b][k] for b in range(nb)])
    yp = cat_p("yp")
    ys = np.concatenate([R[c]["ys"].reshape(4, 32, D) for c in range(nsb)], 0)
    kp = np.stack([R[b]["kp"] for b in range(nb)], 1).reshape(NL, nb, TP, 8, 128)
    vp = np.stack([R[b]["vp"] for b in range(nb)], 1).reshape(NL, nb, TP, 8, 128)
    lfp = np.stack([R[b]["lfp"] for b in range(nb)], 1)
    stp = np.stack([R[b]["stp"] for b in range(nb)], 1)
    shp = np.stack([R[b]["shp"] for b in range(nb)], 1).reshape(NL, nb, 1, RWW)
    ks = np.concatenate([R[c]["ks"].reshape(NL, 4, 32, 8, 128) for c in range(nsb)], 1)
    vs = np.concatenate([R[c]["vs"].reshape(NL, 4, 32, 8, 128) for c in range(nsb)], 1)
    lfs = np.concatenate([R[c]["lfs"].reshape(NL, 4, 32, 8) for c in range(nsb)], 1)
    sts = np.concatenate([R[c]["sts"] for c in range(nsb)], 1)
    shs = np.concatenate([R[c]["shs"].reshape(NL, 4, 1, RWW) for c in range(nsb)], 1)
    gvs = np.concatenate([R[c]["gvs"].reshape(NL, 4, 32, 1024) for c in range(nsb)], 1)
    return (yp, ys, kp, vp, lfp, stp, shp, ks, vs, lfs, sts, shs, gvs)


def kernel(**inputs):
    res, nb, nsb = run(inputs, TPROMPT, DEPTH)
    return assemble(res, TPROMPT, DEPTH, nb, nsb)
```

```python
import numpy as np
from contextlib import ExitStack
import concourse.bass as bass
import concourse.mybir as mybir
from concourse.bass_utils import run_bass_kernel_spmd

F32 = mybir.dt.float32
BF16 = mybir.dt.bfloat16
AF = mybir.ActivationFunctionType
ALU = mybir.AluOpType
AX = mybir.AxisListType

D = 2048
KC = D // 128
NCORE = 8
DEPTH = 4
TPROMPT = 8192
FOXH = 8
RH = 16
RN = 64
RWW = 3328
INW = 14600
DFF = 5632
PAST = 1024
O_F = 3072
O_RW = 3080
O_GU = O_RW + RWW
O_GV = O_GU + 1024
O_GATE = O_GV + 1024
ATT_SCALE = 128 ** -0.5
import os
KIND_FILTER = set(os.environ.get('P1KINDS', '').split(',')) - {''}
P1DBG = int(os.environ.get('P1DBG', '9'))
DBG_OAB = int(os.environ.get('DBG_OAB', '0'))
PIPE_RWKV = int(os.environ.get('PIPE_RWKV', '1'))
SKIP = set(os.environ.get('SKIP', '').split(',')) - {''}


class Tr:
    __slots__ = ("w", "r", "sem", "excl", "uid")
    _n = [0]

    def __init__(self, excl=False):
        Tr._n[0] += 1
        self.uid = Tr._n[0]
        self.w = None
        self.r = {}
        self.sem = None
        self.excl = excl


def PTr():
    return Tr(excl=True)


class KB:
    def __init__(self, nc, es):
        self.nc = nc
        self.es = es
        self.eng = {"pe": nc.tensor, "act": nc.scalar, "dve": nc.vector, "pool": nc.gpsimd, "sp": nc.sync}
        self.esem = {e: es.enter_context(nc.semaphore("s_" + e)) for e in ("pe", "act", "dve", "pool")}
        self.ecnt = {e: 0 for e in self.esem}
        self.waited = {}
        self.dtr = []
        self.nsem = 0
        self.sempool = {}

    def _wait(self, eng, tok):
        key, sem, val = tok
        if self.waited.get((eng, key), 0) >= val:
            return
        self.eng[eng].wait_ge(sem, val)
        self.waited[(eng, key)] = val

    def _deps(self, eng, reads, writes):
        for t in reads:
            if t.w is not None:
                self._dep1(eng, t.w)
        for t in writes:
            if t.w is not None:
                self._dep1(eng, t.w)
            for tok in t.r.values():
                self._dep1(eng, tok)

    def _dep1(self, eng, tok):
        if tok[0] == eng and eng == "pe":
            return
        self._wait(eng, tok)

    def op(self, eng, reads, writes, fn):
        if any(t.excl for t in reads):
            writes = list(writes) + [t for t in reads if t.excl]
            reads = [t for t in reads if not t.excl]
        self._deps(eng, reads, writes)
        ins = fn(self.eng[eng])
        self.ecnt[eng] += 1
        ins.then_inc(self.esem[eng], 1)
        tok = (eng, self.esem[eng], self.ecnt[eng])
        for t in reads:
            t.r[eng] = tok
        for t in writes:
            t.w = tok
            t.r = {}
        return ins

    def dma(self, q, out, in_, reads, writes, semtr):
        self._deps(q, reads, writes)
        if semtr.sem is None:
            semtr.sem = {}
        if q not in semtr.sem:
            pool = self.sempool.setdefault(q, [])
            if pool:
                semtr.sem[q] = pool.pop()
            else:
                semtr.sem[q] = [self.es.enter_context(self.nc.semaphore("d%d" % self.nsem)), 0]
                self.nsem += 1
            self.dtr.append((semtr, q))
        ent = semtr.sem[q]
        ins = self.eng[q].dma_start(out=out, in_=in_)
        ent[1] += 16
        ins.then_inc(ent[0], 16)
        key = (semtr.uid, q)
        tok = (key, ent[0], ent[1])
        for t in reads:
            t.r[key] = tok
        for t in writes:
            t.w = tok
            t.r = {}
        return ins

    def recycle(self, keep=()):
        kept = []
        for (t, q) in self.dtr:
            if any(t is k for k in keep):
                kept.append((t, q))
            else:
                self.sempool.setdefault(q, []).append(t.sem[q])
        self.dtr = kept

    def barrier(self, engines=("pe", "act", "dve", "pool", "sp")):
        for e in engines:
            for s in self.esem:
                if s != e and self.ecnt[s] > 0:
                    self._wait(e, (s, self.esem[s], self.ecnt[s]))
            for (t, q) in self.dtr:
                ent = t.sem[q]
                if ent[1] > 0:
                    self._wait(e, ((t.uid, q), ent[0], ent[1]))


def _colblocks():
    blocks = []
    for j in range(2):
        blocks.append(("q", j, j * 512, 512))
    for j in range(2):
        blocks.append(("k", j, 1024 + j * 512, 512))
    for j in range(2):
        blocks.append(("v", j, 2048 + j * 512, 512))
    blocks.append(("f", 0, O_F, 8))
    for j in range(6):
        blocks.append(("rw", j, O_RW + j * 512, 512))
    blocks.append(("rw", 6, O_RW + 3072, 256))
    for j in range(2):
        blocks.append(("gu", j, O_GU + j * 512, 512))
    for j in range(2):
        blocks.append(("gv", j, O_GV + j * 512, 512))
    for j in range(12):
        blocks.append(("gate", j, O_GATE + j * 512, 512))
    return blocks


class Prog:
    def __init__(self, TP, NL, stop_after=None):
        self.TP = TP
        self.NL = NL
        self.NT = TP // 128
        self.TT = TP + 128
        self.stop_after = stop_after
        self.nc = bass.Bass("TRN2", target_bir_lowering=False)
        self._declare()

    def _declare(self):
        nc, TP, NL, TT = self.nc, self.TP, self.NL, self.TT

        def inp(name, shape, dt=F32):
            return nc.dram_tensor(name, list(shape), dt, kind="ExternalInput").ap()

        def outp(name, shape, dt=F32):
            return nc.dram_tensor(name, list(shape), dt, kind="ExternalOutput").ap()

        def scr(name, shape, dt):
            return nc.dram_tensor(name, list(shape), dt, kind="Internal").ap()

        i = {}
        i["xp"] = inp("xp", [TP, D])
        i["xs"] = inp("xs", [128, D])
        if not (self.stop_after and self.stop_after[0].startswith("p1")):
            i["ck"] = inp("ck", [NL, 4, PAST, 1024])
            i["cv"] = inp("cv", [NL, 4, PAST, 1024])
        i["clf"] = inp("clf", [NL, 4, PAST, 8])
        i["st"] = inp("st", [NL, 4, RH, RN, RN])
        i["sh"] = inp("sh", [NL, 4, RWW])
        for nm, shp in (("pre_mix_g", [NL, D]), ("w_in", [NL, D, INW]), ("fox_bf", [NL, 8]),
                        ("rwkv_mu", [NL, RWW]), ("rwkv_w0", [NL, 1024]), ("rwkv_w_up", [NL, 64, 1024]),
                        ("rwkv_a0", [NL, 1024]), ("rwkv_a_up", [NL, 64, 1024]), ("rwkv_g_up", [NL, 128, 1024]),
                        ("rwkv_k_k", [NL, 1024]), ("rwkv_k_a", [NL, 1024]), ("rwkv_r_k", [NL, 1024]),
                        ("rwkv_gn_g", [NL, 1024]), ("rwkv_gn_b", [NL, 1024]),
                        ("gmlp_ln_g", [NL, 1024]), ("gmlp_ln_b", [NL, 1024]),
                        ("gmlp_w_s", [NL, 8, 128, 128]), ("gmlp_b_s", [NL, 8, 128]),
                        ("w_br_fox", [NL, 1024, D]), ("w_br_rwkv", [NL, 1024, D]), ("w_br_gmlp", [NL, 1024, D]),
                        ("w_out", [NL, D, D]), ("post_mix_g", [NL, D]), ("pre_ffn_g", [NL, D]),
                        ("ffn_w1", [NL, D, DFF]), ("ffn_w3", [NL, D, DFF]), ("ffn_w2", [NL, DFF, D]),
                        ("post_ffn_g", [NL, D])):
            if self.stop_after and self.stop_after[0].startswith("p1") and nm in (
                    "w_br_fox", "w_br_rwkv", "w_br_gmlp", "w_out", "ffn_w1", "ffn_w3", "ffn_w2"):
                continue
            i[nm] = inp(nm, shp)
        if DBG_OAB:
            i["dbg_oa"] = inp("dbg_oa", [TT, 1024])
            i["dbg_ob"] = inp("dbg_ob", [TT, 1024])
        i["c_ident"] = inp("c_ident", [128, 128])
        i["c_triu"] = inp("c_triu", [128, 128])
        i["c_triu32"] = inp("c_triu32", [128, 128])
        i["c_sel"] = inp("c_sel", [16, 8 * 128])
        self.i = i
        o = {}
        o["yp"] = outp("yp", [TP, D])
        o["ys"] = outp("ys", [128, D])
        o["kp"] = outp("kp", [NL, TP, 1024])
        o["vp"] = outp("vp", [NL, TP, 1024])
        o["lfp"] = outp("lfp", [NL, TP, 8])
        o["stp"] = outp("stp", [NL, RH, RN, RN])
        o["shp"] = outp("shp", [NL, RWW])
        o["ks"] = outp("ks", [NL, 128, 1024])
        o["vs"] = outp("vs", [NL, 128, 1024])
        o["lfs"] = outp("lfs", [NL, 128, 8])
        o["sts"] = outp("sts", [NL, 4, RH, RN, RN])
        o["shs"] = outp("shs", [NL, 4, RWW])
        o["gvs"] = outp("gvs", [NL, 128, 1024])
        self.o = o
        s = {}
        s["x"] = scr("s_x", [TT, D], F32)
        s["qT"] = scr("s_qT", [8, 128, TT], BF16)
        s["kT"] = scr("s_kT", [8, 128, TT], BF16)
        s["v"] = scr("s_v", [TT, 1024], BF16)
        s["cT"] = scr("s_cT", [16, TT], BF16)
        s["negc"] = scr("s_negc", [TT, 8], F32)
        s["prw"] = scr("s_prw", [TT, RWW], F32)
        s["oa"] = scr("s_oa", [TT, 1024], BF16)
        s["ob"] = scr("s_ob", [TT, 1024], BF16)
        s["oc"] = scr("s_oc", [TT, 1024], BF16)
        s["gate"] = scr("s_gate", [TT, 6144], BF16)
        self.s = s

    def sb(self, es, name, shape, dt):
        self._uid = getattr(self, "_uid", 0) + 1
        return es.enter_context(self.nc.sbuf_tensor("%s_%d" % (name, self._uid), list(shape), dt))

    def ps(self, es, name, shape, dt):
        self._uid = getattr(self, "_uid", 0) + 1
        return es.enter_context(self.nc.psum_tensor("%s_%d" % (name, self._uid), list(shape), dt))

    def groups(self):
        g = []
        t = 0
        while t < self.NT:
            g.append(list(range(t, min(t + 4, self.NT))))
            t += 4
        g.append([self.NT])
        return g

    def rmsnorm_rstd(self, kb, ssum_ap, rstd_ap, tr_ss, tr_rstd, n, eps):
        kb.op("dve", [tr_ss], [tr_rstd], lambda e: e.tensor_scalar(
            out=rstd_ap, in0=ssum_ap, scalar1=1.0 / n, scalar2=eps, op0=ALU.mult, op1=ALU.add))
        kb.op("act", [tr_rstd], [tr_rstd], lambda e: e.activation(out=rstd_ap, in_=rstd_ap, func=AF.Sqrt))
        kb.op("dve", [tr_rstd], [tr_rstd], lambda e: e.reciprocal(out=rstd_ap, in_=rstd_ap))

    def build(self):
        nc = self.nc
        with ExitStack() as es:
            kb = KB(nc, es)
            self.kb = kb
            self.identb = self.sb(es, "identb", [128, 128], BF16)
            self.identf = self.sb(es, "identf", [128, 128], F32)
            self.triuf = self.sb(es, "triuf", [128, 128], F32)
            self.triu32f = self.sb(es, "triu32f", [128, 128], F32)
            self.onesf = self.sb(es, "onesf", [128, 128], F32)
            self.selb = self.sb(es, "selb", [16, 1024], BF16)
            self.tr_const = Tr()
            c = self.tr_const
            kb.dma("pool", self.identb[:], self.i["c_ident"][:, :], [], [c], c)
            kb.dma("sp", self.identf[:], self.i["c_ident"][:, :], [], [c], c)
            kb.dma("sp", self.triuf[:], self.i["c_triu"][:, :], [], [c], c)
            kb.dma("sp", self.triu32f[:], self.i["c_triu32"][:, :], [], [c], c)
            kb.dma("pool", self.selb[:], self.i["c_sel"][:, :], [], [c], c)
            kb.op("dve", [], [c], lambda e: e.memset(self.onesf[:], 1.0))
            kb.barrier()
            for l in range(self.NL):
                self.phase1(l)
                kb.barrier()
                kb.recycle(keep=(self.tr_const,))
                if self.stop_after in (("p1", l), ("p1s", l)):
                    break
                if DBG_OAB:
                    self.dbg_fill(l)
                    kb.barrier()
                    kb.recycle(keep=(self.tr_const,))
                if "p2a" not in SKIP:
                    self.phase2a(l)
                    kb.barrier()
                    kb.recycle(keep=(self.tr_const,))
                if "p2b" not in SKIP:
                    self.phase2b(l)
                    kb.barrier()
                    kb.recycle(keep=(self.tr_const,))
                if "p3" not in SKIP:
                    self.phase3(l)
                kb.barrier()
                kb.recycle(keep=(self.tr_const,))
                if self.stop_after == ("p3", l):
                    break
            kb.barrier()
        return nc

    def phase1(self, l):
        nc, kb, i, o, s = self.nc, self.kb, self.i, self.o, self.s
        TP, NT = self.TP, self.NT
        xin_p = i["xp"] if l == 0 else s["x"]
        blocks = _colblocks()
        with ExitStack() as es:
            sb = lambda n, shp, dt: self.sb(es, "p1_" + n, shp, dt)
            g_bc = sb("g_bc", [128, D], F32)
            bf_bc = sb("bf_bc", [128, 8], F32)
            lng_bc = sb("lng_bc", [128, 1024], F32)
            lnb_bc = sb("lnb_bc", [128, 1024], F32)
            wsT = sb("wsT", [128, 8, 128], BF16)
            wsT_s = sb("wsT_s", [128, 8, 128], BF16)
            wtmp = sb("wtmp", [128, 8, 128], BF16)
            wtmp_s = sb("wtmp_s", [128, 8, 128], BF16)
            biasT = sb("biasT", [128, 8], F32)
            biasT_s = sb("biasT_s", [128, 8], F32)
            trilb = sb("trilb", [128, 128], BF16)
            tp = Tr()
            kb.dma("sp", g_bc[:], i["pre_mix_g"][l, :].partition_broadcast(128), [], [tp], tp)
            kb.dma("sp", bf_bc[:], i["fox_bf"][l, :].partition_broadcast(128), [], [tp], tp)
            kb.dma("sp", lng_bc[:], i["gmlp_ln_g"][l, :].partition_broadcast(128), [], [tp], tp)
            kb.dma("sp", lnb_bc[:], i["gmlp_ln_b"][l, :].partition_broadcast(128), [], [tp], tp)
            kb.dma("pool", wtmp[:], i["gmlp_w_s"][l].rearrange("g i j -> i g j"), [], [tp], tp)
            kb.op("dve", [], [tp], lambda e: e.memset(wtmp_s[:], 0.0))
            for b in range(4):
                kb.dma("pool", wtmp_s[b * 32:(b + 1) * 32, :, b * 32:(b + 1) * 32],
                       i["gmlp_w_s"][l, :, 0:32, 0:32].rearrange("g i j -> i g j"), [tp], [tp], tp)
                with nc.allow_non_contiguous_dma("tiny bias table"):
                    kb.dma("sp", biasT_s[b * 32:(b + 1) * 32, :],
                           i["gmlp_b_s"][l, :, 0:32].rearrange("g i -> i g"), [], [tp], tp)
            with nc.allow_non_contiguous_dma("tiny bias table"):
                kb.dma("sp", biasT[:], i["gmlp_b_s"][l].rearrange("g i -> i g"), [], [tp], tp)
            ptmp = self.ps(es, "p1_ptmp", [128, 8, 128], BF16)
            t_ptmp = PTr()
            triub = sb("triub", [128, 128], BF16)
            kb.op("dve", [self.tr_const], [tp], lambda e: e.tensor_copy(out=triub[:], in_=self.triuf[:]))
            kb.op("pe", [tp, self.tr_const], [t_ptmp], lambda e: e.transpose(ptmp[:, 0, :], triub[:], self.identb[:]))
            kb.op("dve", [t_ptmp], [tp], lambda e: e.tensor_copy(out=trilb[:], in_=ptmp[:, 0, :]))
            for (wt, wT) in ((wtmp, wsT), (wtmp_s, wsT_s)):
                kb.op("dve", [tp], [tp], lambda e: e.tensor_tensor(
                    out=wt[:], in0=wt[:], in1=trilb[:].unsqueeze(1).to_broadcast([128, 8, 128]), op=ALU.mult))
                for g in range(8):
                    kb.op("pe", [tp, self.tr_const], [t_ptmp],
                          lambda e: e.transpose(ptmp[:, g, :], wt[:, g, :], self.identb[:]))
                kb.op("dve", [t_ptmp], [tp], lambda e: e.tensor_copy(out=wT[:], in_=ptmp[:]))

            if self.stop_after == ("p1s", l):
                kb.barrier()
                return
            xt = sb("xt", [128, D], F32)
            t_xt = Tr()
            junk = sb("junk", [128, D], BF16)
            t_junk = Tr()
            hb = sb("hb", [128, D], BF16)
            t_hb = Tr()
            small = sb("small", [128, 16], F32)
            t_small = Tr()
            hT = [sb("hT%d" % k, [128, KC, 128], BF16) for k in range(4)]
            t_hT = [Tr() for _ in range(4)]
            wb = [sb("wb%d" % k, [128, KC, 512], BF16) for k in range(2)]
            t_wb = [Tr(), Tr()]
            acc = [self.ps(es, "p1_acc%d" % k, [128, 512], F32) for k in range(4)]
            t_acc = [PTr() for _ in range(4)]
            pmix = self.ps(es, "p1_pmix", [128, 2, 512], F32)
            t_pmix = PTr()
            pmisc = self.ps(es, "p1_pmisc", [128, 512], F32)
            t_pmisc = PTr()
            stg = [sb("stg%d" % k, [128, 512], F32) for k in range(3)]
            t_stg = [Tr() for _ in range(3)]
            stgb = [sb("stgb%d" % k, [128, 512], BF16) for k in range(3)]
            t_stgb = [Tr() for _ in range(3)]
            qTg = sb("qTg", [128, 8, 512], BF16)
            t_qTg = Tr()
            kTg = sb("kTg", [128, 8, 512], BF16)
            t_kTg = Tr()
            cTg = sb("cTg", [16, 512], BF16)
            t_cTg = Tr()
            ug = [sb("ug%d" % k, [128, 1024], F32) for k in range(4)]
            t_ug = [Tr() for _ in range(4)]
            vg = [sb("vg%d" % k, [128, 1024], F32) for k in range(4)]
            t_vg = [Tr() for _ in range(4)]
            vnb = sb("vnb", [128, 1024], BF16)
            t_vnb = Tr()
            ocb = sb("ocb", [128, 1024], BF16)
            t_ocb = Tr()
            lf = sb("lf", [128, 8], F32)
            t_lf = Tr()
            cc = sb("cc", [128, 40], F32)
            t_cc = Tr()
            chl = sb("chl", [128, 16], BF16)
            t_chl = Tr()
            carry = sb("carry", [128, 8], F32)
            t_carry = Tr()
            kb.op("dve", [], [t_carry], lambda e: e.memset(carry[:], 0.0))
            gtmp = sb("gtmp", [128, 512], BF16)
            t_gtmp = Tr()
            rr = {"stg": 0, "stgb": 0, "wb": 0}

            def next_stg():
                k = rr["stg"] % 3
                rr["stg"] += 1
                return stg[k], t_stg[k]

            def next_stgb():
                k = rr["stgb"] % 3
                rr["stgb"] += 1
                return stgb[k], t_stgb[k]

            w_l = i["w_in"][l].rearrange("(kc p) n -> p kc n", p=128)

            for grp in self.groups():
                is_s = grp[0] == NT
                ng = len(grp)
                r_g0 = grp[0] * 128
                for sl, t in enumerate(grp):
                    src = i["xs"][:, :] if (is_s and l == 0) else xin_p[t * 128:(t + 1) * 128, :] if not is_s else s["x"][TP:TP + 128, :]
                    if is_s and l == 0:
                        src = i["xs"][:, :]
                    elif is_s:
                        src = s["x"][TP:TP + 128, :]
                    else:
                        src = xin_p[t * 128:(t + 1) * 128, :]
                    kb.dma("sp", xt[:], src, [], [t_xt], t_xt)
                    kb.op("act", [t_xt], [t_junk, t_small], lambda e: e.activation(
                        out=junk[:], in_=xt[:], func=AF.Square, accum_out=small[:, 0:1]))
                    self.rmsnorm_rstd(kb, small[:, 0:1], small[:, 1:2], t_small, t_small, D, 1e-6)
                    kb.op("dve", [t_xt, t_small, tp], [t_hb], lambda e: e.scalar_tensor_tensor(
                        out=hb[:], in0=xt[:], scalar=small[:, 1:2], in1=g_bc[:], op0=ALU.mult, op1=ALU.mult))
                    if P1DBG < 2:
                        continue
                    for half in range(2):
                        for k8 in range(8):
                            kc = half * 8 + k8
                            kb.op("pe", [t_hb, self.tr_const], [t_ptmp], lambda e: e.transpose(
                                ptmp[:, k8, :], hb[:, kc * 128:(kc + 1) * 128], self.identb[:]))
                        kb.op("act", [t_ptmp], [t_hT[sl]], lambda e: e.activation(
                            out=hT[sl][:, half * 8:(half + 1) * 8, :], in_=ptmp[:], func=AF.Copy))
                for (kind, j, c0, w) in blocks:
                    if (KIND_FILTER and kind not in KIND_FILTER) or P1DBG < 3:
                        continue
                    ws = rr["wb"] % 2
                    rr["wb"] += 1
                    kb.dma("pool", wb[ws][:, :, 0:w], w_l[:, :, c0:c0 + w], [], [t_wb[ws]], t_wb[ws])
                    for sl, t in enumerate(grp):
                        r0 = t * 128
                        for kc in range(KC):
                            kb.op("pe", [t_hT[sl], t_wb[ws]], [t_acc[sl]], lambda e: e.matmul(
                                acc[sl][:, 0:w], hT[sl][:, kc, :], wb[ws][:, kc, 0:w],
                                start=(kc == 0), stop=(kc == KC - 1)))
                        A = acc[sl]
                        tA = t_acc[sl]
                        if P1DBG < 4:
                            continue
                        if kind == "q":
                            sg, tsg = next_stgb()
                            kb.op("act", [tA], [tsg], lambda e: e.activation(
                                out=sg[:], in_=A[:], func=AF.Copy, scale=ATT_SCALE))
                            for hh in range(4):
                                kb.op("pe", [tsg, self.tr_const], [t_ptmp], lambda e: e.transpose(
                                    ptmp[:, hh, :], sg[:, hh * 128:(hh + 1) * 128], self.identb[:]))
                            kb.op("dve", [t_ptmp], [t_qTg], lambda e: e.tensor_copy(
                                out=qTg[:, 4 * j:4 * j + 4, sl * 128:(sl + 1) * 128], in_=ptmp[:, 0:4, :]))
                        elif kind == "k":
                            sf, tsf = next_stg()
                            kb.op("dve", [tA], [tsf], lambda e: e.tensor_copy(out=sf[:], in_=A[:]))
                            dst = (o["ks"][l, :, c0 - 1024:c0 - 1024 + 512] if is_s
                                   else o["kp"][l, r0:r0 + 128, c0 - 1024:c0 - 1024 + 512])
                            kb.dma("sp", dst, sf[:], [tsf], [], tsf)
                            sg, tsg = next_stgb()
                            kb.op("act", [tA], [tsg], lambda e: e.activation(out=sg[:], in_=A[:], func=AF.Copy))
                            for hh in range(4):
                                kb.op("pe", [tsg, self.tr_const], [t_ptmp], lambda e: e.transpose(
                                    ptmp[:, hh, :], sg[:, hh * 128:(hh + 1) * 128], self.identb[:]))
                            kb.op("dve", [t_ptmp], [t_kTg], lambda e: e.tensor_copy(
                                out=kTg[:, 4 * j:4 * j + 4, sl * 128:(sl + 1) * 128], in_=ptmp[:, 0:4, :]))
                        elif kind == "v":
                            sf, tsf = next_stg()
                            kb.op("dve", [tA], [tsf], lambda e: e.tensor_copy(out=sf[:], in_=A[:]))
                            dst = (o["vs"][l, :, c0 - 2048:c0 - 2048 + 512] if is_s
                                   else o["vp"][l, r0:r0 + 128, c0 - 2048:c0 - 2048 + 512])
                            kb.dma("sp", dst, sf[:], [tsf], [], tsf)
                            sg, tsg = next_stgb()
                            kb.op("dve", [tA], [tsg], lambda e: e.tensor_copy(out=sg[:], in_=A[:]))
                            kb.dma("sp", s["v"][r0:r0 + 128, c0 - 2048:c0 - 2048 + 512], sg[:], [tsg], [], tsg)
                        elif kind == "f":
                            kb.op("dve", [tA, tp], [t_lf], lambda e: e.tensor_tensor(
                                out=lf[:], in0=A[:, 0:8], in1=bf_bc[:], op=ALU.add))
                            kb.op("act", [t_lf], [t_lf], lambda e: e.activation(
                                out=lf[:], in_=lf[:], func=AF.Exp, scale=-1.0))
                            kb.op("act", [t_lf], [t_lf], lambda e: e.activation(
                                out=lf[:], in_=lf[:], func=AF.Ln, bias=1.0))
                            kb.op("dve", [t_lf], [t_lf], lambda e: e.tensor_scalar(
                                out=lf[:], in0=lf[:], scalar1=-1.0, scalar2=None, op0=ALU.mult))
                            dst = o["lfs"][l, :, :] if is_s else o["lfp"][l, r0:r0 + 128, :]
                            kb.dma("sp", dst, lf[:], [t_lf], [], t_lf)
                            tri = self.triu32f if is_s else self.triuf
                            kb.op("pe", [t_lf, self.tr_const], [t_pmisc], lambda e: e.matmul(
                                pmisc[:, 0:8], tri[:], lf[:], start=True, stop=True))
                            kb.op("pe", [t_lf, self.tr_const], [t_pmisc], lambda e: e.matmul(
                                pmisc[:, 8:16], self.onesf[:], lf[:], start=True, stop=True))
                            if is_s:
                                kb.op("dve", [t_pmisc], [t_cc], lambda e: e.tensor_copy(out=cc[:, 0:8], in_=pmisc[:, 0:8]))
                            else:
                                kb.op("dve", [t_pmisc, t_carry], [t_cc], lambda e: e.tensor_tensor(
                                    out=cc[:, 0:8], in0=pmisc[:, 0:8], in1=carry[:], op=ALU.add))
                            if not is_s:
                                kb.op("dve", [t_pmisc, t_carry], [t_carry], lambda e: e.tensor_tensor(
                                    out=carry[:], in0=pmisc[:, 8:16], in1=carry[:], op=ALU.add))
                            kb.op("dve", [t_cc], [t_cc], lambda e: e.tensor_scalar(
                                out=cc[:, 8:16], in0=cc[:, 0:8], scalar1=-1.0, scalar2=None, op0=ALU.mult))
                            kb.dma("sp", s["negc"][r0:r0 + 128, :], cc[:, 8:16], [t_cc], [], t_cc)
                            kb.op("dve", [t_cc], [t_chl], lambda e: e.tensor_copy(out=chl[:, 0:8], in_=cc[:, 0:8]))
                            kb.op("dve", [t_chl, t_cc], [t_cc], lambda e: e.tensor_tensor(
                                out=cc[:, 16:24], in0=cc[:, 0:8], in1=chl[:, 0:8], op=ALU.subtract))
                            kb.op("dve", [t_cc], [t_chl], lambda e: e.tensor_copy(out=chl[:, 8:16], in_=cc[:, 16:24]))
                            kb.op("pe", [t_chl, self.tr_const], [t_ptmp], lambda e: e.transpose(
                                ptmp[0:16, 0, :], chl[:], self.identb[:]))
                            kb.op("dve", [t_ptmp], [t_cTg], lambda e: e.tensor_copy(
                                out=cTg[:, sl * 128:(sl + 1) * 128], in_=ptmp[0:16, 0, :]))
                        elif kind == "rw":
                            sf, tsf = next_stg()
                            kb.op("dve", [tA], [tsf], lambda e: e.tensor_copy(out=sf[:, 0:w], in_=A[:, 0:w]))
                            cc0 = c0 - O_RW
                            kb.dma("sp", s["prw"][r0:r0 + 128, cc0:cc0 + w], sf[:, 0:w], [tsf], [], tsf)
                            if is_s:
                                for b in range(4):
                                    kb.dma("sp", o["shs"][l, b:b + 1, cc0:cc0 + w],
                                           sf[b * 32 + 31:b * 32 + 32, 0:w], [tsf], [], tsf)
                            elif t == NT - 1:
                                kb.dma("sp", o["shp"][l:l + 1, cc0:cc0 + w], sf[127:128, 0:w], [tsf], [], tsf)
                        elif kind in ("gu", "gv"):
                            dstt, tdst = (ug[sl], t_ug[sl]) if kind == "gu" else (vg[sl], t_vg[sl])
                            dv = dstt[:, j * 512:(j + 1) * 512]
                            sf, tsf = next_stg()
                            kb.op("act", [tA], [tsf], lambda e: e.activation(out=sf[:], in_=A[:], func=AF.Square))
                            kb.op("dve", [tsf], [tsf], lambda e: e.tensor_scalar(
                                out=sf[:], in0=sf[:], scalar1=0.044715, scalar2=1.0, op0=ALU.mult, op1=ALU.add))
                            kb.op("dve", [tsf, tA], [tsf], lambda e: e.tensor_tensor(
                                out=sf[:], in0=sf[:], in1=A[:], op=ALU.mult))
                            kb.op("act", [tsf], [tsf], lambda e: e.activation(
                                out=sf[:], in_=sf[:], func=AF.Sigmoid, scale=1.5957691216))
                            kb.op("dve", [tsf, tA], [tdst], lambda e: e.tensor_tensor(
                                out=dv, in0=sf[:], in1=A[:], op=ALU.mult))
                            if kind == "gv" and j == 1:
                                V = vg[sl]
                                tV = t_vg[sl]
                                kb.op("dve", [tV], [t_small], lambda e: e.reduce_sum(
                                    out=small[:, 4:5], in_=V[:], axis=AX.X))
                                kb.op("dve", [t_small], [t_small], lambda e: e.tensor_scalar(
                                    out=small[:, 5:6], in0=small[:, 4:5], scalar1=-1.0 / 1024, scalar2=None, op0=ALU.mult))
                                kb.op("dve", [tV, t_small], [tV], lambda e: e.tensor_scalar(
                                    out=V[:], in0=V[:], scalar1=small[:, 5:6], scalar2=None, op0=ALU.add))
                                kb.op("act", [tV], [t_junk, t_small], lambda e: e.activation(
                                    out=junk[:, 0:1024], in_=V[:], func=AF.Square, accum_out=small[:, 6:7]))
                                self.rmsnorm_rstd(kb, small[:, 6:7], small[:, 7:8], t_small, t_small, 1024, 1e-5)
                                kb.op("dve", [tV, t_small, tp], [tV], lambda e: e.scalar_tensor_tensor(
                                    out=V[:], in0=V[:], scalar=small[:, 7:8], in1=lng_bc[:], op0=ALU.mult, op1=ALU.mult))
                                kb.op("dve", [tV, tp], [tV], lambda e: e.tensor_tensor(
                                    out=V[:], in0=V[:], in1=lnb_bc[:], op=ALU.add))
                                if is_s:
                                    kb.dma("sp", o["gvs"][l, :, :], V[:], [tV], [], tV)
                                kb.op("act", [tV], [t_vnb], lambda e: e.activation(out=vnb[:], in_=V[:], func=AF.Copy))
                                WT = wsT_s if is_s else wsT
                                BT = biasT_s if is_s else biasT
                                for g in range(8):
                                    kb.op("pe", [t_vnb, tp], [t_pmix], lambda e: e.matmul(
                                        pmix[:, g // 4, (g % 4) * 128:(g % 4 + 1) * 128], WT[:, g, :],
                                        vnb[:, g * 128:(g + 1) * 128], start=True, stop=True))
                                for g in range(8):
                                    kb.op("dve", [t_pmix, tp, t_ug[sl]], [t_ocb], lambda e: e.scalar_tensor_tensor(
                                        out=ocb[:, g * 128:(g + 1) * 128],
                                        in0=pmix[:, g // 4, (g % 4) * 128:(g % 4 + 1) * 128],
                                        scalar=BT[:, g:g + 1], in1=ug[sl][:, g * 128:(g + 1) * 128],
                                        op0=ALU.add, op1=ALU.mult))
                                kb.dma("sp", s["oc"][r0:r0 + 128, :], ocb[:], [t_ocb], [], t_ocb)
                        elif kind == "gate":
                            sg, tsg = next_stgb()
                            kb.op("act", [tA], [t_gtmp], lambda e: e.activation(out=gtmp[:], in_=A[:], func=AF.Sigmoid))
                            kb.op("dve", [t_gtmp], [tsg], lambda e: e.tensor_copy(out=sg[:], in_=gtmp[:]))
                            kb.dma("sp", s["gate"][r0:r0 + 128, j * 512:(j + 1) * 512], sg[:], [tsg], [], tsg)
                nt = ng * 128
                kb.dma("sp", s["qT"][:, :, r_g0:r_g0 + nt].rearrange("h d t -> d h t"), qTg[:, :, 0:nt],
                       [t_qTg], [], t_qTg)
                kb.dma("sp", s["kT"][:, :, r_g0:r_g0 + nt].rearrange("h d t -> d h t"), kTg[:, :, 0:nt],
                       [t_kTg], [], t_kTg)
                kb.dma("sp", s["cT"][:, r_g0:r_g0 + nt], cTg[:, 0:nt], [t_cTg], [], t_cTg)
            kb.barrier()


    def phase2a(self, l):
        nc, kb, i, o, s = self.nc, self.kb, self.i, self.o, self.s
        TP, NT = self.TP, self.NT
        NQB = TP // 512
        with ExitStack() as es:
            sb = lambda n, shp, dt: self.sb(es, "pa_" + n, shp, dt)
            triub = sb("triub", [128, 128], BF16)
            t_c = Tr()
            kb.op("dve", [self.tr_const], [t_c], lambda e: e.tensor_copy(out=triub[:], in_=self.triuf[:]))
            negc = sb("negc", [128, NT, 8], F32)
            cT = sb("cT", [16, TP + 128], BF16)
            with nc.allow_non_contiguous_dma("small bias table"):
                kb.dma("sp", negc[:], s["negc"][0:TP, :].rearrange("(kb p) h -> p kb h", p=128), [], [t_c], t_c)
            kb.dma("sp", cT[:], s["cT"][:, :], [], [t_c], t_c)
            KT = [sb("KT%d" % k, [128, TP], BF16) for k in range(2)]
            t_KT = [Tr(), Tr()]
            V = [sb("V%d" % k, [128, NT, 129], BF16) for k in range(2)]
            t_V = [Tr(), Tr()]
            for k in range(2):
                kb.op("dve", [], [t_V[k]], lambda e: e.memset(V[k][:, :, 128:129], 1.0))
            QT = [sb("QT%d" % k, [128, 512], BF16) for k in range(2)]
            t_QT = [Tr(), Tr()]
            pT = [sb("pT%d" % k, [128, 512], BF16) for k in range(3)]
            t_pT = [Tr() for _ in range(3)]
            osb = [sb("osb%d" % k, [128, 4, 128], BF16) for k in range(2)]
            t_osb = [Tr(), Tr()]
            rcp = sb("rcp", [128, 8], F32)
            t_rcp = Tr()
            ps_s = [self.ps(es, "pa_s%d" % k, [128, 512], F32) for k in range(2)]
            t_ps = [PTr(), PTr()]
            pob = [self.ps(es, "pa_o%d" % k, [128, 512], F32) for k in range(4)]
            t_pob = [PTr() for _ in range(4)]
            rr = {"ps": 0, "pT": 0, "po": 0, "QT": 0, "osb": 0}

            def rot(name, n):
                k = rr[name] % n
                rr[name] += 1
                return k

            for h in range(8):
                hs = h % 2
                kb.dma("sp", KT[hs][:], s["kT"][h, :, 0:TP], [], [t_KT[hs]], t_KT[hs])
                kb.dma("sp", V[hs][:, :, 0:128],
                       s["v"][0:TP, h * 128:(h + 1) * 128].rearrange("(kb p) d -> p kb d", p=128),
                       [], [t_V[hs]], t_V[hs])
                for j in range(NQB):
                    qs = rot("QT", 2)
                    kb.dma("sp", QT[qs][:], s["qT"][h, :, j * 512:(j + 1) * 512], [], [t_QT[qs]], t_QT[qs])
                    nkb = 4 * j + 4
                    for kbi in range(nkb):
                        i0 = max(0, kbi - 4 * j)
                        c_lo = i0 * 128
                        sk = rot("ps", 2)
                        kb.op("pe", [t_KT[hs], t_QT[qs]], [t_ps[sk]], lambda e: e.matmul(
                            ps_s[sk][:, c_lo:512], KT[hs][:, kbi * 128:(kbi + 1) * 128], QT[qs][:, c_lo:512],
                            start=True, stop=False))
                        kb.op("pe", [self.tr_const, t_c], [t_ps[sk]], lambda e: e.matmul(
                            ps_s[sk][:, c_lo:512], self.selb[0:16, h * 128:(h + 1) * 128],
                            cT[0:16, j * 512 + c_lo:(j + 1) * 512], start=False, stop=True))
                        tk = rot("pT", 3)
                        kb.op("act", [t_ps[sk], t_c], [t_pT[tk]], lambda e: e.activation(
                            out=pT[tk][:, c_lo:512], in_=ps_s[sk][:, c_lo:512], func=AF.Exp,
                            bias=negc[:, kbi, h:h + 1]))
                        if kbi >= 4 * j:
                            kb.op("pool", [t_pT[tk], t_c], [t_pT[tk]], lambda e: e.tensor_tensor(
                                out=pT[tk][:, c_lo:c_lo + 128], in0=pT[tk][:, c_lo:c_lo + 128],
                                in1=triub[:], op=ALU.mult))
                        for ii in range(i0, 4):
                            kb.op("pe", [t_pT[tk], t_V[hs]], [t_pob[ii]], lambda e: e.matmul(
                                pob[ii][:, 0:129], pT[tk][:, ii * 128:(ii + 1) * 128], V[hs][:, kbi, :],
                                start=(kbi == 0), stop=(kbi == 4 * j + ii)))
                    ok = rot("osb", 2)
                    for ii in range(4):
                        kb.op("dve", [t_pob[ii]], [t_rcp], lambda e: e.reciprocal(
                            out=rcp[:, ii:ii + 1], in_=pob[ii][:, 128:129]))
                        kb.op("dve", [t_pob[ii], t_rcp], [t_osb[ok]], lambda e: e.tensor_scalar(
                            out=osb[ok][:, ii, :], in0=pob[ii][:, 0:128], scalar1=rcp[:, ii:ii + 1],
                            scalar2=None, op0=ALU.mult))
                    kb.dma("sp", s["oa"][j * 512:(j + 1) * 512, h * 128:(h + 1) * 128].rearrange(
                        "(i p) d -> p i d", p=128), osb[ok][:], [t_osb[ok]], [], t_osb[ok])

            ckb = sb("ckb", [128, 8, 1024], BF16)
            t_ckb = Tr()
            KTc = sb("KTc", [128, 8, 1024], BF16)
            t_KTc = Tr()
            Vc = sb("Vc", [128, 8, 8, 129], BF16)
            t_Vc = Tr()
            kb.op("dve", [], [t_Vc], lambda e: e.memset(Vc[:, :, :, 128:129], 1.0))
            Vn = sb("Vn", [32, 8, 129], BF16)
            t_Vn = Tr()
            kb.op("dve", [], [t_Vn], lambda e: e.memset(Vn[:, :, 128:129], 1.0))
            KTn = sb("KTn", [128, 8, 32], BF16)
            t_KTn = Tr()
            QTs = sb("QTs", [128, 8, 32], BF16)
            t_QTs = Tr()
            clf = sb("clf", [128, 8, 8], F32)
            t_clf = Tr()
            cbias = sb("cbias", [128, 8, 8], F32)
            t_cb = Tr()
            carry = sb("carry", [128, 8], F32)
            t_carry = Tr()
            negn = sb("negn", [32, 8], F32)
            t_negn = Tr()
            pts = self.ps(es, "pa_pts", [128, 8, 128], BF16)
            t_pts = PTr()
            pcs = self.ps(es, "pa_pcs", [128, 512], F32)
            t_pcs = PTr()
            for b in range(4):
                r0 = TP + 32 * b
                kb.dma("pool", ckb[:], i["ck"][l, b].rearrange("(kb p) c -> p kb c", p=128), [], [t_ckb], t_ckb)
                for kbi in range(8):
                    kb.dma("pool", Vc[:, kbi, :, 0:128],
                           i["cv"][l, b, kbi * 128:(kbi + 1) * 128, :].rearrange("p (h d) -> p h d", h=8),
                           [], [t_Vc], t_Vc)
                with nc.allow_non_contiguous_dma("small table"):
                    kb.dma("sp", clf[:], i["clf"][l, b].rearrange("(kb p) h -> p kb h", p=128), [], [t_clf], t_clf)
                    kb.dma("sp", negn[:], s["negc"][r0:r0 + 32, :], [], [t_negn], t_negn)
                    kb.dma("sp", KTn[:], s["kT"][:, :, r0:r0 + 32].rearrange("h d t -> d h t"), [], [t_KTn], t_KTn)
                    kb.dma("sp", QTs[:], s["qT"][:, :, r0:r0 + 32].rearrange("h d t -> d h t"), [], [t_QTs], t_QTs)
                kb.dma("sp", Vn[:, :, 0:128], s["v"][r0:r0 + 32, :].rearrange("t (h d) -> t h d", h=8),
                       [], [t_Vn], t_Vn)
                for kbi in range(8):
                    for hh in range(8):
                        kb.op("pe", [t_ckb, self.tr_const], [t_pts], lambda e: e.transpose(
                            pts[:, hh, :], ckb[:, kbi, hh * 128:(hh + 1) * 128], self.identb[:]))
                    kb.op("act", [t_pts], [t_KTc], lambda e: e.activation(
                        out=KTc[:, :, kbi * 128:(kbi + 1) * 128], in_=pts[:], func=AF.Copy))
                kb.op("dve", [], [t_carry], lambda e: e.memset(carry[:], 0.0))
                for kbi in range(8):
                    kb.op("pe", [t_clf, self.tr_const], [t_pcs], lambda e: e.matmul(
                        pcs[:, 0:8], self.triuf[:], clf[:, kbi, :], start=True, stop=True))
                    kb.op("pe", [t_clf, self.tr_const], [t_pcs], lambda e: e.matmul(
                        pcs[:, 8:16], self.onesf[:], clf[:, kbi, :], start=True, stop=True))
                    kb.op("dve", [t_pcs, t_carry], [t_cb], lambda e: e.tensor_tensor(
                        out=cbias[:, kbi, :], in0=pcs[:, 0:8], in1=carry[:], op=ALU.add))
                    kb.op("dve", [t_pcs, t_carry], [t_carry], lambda e: e.tensor_tensor(
                        out=carry[:], in0=pcs[:, 8:16], in1=carry[:], op=ALU.add))
                kb.op("dve", [t_cb, t_carry], [t_cb], lambda e: e.tensor_tensor(
                    out=cbias[:], in0=carry[:].unsqueeze(1).to_broadcast([128, 8, 8]), in1=cbias[:], op=ALU.subtract))
                for h in range(8):
                    pk = rot("po", 4)
                    for kbi in range(9):
                        nk = 128 if kbi < 8 else 32
                        sk = rot("ps", 2)
                        lhs = KTc[:, h, kbi * 128:(kbi + 1) * 128] if kbi < 8 else KTn[:, h, :]
                        tl = t_KTc if kbi < 8 else t_KTn
                        kb.op("pe", [tl, t_QTs], [t_ps[sk]], lambda e: e.matmul(
                            ps_s[sk][0:nk, 0:32], lhs, QTs[:, h, :], start=True, stop=False))
                        kb.op("pe", [self.tr_const, t_c], [t_ps[sk]], lambda e: e.matmul(
                            ps_s[sk][0:nk, 0:32], self.selb[0:16, h * 128:h * 128 + nk],
                            cT[0:16, r0:r0 + 32], start=False, stop=True))
                        tk = rot("pT", 3)
                        bias = cbias[:, kbi, h:h + 1] if kbi < 8 else negn[:, h:h + 1]
                        tb_ = t_cb if kbi < 8 else t_negn
                        kb.op("act", [t_ps[sk], tb_], [t_pT[tk]], lambda e: e.activation(
                            out=pT[tk][0:nk, 0:32], in_=ps_s[sk][0:nk, 0:32], func=AF.Exp, bias=bias))
                        if kbi == 8:
                            kb.op("pool", [t_pT[tk], t_c], [t_pT[tk]], lambda e: e.tensor_tensor(
                                out=pT[tk][0:32, 0:32], in0=pT[tk][0:32, 0:32], in1=triub[0:32, 0:32], op=ALU.mult))
                        rhs = Vc[:, kbi, h, :] if kbi < 8 else Vn[:, h, :]
                        tv = t_Vc if kbi < 8 else t_Vn
                        kb.op("pe", [t_pT[tk], tv], [t_pob[pk]], lambda e: e.matmul(
                            pob[pk][0:32, 0:129], pT[tk][0:nk, 0:32], rhs, start=(kbi == 0), stop=(kbi == 8)))
                    ok = rot("osb", 2)
                    kb.op("dve", [t_pob[pk]], [t_rcp], lambda e: e.reciprocal(
                        out=rcp[0:32, 0:1], in_=pob[pk][0:32, 128:129]))
                    kb.op("dve", [t_pob[pk], t_rcp], [t_osb[ok]], lambda e: e.tensor_scalar(
                        out=osb[ok][0:32, 0, :], in0=pob[pk][0:32, 0:128], scalar1=rcp[0:32, 0:1],
                        scalar2=None, op0=ALU.mult))
                    kb.dma("sp", s["oa"][r0:r0 + 32, h * 128:(h + 1) * 128], osb[ok][0:32, 0, :],
                           [t_osb[ok]], [], t_osb[ok])
            kb.barrier()

    def phase2b(self, l):
        nc, kb, i, o, s = self.nc, self.kb, self.i, self.o, self.s
        TP = self.TP
        with ExitStack() as es:
            sb = lambda n, shp, dt: self.sb(es, "pb_" + n, shp, dt)
            h3 = lambda ap: ap.rearrange("p (h n) -> p h n", h=16)
            tprm = Tr()
            mu = sb("mu", [64, RWW], F32)
            prm = {}
            for nm in ("rwkv_w0", "rwkv_a0", "rwkv_k_k", "rwkv_k_a", "rwkv_r_k", "rwkv_gn_g", "rwkv_gn_b"):
                prm[nm] = sb(nm, [64, 1024], F32)
                kb.dma("sp", prm[nm][:], i[nm][l, :].partition_broadcast(64), [], [tprm], tprm)
            kb.dma("sp", mu[:], i["rwkv_mu"][l, :].partition_broadcast(64), [], [tprm], tprm)
            wup = sb("wup", [64, 1024], BF16)
            aup = sb("aup", [64, 1024], BF16)
            gup = sb("gup", [128, 1024], BF16)
            kb.dma("pool", wup[:], i["rwkv_w_up"][l], [], [tprm], tprm)
            kb.dma("pool", aup[:], i["rwkv_a_up"][l], [], [tprm], tprm)
            kb.dma("pool", gup[:], i["rwkv_g_up"][l], [], [tprm], tprm)
            mSU = sb("mSU", [64, 64], F32)
            mSL = sb("mSL", [64, 64], F32)
            kb.op("dve", [self.tr_const], [tprm], lambda e: e.tensor_tensor(
                out=mSU[:], in0=self.triuf[0:64, 0:64], in1=self.identf[0:64, 0:64], op=ALU.subtract))
            kb.op("dve", [self.tr_const], [tprm], lambda e: e.tensor_tensor(
                out=mSL[:], in0=self.onesf[0:64, 0:64], in1=self.triuf[0:64, 0:64], op=ALU.subtract))
            identb64 = self.identb
            names_f = ["p", "xs"]
            pt = sb("p", [64, RWW], F32); t_p = Tr()
            xs = sb("xs", [64, RWW], F32); t_xs = Tr()
            Fc, TFc, Bc, TBc = {}, {}, {}, {}
            for nm in ("logw", "a", "kk", "k2", "t1", "t2", "t3", "Up", "Hf", "ysb", "u1"):
                Fc[nm] = sb(nm, [64, 1024], F32)
                TFc[nm] = Tr()
            for nm in ("Rt", "Zb", "Ub", "Hb", "outb", "P0", "P1", "Q0", "Q1",
                       "Tt0", "Tt1", "LakT", "ArbT", "ArkT", "WT"):
                Bc[nm] = sb(nm, [64, 1024], BF16)
                TBc[nm] = Tr()
            SETS = []
            for k in range(2):
                Fk, TFk, Bk, TBk = dict(Fc), dict(TFc), dict(Bc), dict(TBc)
                Fk["g"] = sb("g%d" % k, [64, 1024], F32)
                TFk["g"] = Tr()
                for nm in ("At", "Bt", "Kt", "Vb", "BT", "KT"):
                    Bk[nm] = sb("%s%d" % (nm, k), [64, 1024], BF16)
                    TBk[nm] = Tr()
                SETS.append({"B": Bk, "TB": TBk, "F": Fk, "TF": TFk,
                             "ART": sb("ART%d" % k, [64, 16, 2, 64], BF16), "t_ART": Tr(),
                             "gC": sb("gC%d" % k, [64, 16], F32), "t_gC": Tr(),
                             "smb": sb("smb%d" % k, [64, 16], F32), "t_smb": Tr()})
            F, TF, B, TB = Fc, TFc, Bc, TBc
            lin = sb("lin", [64, 256], BF16); t_lin = Tr()
            linT = sb("linT", [128, 3, 64], BF16); t_linT = Tr()
            sm = sb("sm", [64, 96], F32); t_sm = Tr()
            stio = sb("stio", [64, 16, 64], F32); t_stio = Tr()
            big = [self.ps(es, "pb_big%d" % k, [64, 1024], F32) for k in range(3)]
            t_big = [PTr() for _ in range(3)]
            ptb = [self.ps(es, "pb_ptb%d" % k, [128, 16, 64], BF16) for k in range(2)]
            t_ptb = [PTr(), PTr()]
            rr = {"big": 0, "ptb": 0}

            def rot(name, n):
                k = rr[name] % n
                rr[name] += 1
                return k

            def heads_mm(C, out_fn, lhs_fn, rhs_fn, reads, tout):
                for h in range(16):
                    kb.op("pe", reads, [tout], lambda e: e.matmul(out_fn(h), lhs_fn(h), rhs_fn(h), start=True, stop=True))

            def hs(h):
                return slice(h * 64, (h + 1) * 64)

            def transpose_heads(C, src, t_src, dst_fn, t_dst):
                k = rot("ptb", 2)
                for h in range(16):
                    kb.op("pe", [t_src, self.tr_const], [t_ptb[k]], lambda e: e.transpose(
                        ptb[k][0:64, h, 0:C], src[0:C, hs(h)], self.identb[0:C, 0:C]))
                kb.op("act", [t_ptb[k]], [t_dst], lambda e: e.activation(
                    out=dst_fn(), in_=ptb[k][0:64, :, 0:C], func=AF.Copy))

            def stage1(C, r0, prev_kind, prev_ap, S):
                B, TB, F, TF, ART, t_ART, gC, t_gC, smb, t_smb = S["B"], S["TB"], S["F"], S["TF"], S["ART"], S["t_ART"], S["gC"], S["t_gC"], S["smb"], S["t_smb"]
                cs = slice(0, C)
                kb.dma("sp", pt[cs, :], s["prw"][r0:r0 + C, :], [], [t_p], t_p)
                yield
                if prev_kind == "zero":
                    kb.op("dve", [], [t_xs], lambda e: e.memset(xs[0:1, :], 0.0))
                    kb.dma("sp", xs[1:C, :], s["prw"][r0:r0 + C - 1, :], [], [t_xs], t_xs)
                elif prev_kind == "ap":
                    kb.dma("sp", xs[0:1, :], prev_ap, [], [t_xs], t_xs)
                    kb.dma("sp", xs[1:C, :], s["prw"][r0:r0 + C - 1, :], [], [t_xs], t_xs)
                else:
                    kb.dma("sp", xs[cs, :], s["prw"][r0 - 1:r0 + C - 1, :], [], [t_xs], t_xs)
                kb.op("pool", [t_xs, t_p], [t_xs], lambda e: e.tensor_tensor(out=xs[cs, :], in0=xs[cs, :], in1=pt[cs, :], op=ALU.subtract))
                yield
                kb.op("pool", [t_xs, tprm], [t_xs], lambda e: e.tensor_tensor(out=xs[cs, :], in0=xs[cs, :], in1=mu[cs, :], op=ALU.mult))
                yield
                kb.op("dve", [t_xs, t_p], [t_xs], lambda e: e.tensor_tensor(out=xs[cs, :], in0=xs[cs, :], in1=pt[cs, :], op=ALU.add))
                yield
                r_ = xs[cs, 0:1024]
                k_ = xs[cs, 1024:2048]
                v_ = xs[cs, 2048:3072]
                kb.op("act", [t_xs], [t_lin], lambda e: e.activation(out=lin[cs, 0:64], in_=xs[cs, 3072:3136], func=AF.Tanh))
                yield
                kb.op("act", [t_xs], [t_lin], lambda e: e.activation(out=lin[cs, 64:128], in_=xs[cs, 3136:3200], func=AF.Copy))
                yield
                kb.op("act", [t_xs], [t_lin], lambda e: e.activation(out=lin[cs, 128:256], in_=xs[cs, 3200:3328], func=AF.Sigmoid))
                yield
                k2 = rot("ptb", 2)
                kb.op("pe", [t_lin, self.tr_const], [t_ptb[k2]], lambda e: e.transpose(ptb[k2][0:64, 0, 0:C], lin[cs, 0:64], self.identb[0:C, 0:C]))
                yield
                kb.op("pe", [t_lin, self.tr_const], [t_ptb[k2]], lambda e: e.transpose(ptb[k2][0:64, 1, 0:C], lin[cs, 64:128], self.identb[0:C, 0:C]))
                yield
                kb.op("pe", [t_lin, self.tr_const], [t_ptb[k2]], lambda e: e.transpose(ptb[k2][0:128, 2, 0:C], lin[cs, 128:256], self.identb[0:C, 0:C]))
                yield
                kb.op("act", [t_ptb[k2]], [t_linT], lambda e: e.activation(out=linT[0:64, 0:2, 0:C], in_=ptb[k2][0:64, 0:2, 0:C], func=AF.Copy))
                yield
                kb.op("act", [t_ptb[k2]], [t_linT], lambda e: e.activation(out=linT[:, 2, 0:C], in_=ptb[k2][:, 2, 0:C], func=AF.Copy))
                yield
                bw = rot("big", 3)
                for hf in range(2):
                    kb.op("pe", [t_linT, tprm], [t_big[bw]], lambda e: e.matmul(
                        big[bw][cs, hf * 512:(hf + 1) * 512], linT[0:64, 0, 0:C], wup[:, hf * 512:(hf + 1) * 512], start=True, stop=True))
                kb.op("dve", [t_big[bw], tprm], [TF["logw"]], lambda e: e.tensor_tensor(
                    out=F["logw"][cs, :], in0=big[bw][cs, :], in1=prm["rwkv_w0"][cs, :], op=ALU.add))
                yield
                kb.op("act", [TF["logw"]], [TF["logw"]], lambda e: e.activation(out=F["logw"][cs, :], in_=F["logw"][cs, :], func=AF.Sigmoid))
                yield
                kb.op("pool", [TF["logw"]], [TF["logw"]], lambda e: e.tensor_scalar(
                    out=F["logw"][cs, :], in0=F["logw"][cs, :], scalar1=-0.6065306597126334, scalar2=None, op0=ALU.mult))
                yield
                ba = rot("big", 3)
                for hf in range(2):
                    kb.op("pe", [t_linT, tprm], [t_big[ba]], lambda e: e.matmul(
                        big[ba][cs, hf * 512:(hf + 1) * 512], linT[0:64, 1, 0:C], aup[:, hf * 512:(hf + 1) * 512], start=True, stop=True))
                kb.op("dve", [t_big[ba], tprm], [TF["a"]], lambda e: e.tensor_tensor(
                    out=F["a"][cs, :], in0=big[ba][cs, :], in1=prm["rwkv_a0"][cs, :], op=ALU.add))
                yield
                kb.op("act", [TF["a"]], [TF["a"]], lambda e: e.activation(out=F["a"][cs, :], in_=F["a"][cs, :], func=AF.Sigmoid))
                yield
                bg = rot("big", 3)
                for hf in range(2):
                    kb.op("pe", [t_linT, tprm], [t_big[bg]], lambda e: e.matmul(
                        big[bg][cs, hf * 512:(hf + 1) * 512], linT[0:128, 2, 0:C], gup[:, hf * 512:(hf + 1) * 512], start=True, stop=True))
                kb.op("act", [t_big[bg]], [TF["g"]], lambda e: e.activation(out=F["g"][cs, :], in_=big[bg][cs, :], func=AF.Copy))
                yield
                kb.op("pool", [t_xs, tprm], [TF["kk"]], lambda e: e.tensor_tensor(out=F["kk"][cs, :], in0=k_, in1=prm["rwkv_k_k"][cs, :], op=ALU.mult))
                yield
                kb.op("pool", [TF["kk"]], [TF["t1"]], lambda e: e.tensor_tensor(out=F["t1"][cs, :], in0=F["kk"][cs, :], in1=F["kk"][cs, :], op=ALU.mult))
                yield
                kb.op("dve", [TF["t1"]], [t_sm], lambda e: e.reduce_sum(out=sm[cs, 0:16], in_=h3(F["t1"][cs, :]), axis=AX.X))
                yield
                kb.op("dve", [t_sm], [t_sm], lambda e: e.tensor_scalar(out=sm[cs, 0:16], in0=sm[cs, 0:16], scalar1=1e-24, scalar2=None, op0=ALU.max))
                yield
                kb.op("act", [t_sm], [t_sm], lambda e: e.activation(out=sm[cs, 0:16], in_=sm[cs, 0:16], func=AF.Sqrt))
                yield
                kb.op("dve", [t_sm], [t_sm], lambda e: e.reciprocal(out=sm[cs, 0:16], in_=sm[cs, 0:16]))
                yield
                kb.op("dve", [TF["kk"], t_sm], [TF["kk"]], lambda e: e.tensor_tensor(
                    out=h3(F["kk"][cs, :]), in0=h3(F["kk"][cs, :]), in1=sm[cs, 0:16].unsqueeze(2).to_broadcast([C, 16, 64]), op=ALU.mult))
                yield
                kb.op("dve", [TF["a"], tprm], [TF["k2"]], lambda e: e.scalar_tensor_tensor(
                    out=F["k2"][cs, :], in0=F["a"][cs, :], scalar=-1.0, in1=prm["rwkv_k_a"][cs, :], op0=ALU.add, op1=ALU.mult))
                yield
                kb.op("dve", [TF["k2"], t_xs], [TF["k2"]], lambda e: e.scalar_tensor_tensor(
                    out=F["k2"][cs, :], in0=F["k2"][cs, :], scalar=1.0, in1=k_, op0=ALU.add, op1=ALU.mult))
                yield
                kb.op("pool", [t_xs, TF["k2"]], [TF["t1"]], lambda e: e.tensor_tensor(out=F["t1"][cs, :], in0=r_, in1=F["k2"][cs, :], op=ALU.mult))
                yield
                kb.op("pool", [TF["t1"], tprm], [TF["t1"]], lambda e: e.tensor_tensor(out=F["t1"][cs, :], in0=F["t1"][cs, :], in1=prm["rwkv_r_k"][cs, :], op=ALU.mult))
                yield
                kb.op("dve", [TF["t1"]], [t_smb], lambda e: e.reduce_sum(out=smb[cs, 0:16], in_=h3(F["t1"][cs, :]), axis=AX.X))
                yield
                bl = rot("big", 3)
                for hf in range(2):
                    kb.op("pe", [TF["logw"], self.tr_const], [t_big[bl]], lambda e: e.matmul(
                        big[bl][cs, hf * 512:(hf + 1) * 512], self.triuf[0:C, 0:C], F["logw"][cs, hf * 512:(hf + 1) * 512], start=True, stop=True))
                kb.op("act", [t_big[bl]], [TF["t1"]], lambda e: e.activation(out=F["t1"][cs, :], in_=big[bl][cs, :], func=AF.Exp))
                yield
                kb.op("act", [t_big[bl]], [TF["t2"]], lambda e: e.activation(out=F["t2"][cs, :], in_=big[bl][cs, :], func=AF.Exp, scale=-1.0))
                yield
                kb.op("dve", [t_big[bl], TF["logw"]], [TF["t3"]], lambda e: e.tensor_tensor(out=F["t3"][cs, :], in0=big[bl][cs, :], in1=F["logw"][cs, :], op=ALU.subtract))
                yield
                kb.op("act", [TF["t3"]], [TF["t3"]], lambda e: e.activation(out=F["t3"][cs, :], in_=F["t3"][cs, :], func=AF.Exp))
                yield
                bq = rot("big", 3)
                for h in range(16):
                    kb.op("pe", [TF["logw"], self.tr_const], [t_big[bq]], lambda e: e.matmul(
                        big[bq][0:64, h:h + 1], F["logw"][cs, hs(h)], self.onesf[0:C, 0:1], start=True, stop=True))
                kb.op("act", [t_big[bq]], [t_gC], lambda e: e.activation(out=gC[:], in_=big[bq][0:64, 0:16], func=AF.Exp))
                yield
                kb.op("dve", [TF["kk"], TF["t3"]], [TB["At"]], lambda e: e.scalar_tensor_tensor(
                    out=B["At"][cs, :], in0=F["kk"][cs, :], scalar=-1.0, in1=F["t3"][cs, :], op0=ALU.mult, op1=ALU.mult))
                yield
                kb.op("pool", [TF["kk"], TF["a"]], [TF["t3"]], lambda e: e.tensor_tensor(out=F["t3"][cs, :], in0=F["kk"][cs, :], in1=F["a"][cs, :], op=ALU.mult))
                yield
                kb.op("dve", [TF["t3"], TF["t2"]], [TB["Bt"]], lambda e: e.tensor_tensor(out=B["Bt"][cs, :], in0=F["t3"][cs, :], in1=F["t2"][cs, :], op=ALU.mult))
                yield
                kb.op("dve", [TF["k2"], TF["t2"]], [TB["Kt"]], lambda e: e.tensor_tensor(out=B["Kt"][cs, :], in0=F["k2"][cs, :], in1=F["t2"][cs, :], op=ALU.mult))
                yield
                kb.op("dve", [t_xs, TF["t1"]], [TB["Rt"]], lambda e: e.tensor_tensor(out=B["Rt"][cs, :], in0=r_, in1=F["t1"][cs, :], op=ALU.mult))
                yield
                kb.op("pool", [t_xs], [TB["Vb"]], lambda e: e.tensor_copy(out=B["Vb"][cs, :], in_=v_))
                yield
                transpose_heads(C, B["At"], TB["At"], lambda: ART[:, :, 0, 0:C], t_ART)
                yield
                transpose_heads(C, B["Rt"], TB["Rt"], lambda: ART[:, :, 1, 0:C], t_ART)
                yield
                transpose_heads(C, B["Bt"], TB["Bt"], lambda: h3(B["BT"][:, :])[:, :, 0:C], TB["BT"])
                yield
                transpose_heads(C, B["Kt"], TB["Kt"], lambda: h3(B["KT"][:, :])[:, :, 0:C], TB["KT"])
                yield
            def stage2(C, r0, S):
                B, TB, F, TF, ART, t_ART, gC, t_gC, smb, t_smb = S["B"], S["TB"], S["F"], S["TF"], S["ART"], S["t_ART"], S["gC"], S["t_gC"], S["smb"], S["t_smb"]
                nrounds = 5 if C == 64 else 4
                cs = slice(0, C)
                BT3 = h3(B["BT"][:, :])
                KT3 = h3(B["KT"][:, :])
                b1 = rot("big", 3)
                heads_mm(C, lambda h: h3(big[b1][:, :])[cs, h, 0:C], lambda h: ART[:, h, 0, 0:C], lambda h: BT3[:, h, 0:C],
                         [t_ART, TB["BT"]], t_big[b1])
                yield
                kb.op("dve", [t_big[b1], tprm], [TB["Q0"]], lambda e: e.tensor_tensor(
                    out=h3(B["Q0"][:, :])[cs, :, 0:C], in0=h3(big[b1][:, :])[cs, :, 0:C],
                    in1=mSL[cs, 0:C].unsqueeze(1).to_broadcast([C, 16, C]), op=ALU.mult))
                yield
                for (lhsT3, tl, dstA, dstB) in ((BT3, TB["BT"], "P0", "ArbT"), (KT3, TB["KT"], "LakT", "ArkT")):
                    for half in range(2):
                        bb = rot("big", 3)
                        v4 = big[bb][:, :].rearrange("p (h two n) -> p h two n", h=8, two=2)
                        for hh in range(8):
                            h = half * 8 + hh
                            if C == 64:
                                kb.op("pe", [tl, t_ART], [t_big[bb]], lambda e: e.matmul(
                                    v4[cs, hh, :, 0:C], lhsT3[:, h, 0:C], ART[:, h, :, 0:C], start=True, stop=True))
                            else:
                                for two in range(2):
                                    kb.op("pe", [tl, t_ART], [t_big[bb]], lambda e: e.matmul(
                                        v4[cs, hh, two, 0:C], lhsT3[:, h, 0:C], ART[:, h, two, 0:C], start=True, stop=True))
                        dA = h3(B[dstA][:, :])[cs, half * 8:(half + 1) * 8, 0:C]
                        dB = h3(B[dstB][:, :])[cs, half * 8:(half + 1) * 8, 0:C]
                        kb.op("dve", [t_big[bb], tprm], [TB[dstA]], lambda e: e.tensor_tensor(
                            out=dA, in0=v4[cs, :, 0, 0:C], in1=mSU[cs, 0:C].unsqueeze(1).to_broadcast([C, 8, C]), op=ALU.mult))
                        kb.op("dve", [t_big[bb], self.tr_const], [TB[dstB]], lambda e: e.tensor_tensor(
                            out=dB, in0=v4[cs, :, 1, 0:C], in1=self.triuf[0:C, 0:C].unsqueeze(1).to_broadcast([C, 8, C]), op=ALU.mult))
                Pn, Qn, Tn = "P0", "Q0", "Tt0"
                kb.op("dve", [TB["P0"], self.tr_const], [TB["Tt0"]], lambda e: e.tensor_tensor(
                    out=h3(B["Tt0"][:, :])[cs, :, 0:C], in0=h3(B["P0"][:, :])[cs, :, 0:C],
                    in1=self.identf[0:C, 0:C].unsqueeze(1).to_broadcast([C, 16, C]), op=ALU.add))
                yield
                for rd in range(nrounds):
                    Pn2 = "P1" if Pn == "P0" else "P0"
                    Qn2 = "Q1" if Qn == "Q0" else "Q0"
                    Tn2 = "Tt1" if Tn == "Tt0" else "Tt0"
                    P3, Q3, T3 = h3(B[Pn][:, :]), h3(B[Qn][:, :]), h3(B[Tn][:, :])
                    bqq = rot("big", 3)
                    heads_mm(C, lambda h: h3(big[bqq][:, :])[cs, h, 0:C], lambda h: P3[cs, h, 0:C], lambda h: Q3[cs, h, 0:C],
                             [TB[Pn], TB[Qn]], t_big[bqq])
                    kb.op("dve", [t_big[bqq]], [TB[Qn2]], lambda e: e.tensor_copy(
                        out=h3(B[Qn2][:, :])[cs, :, 0:C], in_=h3(big[bqq][:, :])[cs, :, 0:C]))
                    yield
                    if rd < nrounds - 1:
                        bp = rot("big", 3)
                        heads_mm(C, lambda h: h3(big[bp][:, :])[cs, h, 0:C], lambda h: Q3[cs, h, 0:C], lambda h: P3[cs, h, 0:C],
                                 [TB[Pn], TB[Qn]], t_big[bp])
                        kb.op("act", [t_big[bp]], [TB[Pn2]], lambda e: e.activation(
                            out=h3(B[Pn2][:, :])[cs, :, 0:C], in_=h3(big[bp][:, :])[cs, :, 0:C], func=AF.Copy))
                        yield
                    Q3n = h3(B[Qn2][:, :])
                    bt = rot("big", 3)
                    heads_mm(C, lambda h: h3(big[bt][:, :])[cs, h, 0:C], lambda h: Q3n[cs, h, 0:C], lambda h: T3[cs, h, 0:C],
                             [TB[Qn2], TB[Tn]], t_big[bt])
                    kb.op("dve", [t_big[bt], TB[Tn]], [TB[Tn2]], lambda e: e.tensor_tensor(
                        out=h3(B[Tn2][:, :])[cs, :, 0:C], in0=h3(big[bt][:, :])[cs, :, 0:C], in1=T3[cs, :, 0:C], op=ALU.add))
                    Pn, Qn, Tn = Pn2, Qn2, Tn2
                T3 = h3(B[Tn][:, :])
                LakT3, ArbT3, ArkT3 = h3(B["LakT"][:, :]), h3(B["ArbT"][:, :]), h3(B["ArkT"][:, :])
                bz = rot("big", 3)
                heads_mm(C, lambda h: big[bz][cs, hs(h)], lambda h: LakT3[cs, h, 0:C], lambda h: B["Vb"][cs, hs(h)],
                         [TB["LakT"], TB["Vb"]], t_big[bz])
                yield
                kb.op("act", [t_big[bz]], [TB["Zb"]], lambda e: e.activation(out=B["Zb"][cs, :], in_=big[bz][cs, :], func=AF.Copy))
                yield
                bu = rot("big", 3)
                heads_mm(C, lambda h: big[bu][cs, hs(h)], lambda h: T3[cs, h, 0:C], lambda h: B["Zb"][cs, hs(h)],
                         [TB[Tn], TB["Zb"]], t_big[bu])
                yield
                kb.op("act", [t_big[bu]], [TF["Up"]], lambda e: e.activation(out=F["Up"][cs, :], in_=big[bu][cs, :], func=AF.Copy))
                yield
                bwt = rot("big", 3)
                heads_mm(C, lambda h: h3(big[bwt][:, :])[0:64, h, 0:C], lambda h: B["At"][cs, hs(h)], lambda h: T3[cs, h, 0:C],
                         [TB["At"], TB[Tn]], t_big[bwt])
                yield
                WT3 = h3(B["WT"][:, :])
                kb.op("act", [t_big[bwt]], [TB["WT"]], lambda e: e.activation(
                    out=WT3[:, :, 0:C], in_=h3(big[bwt][:, :])[0:64, :, 0:C], func=AF.Copy))
                yield
                Hb3 = h3(B["Hb"][:, :])
                b_u = rot("big", 3)
                heads_mm(C, lambda h: big[b_u][cs, hs(h)], lambda h: WT3[:, h, 0:C], lambda h: Hb3[:, h, :],
                         [TB["WT"], TB["Hb"]], t_big[b_u])
                yield
                kb.op("dve", [t_big[b_u], TF["Up"]], [TB["Ub"]], lambda e: e.tensor_tensor(
                    out=B["Ub"][cs, :], in0=big[b_u][cs, :], in1=F["Up"][cs, :], op=ALU.add))
                yield
                b_y = rot("big", 3)
                for h in range(16):
                    kb.op("pe", [t_ART, TB["Hb"]], [t_big[b_y]], lambda e: e.matmul(
                        big[b_y][cs, hs(h)], ART[:, h, 1, 0:C], Hb3[:, h, :], start=True, stop=False))
                    kb.op("pe", [TB["ArbT"], TB["Ub"]], [t_big[b_y]], lambda e: e.matmul(
                        big[b_y][cs, hs(h)], ArbT3[cs, h, 0:C], B["Ub"][cs, hs(h)], start=False, stop=False))
                    kb.op("pe", [TB["ArkT"], TB["Vb"]], [t_big[b_y]], lambda e: e.matmul(
                        big[b_y][cs, hs(h)], ArkT3[cs, h, 0:C], B["Vb"][cs, hs(h)], start=False, stop=True))
                b_h = rot("big", 3)
                for h in range(16):
                    kb.op("pe", [TB["Bt"], TB["Ub"]], [t_big[b_h]], lambda e: e.matmul(
                        big[b_h][0:64, hs(h)], B["Bt"][cs, hs(h)], B["Ub"][cs, hs(h)], start=True, stop=False))
                    kb.op("pe", [TB["Kt"], TB["Vb"]], [t_big[b_h]], lambda e: e.matmul(
                        big[b_h][0:64, hs(h)], B["Kt"][cs, hs(h)], B["Vb"][cs, hs(h)], start=False, stop=True))
                kb.op("dve", [t_big[b_h], TF["Hf"]], [TF["Hf"]], lambda e: e.tensor_tensor(
                    out=F["Hf"][:, :], in0=big[b_h][0:64, :], in1=F["Hf"][:, :], op=ALU.add))
                yield
                kb.op("dve", [TF["Hf"], t_gC], [TF["Hf"]], lambda e: e.tensor_tensor(
                    out=h3(F["Hf"][:, :]), in0=h3(F["Hf"][:, :]), in1=gC[:, :].unsqueeze(2).to_broadcast([64, 16, 64]), op=ALU.mult))
                yield
                kb.op("pool", [TF["Hf"]], [TB["Hb"]], lambda e: e.tensor_copy(out=B["Hb"][:, :], in_=F["Hf"][:, :]))
                yield
                Y = F["ysb"]
                tY = TF["ysb"]
                kb.op("act", [t_big[b_y]], [tY], lambda e: e.activation(out=Y[cs, :], in_=big[b_y][cs, :], func=AF.Copy))
                yield
                kb.op("dve", [tY], [t_sm], lambda e: e.reduce_sum(out=sm[cs, 32:48], in_=h3(Y[cs, :]), axis=AX.X))
                yield
                kb.op("dve", [t_sm], [t_sm], lambda e: e.tensor_scalar(out=sm[cs, 32:48], in0=sm[cs, 32:48], scalar1=-1.0 / 64, scalar2=None, op0=ALU.mult))
                yield
                kb.op("dve", [tY, t_sm], [tY], lambda e: e.tensor_tensor(
                    out=h3(Y[cs, :]), in0=h3(Y[cs, :]), in1=sm[cs, 32:48].unsqueeze(2).to_broadcast([C, 16, 64]), op=ALU.add))
                yield
                kb.op("pool", [tY], [TF["u1"]], lambda e: e.tensor_tensor(out=F["u1"][cs, :], in0=Y[cs, :], in1=Y[cs, :], op=ALU.mult))
                yield
                kb.op("dve", [TF["u1"]], [t_sm], lambda e: e.reduce_sum(out=sm[cs, 48:64], in_=h3(F["u1"][cs, :]), axis=AX.X))
                yield
                kb.op("dve", [t_sm], [t_sm], lambda e: e.tensor_scalar(out=sm[cs, 48:64], in0=sm[cs, 48:64], scalar1=1.0 / 64, scalar2=64e-5, op0=ALU.mult, op1=ALU.add))
                yield
                kb.op("act", [t_sm], [t_sm], lambda e: e.activation(out=sm[cs, 48:64], in_=sm[cs, 48:64], func=AF.Sqrt))
                yield
                kb.op("dve", [t_sm], [t_sm], lambda e: e.reciprocal(out=sm[cs, 48:64], in_=sm[cs, 48:64]))
                yield
                kb.op("dve", [tY, t_sm], [tY], lambda e: e.tensor_tensor(
                    out=h3(Y[cs, :]), in0=h3(Y[cs, :]), in1=sm[cs, 48:64].unsqueeze(2).to_broadcast([C, 16, 64]), op=ALU.mult))
                yield
                kb.op("pool", [tY, tprm], [tY], lambda e: e.tensor_tensor(out=Y[cs, :], in0=Y[cs, :], in1=prm["rwkv_gn_g"][cs, :], op=ALU.mult))
                yield
                kb.op("pool", [tY, tprm], [tY], lambda e: e.tensor_tensor(out=Y[cs, :], in0=Y[cs, :], in1=prm["rwkv_gn_b"][cs, :], op=ALU.add))
                yield
                kb.op("dve", [TB["Vb"], t_smb], [TF["u1"]], lambda e: e.tensor_tensor(
                    out=h3(F["u1"][cs, :]), in0=h3(B["Vb"][cs, :]), in1=smb[cs, 0:16].unsqueeze(2).to_broadcast([C, 16, 64]), op=ALU.mult))
                yield
                kb.op("pool", [tY, TF["u1"]], [tY], lambda e: e.tensor_tensor(out=Y[cs, :], in0=Y[cs, :], in1=F["u1"][cs, :], op=ALU.add))
                yield
                kb.op("dve", [tY, TF["g"]], [TB["outb"]], lambda e: e.tensor_tensor(out=B["outb"][cs, :], in0=Y[cs, :], in1=F["g"][cs, :], op=ALU.mult))
                yield
                kb.dma("sp", s["ob"][r0:r0 + C, :], B["outb"][cs, :], [TB["outb"]], [], TB["outb"])
                yield

            def store_state(dst_ap):
                bs = rot("big", 3)
                for h in range(16):
                    kb.op("pe", [TF["Hf"], self.tr_const], [t_big[bs]], lambda e: e.transpose(
                        big[bs][0:64, hs(h)], F["Hf"][:, hs(h)], self.identf[0:64, 0:64]))
                kb.op("dve", [t_big[bs]], [t_stio], lambda e: e.tensor_copy(out=stio[:], in_=h3(big[bs][:, :])))
                kb.dma("sp", dst_ap.rearrange("h i j -> i h j"), stio[:], [t_stio], [], t_stio)

            def load_state(src_ap):
                kb.dma("sp", stio[:], src_ap.rearrange("h i j -> i h j"), [], [t_stio], t_stio)
                bs = rot("big", 3)
                for h in range(16):
                    kb.op("pe", [t_stio, self.tr_const], [t_big[bs]], lambda e: e.transpose(
                        big[bs][0:64, hs(h)], stio[:, h, :], self.identf[0:64, 0:64]))
                kb.op("dve", [t_big[bs]], [TF["Hf"]], lambda e: e.tensor_copy(out=F["Hf"][:, :], in_=big[bs][:, :]))
                kb.op("pool", [TF["Hf"]], [TB["Hb"]], lambda e: e.tensor_copy(out=B["Hb"][:, :], in_=F["Hf"][:, :]))

            def drain(g):
                if g is not None:
                    for _ in g:
                        pass

            def interleave(g1, g2):
                a_done = g1 is None
                b_done = g2 is None
                while not (a_done and b_done):
                    if not a_done:
                        try:
                            next(g1)
                        except StopIteration:
                            a_done = True
                    if not b_done:
                        try:
                            next(g2)
                        except StopIteration:
                            b_done = True

            kb.op("dve", [], [TF["Hf"]], lambda e: e.memset(F["Hf"][:, :], 0.0))
            kb.op("dve", [], [TB["Hb"]], lambda e: e.memset(B["Hb"][:, :], 0.0))
            prev2 = None
            nch = TP // 64
            for c in range(nch):
                S = SETS[c % 2]
                g1 = stage1(64, c * 64, "zero" if c == 0 else "rows", None, S)
                if PIPE_RWKV:
                    interleave(g1, prev2)
                else:
                    drain(prev2)
                    drain(g1)
                prev2 = stage2(64, c * 64, S)
            drain(prev2)
            store_state(o["stp"][l])
            for b in range(4):
                S = SETS[b % 2]
                load_state(i["st"][l, b])
                drain(stage1(32, TP + 32 * b, "ap", i["sh"][l, b:b + 1, :], S))
                drain(stage2(32, TP + 32 * b, S))
                store_state(o["sts"][l, b])
            kb.barrier()

    def dbg_fill(self, l):
        kb, i, s = self.kb, self.i, self.s
        with ExitStack() as es:
            a = self.sb(es, "dbg_a", [128, 1024], F32)
            b = self.sb(es, "dbg_b", [128, 1024], BF16)
            ta, tb = Tr(), Tr()
            for (src, dst) in ((i["dbg_oa"], s["oa"]), (i["dbg_ob"], s["ob"])):
                for t in range(self.NT + 1):
                    kb.dma("sp", a[:], src[t * 128:(t + 1) * 128, :], [], [ta], ta)
                    kb.op("dve", [ta], [tb], lambda e: e.tensor_copy(out=b[:], in_=a[:]))
                    kb.dma("sp", dst[t * 128:(t + 1) * 128, :], b[:], [tb], [], tb)
            kb.barrier()

    def phase3(self, l):
        nc, kb, i, o, s = self.nc, self.kb, self.i, self.o, self.s
        TP, NT = self.TP, self.NT
        last = (l == self.NL - 1)
        xin_p = i["xp"] if l == 0 else s["x"]
        with ExitStack() as es:
            sb = lambda n, shp, dt: self.sb(es, "p3_" + n, shp, dt)
            g_bc = sb("g_bc", [128, D], F32)
            t_g = Tr()
            xt = [sb("xt%d" % k, [128, D], F32) for k in range(4)]
            t_xt = [Tr() for _ in range(4)]
            yb = [sb("yb%d" % k, [128, D], F32) for k in range(4)]
            t_yb = [Tr() for _ in range(4)]
            big = [sb("big%d" % k, [128, 44, 128], BF16) for k in range(4)]
            t_big = [Tr() for _ in range(4)]
            mT = [sb("mT%d" % k, [128, KC, 128], BF16) for k in range(4)]
            t_mT = [Tr() for _ in range(4)]
            wb = [sb("wb%d" % k, [128, KC, 512], BF16) for k in range(2)]
            t_wb = [Tr(), Tr()]
            ld = [sb("ld%d" % k, [128, 1024], BF16) for k in range(2)]
            t_ld = [Tr(), Tr()]
            gt = [sb("gt%d" % k, [128, 512], BF16) for k in range(4)]
            t_gt = [Tr() for _ in range(4)]
            tmpf = [sb("tmpf%d" % k, [128, 512], F32) for k in range(2)]
            t_tmpf = [Tr(), Tr()]
            macc = [sb("macc%d" % k, [128, 512], F32) for k in range(4)]
            t_macc = [Tr() for _ in range(4)]
            tmpb = [sb("tmpb%d" % k, [128, 512], BF16) for k in range(2)]
            t_tmpb = [Tr(), Tr()]
            sil = [sb("sil%d" % k, [128, 512], F32) for k in range(4)]
            t_sil = [Tr() for _ in range(4)]
            hb = sb("hb", [128, D], BF16)
            t_hb = Tr()
            junk = sb("junk", [128, 512], BF16)
            t_junk = Tr()
            small = [sb("small%d" % k, [128, 16], F32) for k in range(4)]
            t_small = [Tr() for _ in range(4)]
            acc = [self.ps(es, "p3_acc%d" % k, [128, 512], F32) for k in range(4)]
            t_acc = [PTr() for _ in range(4)]
            ptmp = [self.ps(es, "p3_ptmp%d" % k, [128, 8, 128], BF16) for k in range(2)]
            t_ptmp = [PTr(), PTr()]
            rr = {"wb": 0, "ld": 0, "gt": 0, "tmpf": 0, "tmpb": 0, "pt": 0}

            def rot(name, n):
                k = rr[name] % n
                rr[name] += 1
                return k

            def transpose_into(src_ap_fn, t_src, nblk, dst_ap_fn, t_dst):
                b = 0
                while b < nblk:
                    n = min(8, nblk - b)
                    p = rot("pt", 2)
                    for q in range(n):
                        kb.op("pe", [t_src, self.tr_const], [t_ptmp[p]], lambda e: e.transpose(
                            ptmp[p][:, q, :], src_ap_fn(b + q), self.identb[:]))
                    kb.op("act", [t_ptmp[p]], [t_dst], lambda e: e.activation(
                        out=dst_ap_fn(b, n), in_=ptmp[p][:, 0:n, :], func=AF.Copy))
                    b += n

            def linear(grp, lhs_fn, t_lhs, W2d, K, ncb, evac, kpart=KC):
                nkc = K // 128
                Wv = W2d.rearrange("(kc p) n -> p kc n", p=128)
                for cb in range(ncb):
                    k0 = 0
                    while k0 < nkc:
                        nk = min(kpart, nkc - k0)
                        ws = rot("wb", 2)
                        kb.dma("pool", wb[ws][:, 0:nk, :], Wv[:, k0:k0 + nk, cb * 512:(cb + 1) * 512],
                               [], [t_wb[ws]], t_wb[ws])
                        for sl in range(len(grp)):
                            for kc in range(nk):
                                kb.op("pe", [t_lhs[sl], t_wb[ws]], [t_acc[sl]], lambda e: e.matmul(
                                    acc[sl][:], lhs_fn(sl, k0 + kc), wb[ws][:, kc, :],
                                    start=(k0 + kc == 0), stop=(k0 + kc == nkc - 1)))
                            if k0 + nk == nkc:
                                evac(sl, cb)
                        k0 += nk

            def rms_apply(sl, gname, dst_is_x):
                sm = small[sl]
                kb.op("dve", [t_small[sl]], [t_small[sl]], lambda e: e.reduce_sum(
                    out=sm[:, 4:5], in_=sm[:, 0:4], axis=AX.X))
                self.rmsnorm_rstd(kb, sm[:, 4:5], sm[:, 5:6], t_small[sl], t_small[sl], D, 1e-6)
                kb.op("dve", [t_yb[sl], t_small[sl], t_g], [t_yb[sl]], lambda e: e.scalar_tensor_tensor(
                    out=yb[sl][:], in0=yb[sl][:], scalar=sm[:, 5:6], in1=g_bc[:], op0=ALU.mult, op1=ALU.mult))
                kb.op("dve", [t_yb[sl], t_xt[sl]], [t_xt[sl]], lambda e: e.tensor_tensor(
                    out=xt[sl][:], in0=xt[sl][:], in1=yb[sl][:], op=ALU.add))

            br_w = [i["w_br_fox"][l], i["w_br_rwkv"][l], i["w_br_gmlp"][l]]
            br_o = [s["oa"], s["ob"], s["oc"]]

            for grp in self.groups():
                is_s = grp[0] == NT
                ng = len(grp)
                rows = [(TP if is_s else t * 128) for t in grp]
                for sl, t in enumerate(grp):
                    r0 = rows[sl]
                    if l == 0:
                        src = i["xs"][:, :] if is_s else i["xp"][r0:r0 + 128, :]
                    else:
                        src = s["x"][r0:r0 + 128, :]
                    kb.dma("sp", xt[sl][:], src, [], [t_xt[sl]], t_xt[sl])
                    for b in range(3):
                        k = rot("ld", 2)
                        kb.dma("sp", ld[k][:], br_o[b][r0:r0 + 128, :], [], [t_ld[k]], t_ld[k])
                        transpose_into(lambda q: ld[k][:, q * 128:(q + 1) * 128], t_ld[k], 8,
                                       lambda b0, n: big[sl][:, b * 8 + b0:b * 8 + b0 + n, :], t_big[sl])
                for cb in range(4):
                    for b in range(3):
                        def evacA(sl, _cb, b=b, cb=cb):
                            r0 = rows[sl]
                            gk = rot("gt", 4)
                            kb.dma("sp", gt[gk][:], s["gate"][r0:r0 + 128, b * 2048 + cb * 512:b * 2048 + (cb + 1) * 512],
                                   [], [t_gt[gk]], t_gt[gk])
                            if b == 0:
                                kb.op("dve", [t_acc[sl], t_gt[gk]], [t_macc[sl]], lambda e: e.tensor_tensor(
                                    out=macc[sl][:], in0=acc[sl][:], in1=gt[gk][:], op=ALU.mult))
                            else:
                                tk = rot("tmpf", 2)
                                kb.op("dve", [t_acc[sl], t_gt[gk]], [t_tmpf[tk]], lambda e: e.tensor_tensor(
                                    out=tmpf[tk][:], in0=acc[sl][:], in1=gt[gk][:], op=ALU.mult))
                                if b == 1:
                                    kb.op("dve", [t_tmpf[tk], t_macc[sl]], [t_macc[sl]], lambda e: e.tensor_tensor(
                                        out=macc[sl][:], in0=macc[sl][:], in1=tmpf[tk][:], op=ALU.add))
                                else:
                                    bk = rot("tmpb", 2)
                                    kb.op("dve", [t_tmpf[tk], t_macc[sl]], [t_tmpb[bk]], lambda e: e.tensor_tensor(
                                        out=tmpb[bk][:], in0=macc[sl][:], in1=tmpf[tk][:], op=ALU.add))
                                    transpose_into(lambda q: tmpb[bk][:, q * 128:(q + 1) * 128], t_tmpb[bk], 4,
                                                   lambda b0, n: mT[sl][:, cb * 4 + b0:cb * 4 + b0 + n, :], t_mT[sl])
                        Wv = br_w[b].rearrange("(kc p) n -> p kc n", p=128)
                        ws = rot("wb", 2)
                        kb.dma("pool", wb[ws][:, 0:8, :], Wv[:, :, cb * 512:(cb + 1) * 512], [], [t_wb[ws]], t_wb[ws])
                        for sl in range(ng):
                            for kc in range(8):
                                kb.op("pe", [t_big[sl], t_wb[ws]], [t_acc[sl]], lambda e: e.matmul(
                                    acc[sl][:], big[sl][:, b * 8 + kc, :], wb[ws][:, kc, :],
                                    start=(kc == 0), stop=(kc == 7)))
                            evacA(sl, cb)
                kb.dma("sp", g_bc[:], i["post_mix_g"][l, :].partition_broadcast(128), [], [t_g], t_g)

                def evacB(sl, cb):
                    kb.op("act", [t_acc[sl]], [t_junk, t_small[sl]], lambda e: e.activation(
                        out=junk[:], in_=acc[sl][:], func=AF.Square, accum_out=small[sl][:, cb:cb + 1]))
                    kb.op("dve", [t_acc[sl]], [t_yb[sl]], lambda e: e.tensor_copy(
                        out=yb[sl][:, cb * 512:(cb + 1) * 512], in_=acc[sl][:]))
                linear(grp, lambda sl, kc: mT[sl][:, kc, :], t_mT, i["w_out"][l], D, 4, evacB)
                for sl in range(ng):
                    rms_apply(sl, "post_mix_g", False)
                kb.dma("sp", g_bc[:], i["pre_ffn_g"][l, :].partition_broadcast(128), [], [t_g], t_g)
                for sl in range(ng):
                    sm = small[sl]
                    kb.op("act", [t_xt[sl]], [t_hb, t_small[sl]], lambda e: e.activation(
                        out=hb[:], in_=xt[sl][:], func=AF.Square, accum_out=sm[:, 6:7]))
                    self.rmsnorm_rstd(kb, sm[:, 6:7], sm[:, 7:8], t_small[sl], t_small[sl], D, 1e-6)
                    kb.op("dve", [t_xt[sl], t_small[sl], t_g], [t_hb], lambda e: e.scalar_tensor_tensor(
                        out=hb[:], in0=xt[sl][:], scalar=sm[:, 7:8], in1=g_bc[:], op0=ALU.mult, op1=ALU.mult))
                    transpose_into(lambda q: hb[:, q * 128:(q + 1) * 128], t_hb, 16,
                                   lambda b0, n: mT[sl][:, b0:b0 + n, :], t_mT[sl])
                for cb in range(DFF // 512):
                    def evac1(sl, _cb):
                        kb.op("act", [t_acc[sl]], [t_sil[sl]], lambda e: e.activation(
                            out=sil[sl][:], in_=acc[sl][:], func=AF.Silu))

                    def evac3(sl, _cb, cb=cb):
                        bk = rot("tmpb", 2)
                        kb.op("dve", [t_acc[sl], t_sil[sl]], [t_tmpb[bk]], lambda e: e.tensor_tensor(
                            out=tmpb[bk][:], in0=acc[sl][:], in1=sil[sl][:], op=ALU.mult))
                        transpose_into(lambda q: tmpb[bk][:, q * 128:(q + 1) * 128], t_tmpb[bk], 4,
                                       lambda b0, n: big[sl][:, cb * 4 + b0:cb * 4 + b0 + n, :], t_big[sl])
                    for (Wd, ev) in ((i["ffn_w1"][l], evac1), (i["ffn_w3"][l], evac3)):
                        Wv = Wd.rearrange("(kc p) n -> p kc n", p=128)
                        ws = rot("wb", 2)
                        kb.dma("pool", wb[ws][:], Wv[:, :, cb * 512:(cb + 1) * 512], [], [t_wb[ws]], t_wb[ws])
                        for sl in range(ng):
                            for kc in range(KC):
                                kb.op("pe", [t_mT[sl], t_wb[ws]], [t_acc[sl]], lambda e: e.matmul(
                                    acc[sl][:], mT[sl][:, kc, :], wb[ws][:, kc, :],
                                    start=(kc == 0), stop=(kc == KC - 1)))
                            ev(sl, cb)
                kb.dma("sp", g_bc[:], i["post_ffn_g"][l, :].partition_broadcast(128), [], [t_g], t_g)
                linear(grp, lambda sl, kc: big[sl][:, kc, :], t_big, i["ffn_w2"][l], DFF, 4, evacB, kpart=11)
                for sl in range(ng):
                    rms_apply(sl, "post_ffn_g", True)
                    r0 = rows[sl]
                    if last:
                        dst = o["ys"][:, :] if is_s else o["yp"][r0:r0 + 128, :]
                    else:
                        dst = s["x"][r0:r0 + 128, :]
                    kb.dma("sp", dst, xt[sl][:], [t_xt[sl]], [], t_xt[sl])
            kb.barrier()

_CACHE = {}
PROG_INPUTS = {}


def _consts():
    ident = np.eye(128, dtype=np.float32)
    triu = np.triu(np.ones((128, 128), np.float32))
    triu32 = np.zeros((128, 128), np.float32)
    for b in range(4):
        triu32[b * 32:(b + 1) * 32, b * 32:(b + 1) * 32] = np.triu(np.ones((32, 32), np.float32))
    sel = np.zeros((16, 8, 128), np.float32)
    for h in range(8):
        sel[h, h, :] = 1.0
        sel[8 + h, h, :] = 1.0
    return {"c_ident": ident, "c_triu": triu, "c_triu32": triu32, "c_sel": sel.reshape(16, 1024)}


WEIGHTS = ["pre_mix_g", "w_in", "fox_bf", "rwkv_mu", "rwkv_w0", "rwkv_w_up", "rwkv_a0", "rwkv_a_up", "rwkv_g_up",
           "rwkv_k_k", "rwkv_k_a", "rwkv_r_k", "rwkv_gn_g", "rwkv_gn_b", "gmlp_ln_g", "gmlp_ln_b", "gmlp_w_s",
           "gmlp_b_s", "w_br_fox", "w_br_rwkv", "w_br_gmlp", "w_out", "post_mix_g", "pre_ffn_g", "ffn_w1",
           "ffn_w3", "ffn_w2", "post_ffn_g"]


def run(inputs, TP, NL, stop_after=None, trace=False):
    key = (TP, NL, stop_after)
    if key not in _CACHE:
        pg = Prog(TP, NL, stop_after)
        _CACHE[key] = pg.build()
        PROG_INPUTS[key] = set(pg.i.keys())
    nc = _CACHE[key]
    f = lambda a: np.ascontiguousarray(np.asarray(a, dtype=np.float32))
    shared = {nm: f(inputs[nm])[:NL] for nm in WEIGHTS}
    shared["rwkv_r_k"] = shared["rwkv_r_k"].reshape(NL, 1024)
    shared.update(_consts())
    if DBG_OAB:
        shared["dbg_oa"] = f(inputs["dbg_oa"])
        shared["dbg_ob"] = f(inputs["dbg_ob"])
    xp = f(inputs["x_prompt"])
    xs = f(inputs["x_sample"])
    nb = xp.shape[0]
    nsb = xs.shape[0] // 4
    in_maps = []
    for c in range(NCORE):
        m = dict(shared)
        m["xp"] = np.ascontiguousarray(xp[c % nb, :TP])
        sc = c % nsb
        m["xs"] = np.ascontiguousarray(xs[4 * sc:4 * sc + 4].reshape(128, D))
        m["ck"] = np.ascontiguousarray(f(inputs["cache_fox_k"])[:NL, 4 * sc:4 * sc + 4].reshape(NL, 4, PAST, 1024))
        m["cv"] = np.ascontiguousarray(f(inputs["cache_fox_v"])[:NL, 4 * sc:4 * sc + 4].reshape(NL, 4, PAST, 1024))
        m["clf"] = np.ascontiguousarray(f(inputs["cache_fox_logf"])[:NL, 4 * sc:4 * sc + 4])
        m["st"] = np.ascontiguousarray(f(inputs["state_rwkv"])[:NL, 4 * sc:4 * sc + 4])
        m["sh"] = np.ascontiguousarray(f(inputs["state_rwkv_shift"])[:NL, 4 * sc:4 * sc + 4].reshape(NL, 4, RWW))
        in_maps.append({k: v for k, v in m.items() if k in PROG_INPUTS[key]})
    res = run_bass_kernel_spmd(nc, in_maps, core_ids=list(range(NCORE)), trace=trace)
    return res, nb, nsb


def assemble(res, TP, NL, nb, nsb):
    R = res.results
    cat_p = lambda k: np.stack([R[b][k] for b in range(nb)])
    yp = cat_p("yp")
    ys = np.concatenate([R[c]["ys"].reshape(4, 32, D) for c in range(nsb)], 0)
    kp = np.stack([R[b]["kp"] for b in range(nb)], 1).reshape(NL, nb, TP, 8, 128)
    vp = np.stack([R[b]["vp"] for b in range(nb)], 1).reshape(NL, nb, TP, 8, 128)
    lfp = np.stack([R[b]["lfp"] for b in range(nb)], 1)
    stp = np.stack([R[b]["stp"] for b in range(nb)], 1)
    shp = np.stack([R[b]["shp"] for b in range(nb)], 1).reshape(NL, nb, 1, RWW)
    ks = np.concatenate([R[c]["ks"].reshape(NL, 4, 32, 8, 128) for c in range(nsb)], 1)
    vs = np.concatenate([R[c]["vs"].reshape(NL, 4, 32, 8, 128) for c in range(nsb)], 1)
    lfs = np.concatenate([R[c]["lfs"].reshape(NL, 4, 32, 8) for c in range(nsb)], 1)
    sts = np.concatenate([R[c]["sts"] for c in range(nsb)], 1)
    shs = np.concatenate([R[c]["shs"].reshape(NL, 4, 1, RWW) for c in range(nsb)], 1)
    gvs = np.concatenate([R[c]["gvs"].reshape(NL, 4, 32, 1024) for c in range(nsb)], 1)
    return (yp, ys, kp, vp, lfp, stp, shp, ks, vs, lfs, sts, shs, gvs)


def kernel(**inputs):
    res, nb, nsb = run(inputs, TPROMPT, DEPTH)
    return assemble(res, TPROMPT, DEPTH, nb, nsb)
```
